# Optimizing a Trainium2 kernel written in Bass

```python
import math
import functools
import numpy as np
import jax
import jax.numpy as jnp
from jax import lax

D_MODEL = 1024
BATCH = 16
SEQ = 2048
DEPTH = 4

GRID_W = 64
CTX_LEN = 256
N_BRANCH = 4
D_BR = D_MODEL // 2
NORM_EPS = 1e-6

RWKV_HEADS = 8
RWKV_HEAD = D_BR // RWKV_HEADS
RWKV_DECAY_LORA = 64
RWKV_AAA_LORA = 64
RWKV_GN_EPS = 64e-5

SSD_HEADS = 8
SSD_HEADDIM = D_BR // SSD_HEADS
SSD_STATE = 128
SSD_GROUPS = 2
SSD_CONV = 3
SSD_CHUNK = 128
SSD_CONV_DIM = D_BR + 2 * SSD_GROUPS * SSD_STATE

HY_ORDER = 2
HY_SHORT = 3
HY_FILTER_WIDTH = 64
HY_BANDS = 16
HY_EMB = 1 + 2 * HY_BANDS
HY_TARGET = 1e-2
HY_FAST_DECAY = 0.3
HY_SLOW_DECAY = 1.5
HY_SHIFT = 0.05

RET_HEADS = 8
RET_HEAD = D_BR // RET_HEADS
RET_CHUNK = 128
RET_GN_EPS = 1e-5
ROPE_BASE = 10000.0

IN_SPLITS = (3 * D_BR, 2 * RWKV_DECAY_LORA, 2 * RWKV_AAA_LORA, D_BR,
             SSD_CONV_DIM, 2 * SSD_HEADS, D_BR,
             (HY_ORDER + 1) * D_BR, D_BR,
             3 * D_BR, D_BR)
N_IN = sum(IN_SPLITS)

kernel_name = 'hybrid_rwkv7_ssd_hyena_retention_dit'


def rmsnorm(x, w):
    xf = x.astype(jnp.float32)
    y = xf * lax.rsqrt(jnp.mean(jnp.square(xf), axis=-1, keepdims=True) + NORM_EPS)
    return (y * w.astype(jnp.float32)).astype(x.dtype)


def head_groupnorm(y, w, b, n_heads, eps):
    shp = y.shape
    yf = y.astype(jnp.float32).reshape(shp[:-1] + (n_heads, shp[-1] // n_heads))
    mu = jnp.mean(yf, axis=-1, keepdims=True)
    var = jnp.mean(jnp.square(yf - mu), axis=-1, keepdims=True)
    yf = ((yf - mu) * lax.rsqrt(var + eps)).reshape(shp)
    return yf * w.astype(jnp.float32) + b.astype(jnp.float32)


def modulate(h, cond, norm_w, ada_w, ada_b):
    mod = cond @ ada_w + ada_b
    shift, scale, gate = jnp.split(mod, 3, axis=-1)
    return rmsnorm(h, norm_w) * (1.0 + scale) + shift, gate


def split_in(p):
    idx = np.cumsum(IN_SPLITS)[:-1].tolist()
    return jnp.split(p, idx, axis=-1)


def identity(t):
    return t


def flip_t(t):
    return jnp.flip(t, axis=1)


def silu_gate(y, z):
    return y * jax.nn.silu(z.astype(jnp.float32))


def dwconv_centred(x, w, b):
    k = w.shape[-1]
    rhs = jnp.transpose(w)[:, None, :].astype(x.dtype)
    y = lax.conv_general_dilated(x, rhs, window_strides=(1,), padding=[(k // 2, k // 2)],
                                 dimension_numbers=('NWC', 'WIO', 'NWC'),
                                 feature_group_count=x.shape[-1])
    return y + b.astype(x.dtype)


def neighbour_mean(x):
    pad = [(0, 0), (1, 1)] + [(0, 0)] * (x.ndim - 2)
    xp = jnp.pad(x, pad)
    return 0.5 * (xp[:, :-2] + xp[:, 2:])


def bidirectional_scan(scan_fns, ctx_inputs, lat_inputs, s0):
    y_ctx, y_lat = 0.0, 0.0
    for d in range(2):
        orient = flip_t if d == 1 else identity
        y_c, s_c = scan_fns[d](*[orient(t) for t in ctx_inputs[d]], s0)
        y_l, _ = scan_fns[d](*[orient(t) for t in lat_inputs[d]], s_c)
        y_ctx = y_ctx + orient(y_c)
        y_lat = y_lat + orient(y_l)
    return y_ctx, y_lat


def rwkv_heads(t):
    return t.reshape(t.shape[:2] + (RWKV_HEADS, RWKV_HEAD))


def rwkv_token_inputs(p_rkv, z_w, z_a, mu, w0, w2, a0, a2, k_k, k_a):
    b_, t_ = p_rkv.shape[:2]
    rkv = p_rkv.reshape(b_, t_, 3, D_BR)
    rkv = rkv + mu * (neighbour_mean(rkv) - rkv)
    r, k, v = rkv[:, :, 0], rkv[:, :, 1], rkv[:, :, 2]
    z_w = z_w.reshape(b_, t_, 2, RWKV_DECAY_LORA)
    z_a = z_a.reshape(b_, t_, 2, RWKV_AAA_LORA)
    w_raw = (w0 + jnp.einsum('btdr,drc->btdc', jnp.tanh(z_w), w2)).astype(jnp.float32)
    decay = jnp.exp(-jnp.exp(-jax.nn.softplus(-w_raw) - 0.5))
    a = jax.nn.sigmoid((a0 + jnp.einsum('btdr,drc->btdc', z_a, a2)).astype(jnp.float32))
    kk = rwkv_heads((k * k_k).astype(jnp.float32))
    kk = kk * lax.rsqrt(jnp.maximum(jnp.sum(kk * kk, axis=-1, keepdims=True), 1e-12))
    kk = kk.reshape(b_, t_, D_BR)
    k_dir = k.astype(jnp.float32)[:, :, None] * (1.0 + (a - 1.0) * k_a.astype(jnp.float32))
    b_dir = kk[:, :, None] * a
    return (r.astype(jnp.float32), v.astype(jnp.float32), kk, decay, b_dir, k_dir)


def rwkv7_scan(r, decay, kk, b, k, v, s0):
    def step(s, inp):
        r_t, w_t, kk_t, b_t, k_t, v_t = inp
        sa = -jnp.einsum('bhvk,bhk->bhv', s, kk_t)
        s = s * w_t[:, :, None, :] + sa[..., None] * b_t[:, :, None, :] + v_t[..., None] * k_t[:, :, None, :]
        return s, jnp.einsum('bhvk,bhk->bhv', s, r_t)
    xs = tuple(jnp.moveaxis(t, 1, 0) for t in (r, decay, kk, b, k, v))
    s_fin, ys = lax.scan(step, s0, xs)
    return jnp.moveaxis(ys, 0, 1), s_fin


def rwkv_readout(y, tok, r_k, gn_w, gn_b):
    r, v, _, _, _, k_dir = tok
    b_, t_ = r.shape[:2]
    out = head_groupnorm(y.reshape(b_, t_, D_BR), gn_w, gn_b, RWKV_HEADS, RWKV_GN_EPS)
    k_h = k_dir.reshape(b_, t_, 2, RWKV_HEADS, RWKV_HEAD)
    coef = jnp.sum(rwkv_heads(r)[:, :, None] * k_h * r_k.astype(jnp.float32), axis=(2, 4))
    bonus = coef[..., None] * rwkv_heads(v)
    return out + bonus.reshape(b_, t_, D_BR)


def rwkv_branch(tok_c, tok_l, r_k, gn_w, gn_b, need_ctx):
    def dir_inputs(tok, d):
        r, v, kk, decay, b_dir, k_dir = tok
        return tuple(rwkv_heads(t) for t in (r, decay[:, :, d], kk, b_dir[:, :, d], k_dir[:, :, d], v))
    b_ = tok_l[0].shape[0]
    s0 = jnp.zeros((b_, RWKV_HEADS, RWKV_HEAD, RWKV_HEAD), jnp.float32)
    y_c, y_l = bidirectional_scan((rwkv7_scan, rwkv7_scan),
                                  [dir_inputs(tok_c, d) for d in range(2)],
                                  [dir_inputs(tok_l, d) for d in range(2)], s0)
    out_l = rwkv_readout(y_l, tok_l, r_k, gn_w, gn_b)
    out_c = rwkv_readout(y_c, tok_c, r_k, gn_w, gn_b) if need_ctx else None
    return out_c, out_l


def ssd_token_inputs(xbc, dt_raw, conv_w, conv_b, dt_bias):
    b_, t_ = xbc.shape[:2]
    xbc = jax.nn.silu(dwconv_centred(xbc, conv_w, conv_b)).astype(jnp.float32)
    xs, bm, cm = jnp.split(xbc, [D_BR, D_BR + SSD_GROUPS * SSD_STATE], axis=-1)
    xs = xs.reshape(b_, t_, SSD_HEADS, SSD_HEADDIM)
    rep = SSD_HEADS // SSD_GROUPS
    bm = jnp.repeat(bm.reshape(b_, t_, SSD_GROUPS, SSD_STATE), rep, axis=2)
    cm = jnp.repeat(cm.reshape(b_, t_, SSD_GROUPS, SSD_STATE), rep, axis=2)
    dt = jax.nn.softplus(dt_raw.reshape(b_, t_, 2, SSD_HEADS).astype(jnp.float32)
                         + dt_bias.astype(jnp.float32))
    return (xs, dt, bm, cm)


def segsum(x):
    t_ = x.shape[-1]
    xx = jnp.broadcast_to(x[..., :, None], x.shape + (t_,))
    xx = jnp.where(jnp.tril(jnp.ones((t_, t_), bool), -1), xx, 0.0)
    cs = jnp.cumsum(xx, axis=-2)
    return jnp.where(jnp.tril(jnp.ones((t_, t_), bool)), cs, -jnp.inf)


def ssd_chunked(x, dt, bm, cm, s0, a):
    b_, t_, h_, p_ = x.shape
    nc = t_ // SSD_CHUNK
    xd = (x * dt[..., None]).reshape(b_, nc, SSD_CHUNK, h_, p_)
    adt = jnp.transpose((dt * a).reshape(b_, nc, SSD_CHUNK, h_), (0, 3, 1, 2))
    bm = bm.reshape(b_, nc, SSD_CHUNK, h_, -1)
    cm = cm.reshape(b_, nc, SSD_CHUNK, h_, -1)
    a_cs = jnp.cumsum(adt, axis=-1)
    scores = jnp.einsum('bclhn,bcshn->bhcls', cm, bm) * jnp.exp(segsum(adt))
    y_diag = jnp.einsum('bhcls,bcshp->bclhp', scores, xd)
    decay_states = jnp.exp(a_cs[..., -1:] - a_cs)
    states = jnp.einsum('bclhn,bhcl,bclhp->bchpn', bm, decay_states, xd)
    states = jnp.concatenate([s0[:, None], states], axis=1)
    chunk_a = jnp.pad(a_cs[..., -1], ((0, 0), (0, 0), (1, 0)))
    states = jnp.einsum('bhzc,bchpn->bzhpn', jnp.exp(segsum(chunk_a)), states)
    y_off = jnp.einsum('bclhn,bchpn,bhcl->bclhp', cm, states[:, :-1], jnp.exp(a_cs))
    return (y_diag + y_off).reshape(b_, t_, h_, p_), states[:, -1]


def ssd_branch(tok_c, tok_l, z_c, z_l, a_log, d_skip, norm_w, need_ctx):
    fns = tuple(functools.partial(ssd_chunked, a=-jnp.exp(a_log[d].astype(jnp.float32))) for d in range(2))

    def dir_inputs(tok, d):
        xs, dt, bm, cm = tok
        return (xs, dt[:, :, d], bm, cm)

    def readout(y, tok, z):
        xs = tok[0]
        y = (y + d_skip.astype(jnp.float32)[:, None] * xs).reshape(xs.shape[0], xs.shape[1], D_BR)
        return rmsnorm(silu_gate(y, z), norm_w)

    b_ = z_l.shape[0]
    s0 = jnp.zeros((b_, SSD_HEADS, SSD_HEADDIM, SSD_STATE), jnp.float32)
    y_c, y_l = bidirectional_scan(fns, [dir_inputs(tok_c, d) for d in range(2)],
                                  [dir_inputs(tok_l, d) for d in range(2)], s0)
    out_l = readout(y_l, tok_l, z_l)
    out_c = readout(y_c, tok_c, z_c) if need_ctx else None
    return out_c, out_l


def hyena_filters(n, w1, b1, w2, b2, w3, freq):
    j = jnp.arange(n, dtype=jnp.float32)
    t = j / max(n - 1, 1)
    ang = 2.0 * math.pi * j / n
    bands = jnp.linspace(1e-4, HY_BANDS - 1, HY_BANDS)
    feats = jnp.concatenate([t[:, None], jnp.cos(ang[:, None] * bands), -jnp.sin(ang[:, None] * bands)], axis=-1)
    hid = jnp.sin(freq * (feats @ w1 + b1))
    hid = jnp.sin(freq * (hid @ w2 + b2))
    h = (hid @ w3).astype(jnp.float32).reshape(n, HY_ORDER, D_BR)
    lag = jnp.abs(j - n // 2) / (n / 2.0)
    deltas = jnp.abs(jnp.linspace(math.log(HY_TARGET) / HY_SLOW_DECAY, math.log(HY_TARGET) / HY_FAST_DECAY, D_BR))
    window = jnp.exp(-lag[:, None] * deltas[None, :]) + HY_SHIFT
    h = h * window[:, None, :]
    h = h / jnp.sum(jnp.abs(h), axis=0, keepdims=True)
    return jnp.moveaxis(h, 1, 0)


def fftconv_centred(u, h, bias):
    n = u.shape[1]
    uf = jnp.fft.rfft(u, n=2 * n, axis=1)
    hf = jnp.fft.rfft(h, n=2 * n, axis=0)
    y = jnp.fft.irfft(uf * hf[None], n=2 * n, axis=1)[:, n // 2: n // 2 + n]
    return y + u * bias


def hyena_sequence(p_h, conv_w, conv_b, w1, b1, w2, b2, w3, freq, bias):
    uc = dwconv_centred(p_h, conv_w, conv_b).astype(jnp.float32)
    parts = jnp.split(uc, HY_ORDER + 1, axis=-1)
    h = hyena_filters(p_h.shape[1], w1, b1, w2, b2, w3, freq)
    z = parts[0]
    for o in range(HY_ORDER):
        z = parts[o + 1] * fftconv_centred(z, h[o], bias[o].astype(jnp.float32))
    return z


def rotary_2d(x, cos_r, sin_r, cos_c, sin_c):
    half = x.shape[-1] // 2

    def rot(u, cs, sn):
        u1, u2 = jnp.split(u, 2, axis=-1)
        return jnp.concatenate([u1 * cs - u2 * sn, u1 * sn + u2 * cs], axis=-1)
    return jnp.concatenate([rot(x[..., :half], cos_r, sin_r), rot(x[..., half:], cos_c, sin_c)], axis=-1)


def retention_chunked(q, k, v, s0, log_g):
    b_, t_, h_, _ = q.shape
    nc = t_ // RET_CHUNK

    def chunks(u):
        return jnp.moveaxis(u.reshape(b_, nc, RET_CHUNK, h_, u.shape[-1]), 1, 0)
    idx = jnp.arange(RET_CHUNK, dtype=jnp.float32)
    diff = idx[:, None] - idx[None, :]
    inner = jnp.where(diff >= 0, jnp.exp(jnp.maximum(diff, 0.0) * log_g[:, None, None]), 0.0)
    xi = jnp.transpose(jnp.exp((idx + 1.0)[None, :] * log_g[:, None]))
    zeta = jnp.exp((RET_CHUNK - 1.0 - idx)[None, :] * log_g[:, None])
    g_chunk = jnp.exp(RET_CHUNK * log_g)

    def step(s, inp):
        qc, kc, vc = inp
        att = jnp.einsum('bnhd,bmhd->bhnm', qc, kc) * inner
        y = jnp.einsum('bhnm,bmhe->bnhe', att, vc)
        y = y + jnp.einsum('bnhd,bhde->bnhe', qc, s) * xi[None, :, :, None]
        s = g_chunk[None, :, None, None] * s + jnp.einsum('bmhd,bmhe,hm->bhde', kc, vc, zeta)
        return s, y
    s_fin, ys = lax.scan(step, s0, (chunks(q), chunks(k), chunks(v)))
    return jnp.moveaxis(ys, 0, 1).reshape(b_, t_, h_, -1), s_fin


def retention_branch(qkv_c, qkv_l, rope, log_decay, gn_w, gn_b, need_ctx):
    def split_heads(p):
        b_, t_ = p.shape[:2]
        qkv = p.astype(jnp.float32).reshape(b_, t_, 3, RET_HEADS, RET_HEAD)
        return qkv[:, :, 0] * RET_HEAD ** -0.5, qkv[:, :, 1], qkv[:, :, 2]

    def readout(y):
        return head_groupnorm(y.reshape(y.shape[0], y.shape[1], D_BR), gn_w, gn_b, RET_HEADS, RET_GN_EPS)

    q_c, k_c, v_c = split_heads(qkv_c)
    q_l, k_l, v_l = split_heads(qkv_l)
    q_l = rotary_2d(q_l, *rope)
    k_l = rotary_2d(k_l, *rope)
    fns = tuple(functools.partial(retention_chunked, log_g=log_decay[d].astype(jnp.float32)) for d in range(2))
    s0 = jnp.zeros((q_l.shape[0], RET_HEADS, RET_HEAD, RET_HEAD), jnp.float32)
    y_c, y_l = bidirectional_scan(fns, [(q_c, k_c, v_c)] * 2, [(q_l, k_l, v_l)] * 2, s0)
    out_l = readout(y_l)
    out_c = readout(y_c) if need_ctx else None
    return out_c, out_l


def merge_branches(u, ys, merge_w, merge_b, branch_w, out_w):
    acc = None
    for m in range(N_BRANCH):
        g = jax.nn.sigmoid(u @ merge_w[:, m * D_MODEL:(m + 1) * D_MODEL] + merge_b[m * D_MODEL:(m + 1) * D_MODEL])
        term = g * (ys[m].astype(u.dtype) @ branch_w[m])
        acc = term if acc is None else acc + term
    return acc @ out_w


def setup_inputs(seed: int = 0) -> dict:
    key = jax.random.key(seed)
    keys = iter(jax.random.split(key, 64))
    f32 = jnp.float32
    L, D = DEPTH, D_MODEL

    def nrm(shape, scale=1.0):
        return jax.random.normal(next(keys), shape, f32) * scale

    def gain(shape):
        return 1.0 + 0.02 * jax.random.normal(next(keys), shape, f32)

    def unif(shape, lo, hi):
        return jax.random.uniform(next(keys), shape, f32, lo, hi)

    decay_speed = -6.0 + 5.0 * (jnp.arange(D_BR, dtype=f32) / (D_BR - 1)) ** 0.85
    dt = jnp.exp(unif((L, 2, SSD_HEADS), math.log(1e-3), math.log(1e-1)))
    ret_base = jnp.log(1.0 - 2.0 ** (-5.0 - jnp.arange(RET_HEADS, dtype=f32)))
    W = HY_FILTER_WIDTH
    return {
        'x': nrm((BATCH, SEQ, D)),
        'c': nrm((BATCH, D)),
        'ctx': nrm((BATCH, CTX_LEN, D)),
        'c_ctx': nrm((D,)),
        'norm_w': gain((L, D)),
        'ada_w': nrm((L, D, 3 * D), 0.5 * D ** -0.5),
        'ada_b': nrm((L, 3 * D), 0.02),
        'in_w': nrm((L, D, N_IN), D ** -0.5),
        'rwkv_mu': unif((L, 3, D_BR), 0.0, 1.0),
        'rwkv_w0': decay_speed + 0.5 + nrm((L, 2, D_BR), 0.1),
        'rwkv_w2': nrm((L, 2, RWKV_DECAY_LORA, D_BR), 0.5 * RWKV_DECAY_LORA ** -0.5),
        'rwkv_a0': nrm((L, 2, D_BR), 0.1),
        'rwkv_a2': nrm((L, 2, RWKV_AAA_LORA, D_BR), RWKV_AAA_LORA ** -0.5),
        'rwkv_k_k': 0.85 + nrm((L, D_BR), 0.05),
        'rwkv_k_a': 1.0 + nrm((L, D_BR), 0.05),
        'rwkv_r_k': nrm((L, RWKV_HEADS, RWKV_HEAD), 0.1),
        'rwkv_gn_w': gain((L, D_BR)),
        'rwkv_gn_b': nrm((L, D_BR), 0.02),
        'ssd_conv_w': nrm((L, SSD_CONV_DIM, SSD_CONV), SSD_CONV ** -0.5),
        'ssd_conv_b': nrm((L, SSD_CONV_DIM), 0.02),
        'ssd_dt_bias': dt + jnp.log(-jnp.expm1(-dt)),
        'ssd_a_log': jnp.log(unif((L, 2, SSD_HEADS), 1.0, 16.0)),
        'ssd_d': 1.0 + nrm((L, SSD_HEADS), 0.1),
        'ssd_norm_w': gain((L, D_BR)),
        'hy_conv_w': nrm((L, (HY_ORDER + 1) * D_BR, HY_SHORT), HY_SHORT ** -0.5),
        'hy_conv_b': nrm((L, (HY_ORDER + 1) * D_BR), 0.02),
        'hy_w1': nrm((L, HY_EMB, W), HY_EMB ** -0.5),
        'hy_b1': nrm((L, W), 0.1),
        'hy_w2': nrm((L, W, W), W ** -0.5),
        'hy_b2': nrm((L, W), 0.1),
        'hy_w3': nrm((L, W, HY_ORDER * D_BR), W ** -0.5),
        'hy_freq': 1.0 + nrm((L, W), 0.1),
        'hy_bias': nrm((L, HY_ORDER, D_BR), 0.5),
        'ret_log_decay': ret_base * (1.0 + nrm((L, 2, RET_HEADS), 0.05)),
        'ret_gn_w': gain((L, D_BR)),
        'ret_gn_b': nrm((L, D_BR), 0.02),
        'merge_w': nrm((L, D, N_BRANCH * D), D ** -0.5),
        'merge_b': nrm((L, N_BRANCH * D), 0.02),
        'branch_w': nrm((L, N_BRANCH, D_BR, D), D_BR ** -0.5),
        'out_w': nrm((L, D, D), D ** -0.5),
        'final_norm_w': gain((D,)),
    }


def reference(x, c, ctx, c_ctx, norm_w, ada_w, ada_b, in_w,
              rwkv_mu, rwkv_w0, rwkv_w2, rwkv_a0, rwkv_a2, rwkv_k_k, rwkv_k_a, rwkv_r_k, rwkv_gn_w, rwkv_gn_b,
              ssd_conv_w, ssd_conv_b, ssd_dt_bias, ssd_a_log, ssd_d, ssd_norm_w,
              hy_conv_w, hy_conv_b, hy_w1, hy_b1, hy_w2, hy_b2, hy_w3, hy_freq, hy_bias,
              ret_log_decay, ret_gn_w, ret_gn_b,
              merge_w, merge_b, branch_w, out_w, final_norm_w):
    n_lat = x.shape[1]
    ROWS = n_lat // GRID_W
    rows = jnp.repeat(jnp.arange(ROWS, dtype=jnp.float32), GRID_W)
    cols = jnp.tile(jnp.arange(GRID_W, dtype=jnp.float32), ROWS)
    inv_freq = ROPE_BASE ** (-jnp.arange(0, RET_HEAD // 2, 2, dtype=jnp.float32) / (RET_HEAD // 2))
    ang_r = rows[:, None] * inv_freq
    ang_c = cols[:, None] * inv_freq
    rope = (jnp.cos(ang_r)[:, None, :], jnp.sin(ang_r)[:, None, :],
            jnp.cos(ang_c)[:, None, :], jnp.sin(ang_c)[:, None, :])

    cond_lat = jax.nn.silu(c)[:, None, :]
    cond_ctx = jax.nn.silu(c_ctx)[None, None, :]
    h_lat, h_ctx = x, ctx
    for layer in range(DEPTH):
        need_ctx = layer < DEPTH - 1
        u_c, gate_c = modulate(h_ctx, cond_ctx, norm_w[layer], ada_w[layer], ada_b[layer])
        u_l, gate_l = modulate(h_lat, cond_lat, norm_w[layer], ada_w[layer], ada_b[layer])
        pc = split_in(u_c @ in_w[layer])
        pl = split_in(u_l @ in_w[layer])

        rwkv_args = (rwkv_mu[layer], rwkv_w0[layer], rwkv_w2[layer], rwkv_a0[layer], rwkv_a2[layer],
                     rwkv_k_k[layer], rwkv_k_a[layer])
        ya_c, ya_l = rwkv_branch(rwkv_token_inputs(pc[0], pc[1], pc[2], *rwkv_args),
                                 rwkv_token_inputs(pl[0], pl[1], pl[2], *rwkv_args),
                                 rwkv_r_k[layer], rwkv_gn_w[layer], rwkv_gn_b[layer], need_ctx)

        ssd_args = (ssd_conv_w[layer], ssd_conv_b[layer], ssd_dt_bias[layer])
        yb_c, yb_l = ssd_branch(ssd_token_inputs(pc[4], pc[5], *ssd_args),
                                ssd_token_inputs(pl[4], pl[5], *ssd_args),
                                pc[6], pl[6], ssd_a_log[layer], ssd_d[layer], ssd_norm_w[layer], need_ctx)

        hy_args = (hy_conv_w[layer], hy_conv_b[layer], hy_w1[layer], hy_b1[layer], hy_w2[layer],
                   hy_b2[layer], hy_w3[layer], hy_freq[layer], hy_bias[layer])
        yc_l = hyena_sequence(pl[7], *hy_args)

        yd_c, yd_l = retention_branch(pc[9], pl[9], rope, ret_log_decay[layer],
                                      ret_gn_w[layer], ret_gn_b[layer], need_ctx)

        mix_args = (merge_w[layer], merge_b[layer], branch_w[layer], out_w[layer])
        ys_l = (silu_gate(ya_l, pl[3]), yb_l, silu_gate(yc_l, pl[8]), silu_gate(yd_l, pl[10]))
        h_lat = h_lat + gate_l * merge_branches(u_l, ys_l, *mix_args)
        if need_ctx:
            yc_c = hyena_sequence(pc[7], *hy_args)
            ys_c = (silu_gate(ya_c, pc[3]), yb_c, silu_gate(yc_c, pc[8]), silu_gate(yd_c, pc[10]))
            h_ctx = h_ctx + gate_c * merge_branches(u_c, ys_c, *mix_args)
    return rmsnorm(h_lat, final_norm_w)
```

```python
import math
import threading
from contextlib import ExitStack
import numpy as np
import ml_dtypes
import concourse.bass as bass
import concourse.mybir as mybir
from concourse.bass_utils import run_bass_kernel_spmd

F32 = mybir.dt.float32
BF16 = mybir.dt.bfloat16
ALU = mybir.AluOpType
AF = mybir.ActivationFunctionType
AX = mybir.AxisListType

NCORES = 8
D = 1024
NB = 2
TC = 256
TL = 2048
TT = TC + TL
NT = NB * TT
DEPTH = 4
DBR = 512
NCOLS = 9088
NBLK = NCOLS // 128
C_RKV, C_ZW, C_ZA, C_GA = 0, 1536, 1664, 1792
C_XBC, C_GZ = 2304, 3328
C_HY, C_GH = 3840, 5376
C_QKV, C_GR = 5888, 7424
C_QSW, C_KSW, C_DT = 7936, 8448, 8960
TILES = [(0, 256), (256, 512), (768, 512), (1280, 512), (1792, 512)]


class Buf:
    def __init__(self, t, name=""):
        self.t = t
        self.name = name
        self.w = {}
        self.r = {}

    def __getitem__(self, k):
        return self.t[k]


class KB:
    NDS = 8

    def __init__(self, nc):
        self.nc = nc
        self.eng = {"pe": nc.tensor, "act": nc.scalar, "dve": nc.vector, "pool": nc.gpsimd, "sp": nc.sync}
        self.sem = {}
        self.cnt = {}
        self.seen = {e: {} for e in self.eng}
        for e in self.eng:
            self.sem[e] = nc.alloc_semaphore("s_" + e)
            self.cnt[e] = 0
        self.dq = {}
        for q in ("sp", "act", "pool"):
            for j in range(self.NDS):
                k = "d_%s%d" % (q, j)
                self.sem[k] = nc.alloc_semaphore(k)
                self.cnt[k] = 0
            self.dq[q] = 0
        self.psum = [Buf(nc.alloc_psum_tensor("ps%d" % i, [128, 512], F32), "ps%d" % i) for i in range(8)]
        self.psi = 0
        self.ndram = 0
        self.co = None

    def _wait(self, e, key, val):
        if val <= 0:
            return
        if self.seen[e].get(key, 0) < val:
            self.eng[e].wait_ge(self.sem[key], val)
            self.seen[e][key] = val

    def _deps(self, e, reads, writes, waw):
        for b in reads:
            for k, v in b.w.items():
                self._wait(e, k, v)
        for b in writes:
            for k, v in b.r.items():
                self._wait(e, k, v)
            if waw:
                for k, v in b.w.items():
                    if not (e == "pe" and k == "pe"):
                        self._wait(e, k, v)

    def _record(self, key, val, reads, writes, waw):
        for b in reads:
            if b.r.get(key, 0) < val:
                b.r[key] = val
        for b in writes:
            if waw:
                b.w = {key: val}
                b.r = {}
            else:
                b.w[key] = val

    def op(self, e, emit, reads=(), writes=(), waw=True):
        if self.co is not None:
            self.co.tick()
        self._deps(e, reads, writes, waw)
        ins = emit(self.eng[e])
        self.cnt[e] += 1
        ins.then_inc(self.sem[e], 1)
        self._record(e, self.cnt[e], reads, writes, waw)

    def dma(self, q, out, in_, reads=(), writes=(), waw=False, **kw):
        if self.co is not None:
            self.co.tick()
            if q == "pool" and threading.current_thread() is self.co.wt:
                q = "act"
        j = self.dq[q] % self.NDS
        self.dq[q] += 1
        key = "d_%s%d" % (q, j)
        self._wait(q, key, self.cnt[key])
        self._deps(q, reads, writes, waw)
        ins = self.eng[q].dma_start(out=out, in_=in_, **kw)
        self.cnt[key] += 16
        ins.then_inc(self.sem[key], 16)
        self._record(key, self.cnt[key], reads, writes, waw)

    def barrier(self):
        keys = list(self.sem.keys())
        for e in self.eng:
            for k in keys:
                self._wait(e, k, self.cnt[k])

    def ps(self):
        b = self.psum[self.psi % 8]
        self.psi += 1
        return b

    def ps6(self):
        b = self.psum[self.psi % 6]
        self.psi += 1
        return b

    def dram(self, name, shape, dt=F32):
        t = self.nc.dram_tensor(name, list(shape), dt, kind="Internal").ap()
        return Buf(t, name)


class Interleave:
    def __init__(self, fn):
        self.left = 0
        self.go_w = threading.Semaphore(0)
        self.go_m = threading.Semaphore(0)
        self.done = False
        self.exc = None
        self.nops = 0
        self.wt = threading.Thread(target=self._run, args=(fn,))
        self.wt.start()

    def _run(self, fn):
        self.go_w.acquire()
        try:
            fn()
        except BaseException as e:
            self.exc = e
        self.done = True
        self.go_m.release()

    def tick(self):
        if threading.current_thread() is self.wt:
            self.nops += 1
            self.left -= 1
            if self.left <= 0:
                self.go_m.release()
                self.go_w.acquire()

    def give(self, n):
        if self.done:
            return
        self.left = n
        self.go_w.release()
        self.go_m.acquire()
        if self.exc is not None:
            raise self.exc

    def finish(self):
        while not self.done:
            self.give(10 ** 12)
        self.wt.join()
        if self.exc is not None:
            raise self.exc


class Phase:
    def __init__(self, kb):
        self.kb = kb
        self.es = ExitStack()

    def __enter__(self):
        self.kb.barrier()
        self.es.__enter__()
        return self

    def __exit__(self, *a):
        self.kb.barrier()
        return self.es.__exit__(*a)

    def sb(self, name, shape, dt=F32):
        self.kb.ndram += 1
        name = "%s_s%d" % (name, self.kb.ndram)
        t = self.es.enter_context(self.kb.nc.sbuf_tensor(name, list(shape), dt))
        return Buf(t, name)


def rsqrt(kb, ob, out, ib, in_, scale, eps):
    kb.op("dve", lambda e: e.tensor_scalar(out=out, in0=in_, scalar1=scale, scalar2=eps, op0=ALU.mult, op1=ALU.add),
          reads=[ib], writes=[ob])
    kb.op("act", lambda e: e.activation(out=out, in_=out, func=AF.Sqrt), reads=[ob], writes=[ob])
    kb.op("dve", lambda e: e.reciprocal(out=out, in_=out), reads=[ob], writes=[ob])


def layer_front(kb, l, L):
    ones, cond_in, adaw_in, adab_in, normw_in, inw_in = (L[k] for k in ("ones", "cond_in", "adaw_in", "adab_in", "normw_in", "inw_in"))
    modA, modS, modG, HTv, UTv, HT, UTd, PT = (L[k] for k in ("modA", "modS", "modG", "HTv", "UTv", "HT", "UTd", "PT"))
    with Phase(kb) as P:
        cT = P.sb("cT", [128, 8, 3])
        adab = P.sb("adab", [128, 24])
        nw = P.sb("nw", [128, 8])
        mod = P.sb("mod", [128, 24, 3])
        aw = [P.sb("aw%d" % i, [128, 8, 512]) for i in range(2)]
        kb.dma("sp", cT[:], cond_in[:, :, :], writes=[cT])
        kb.dma("sp", adab[:], adab_in[l, :, :], writes=[adab])
        kb.dma("sp", nw[:], normw_in[l, :, :], writes=[nw])
        kb.op("act", lambda e: e.activation(out=cT[:], in_=cT[:], func=AF.Silu), reads=[cT], writes=[cT])
        ps = kb.ps()
        for g in range(6):
            a = aw[g % 2]
            kb.dma("sp", a[:], adaw_in[l, :, :, g * 512:(g + 1) * 512], writes=[a])
            for jj in range(4):
                j = g * 4 + jj
                for c in range(8):
                    kb.op("pe", lambda e: e.matmul(ps[:, j * 3:(j + 1) * 3], lhsT=a[:, c, jj * 128:(jj + 1) * 128], rhs=cT[:, c, :],
                                                   start=(c == 0), stop=(c == 7)),
                          reads=[a, cT], writes=[ps])
        kb.op("dve", lambda e: e.tensor_tensor(out=mod[:], in0=ps[:, 0:72].rearrange("p (j k) -> p j k", k=3),
                                               in1=adab[:].unsqueeze(2).to_broadcast([128, 24, 3]), op=ALU.add),
              reads=[ps, adab], writes=[mod])
        kb.op("dve", lambda e: e.tensor_copy(out=modS[:], in_=mod[:, 0:8, :]), reads=[mod], writes=[modS])
        kb.op("dve", lambda e: e.tensor_copy(out=modG[:], in_=mod[:, 16:24, :]), reads=[mod], writes=[modG])
        kb.op("dve", lambda e: e.scalar_tensor_tensor(out=modA[:], in0=mod[:, 8:16, :], scalar=1.0,
                                                      in1=nw[:].unsqueeze(2).to_broadcast([128, 8, 3]), op0=ALU.add, op1=ALU.mult),
              reads=[mod, nw], writes=[modA])
    with Phase(kb) as P:
        hin = [P.sb("hin%d" % i, [128, 8, 512]) for i in range(2)]
        sq = P.sb("sq", [128, 8, 512])
        rstd = P.sb("rstd", [128, 512])
        tmp = [P.sb("tmp%d" % i, [128, 512]) for i in range(2)]
        uo = [P.sb("uo%d" % i, [128, 8, 512], BF16) for i in range(2)]
        it = 0
        for b in range(NB):
            for (t0, tl) in TILES:
                hi = hin[it % 2]
                u = uo[it % 2]
                it += 1
                j = 2 if t0 < TC else b
                g0 = gtok(b, t0)
                kb.dma("sp", hi[:, :, 0:tl], HTv[:, :, g0:g0 + tl], reads=[HT], writes=[hi])
                kb.op("act", lambda e: e.activation(out=sq[:, :, 0:tl], in_=hi[:, :, 0:tl], func=AF.Square), reads=[hi], writes=[sq])
                ps = kb.ps()
                for c in range(8):
                    kb.op("pe", lambda e: e.matmul(ps[:, 0:tl], lhsT=ones[:], rhs=sq[:, c, 0:tl], start=(c == 0), stop=(c == 7)),
                          reads=[ones, sq], writes=[ps])
                rsqrt(kb, rstd, rstd[:, 0:tl], ps, ps[:, 0:tl], 1.0 / D, 1e-6)
                for c in range(8):
                    tm = tmp[c % 2]
                    kb.op("dve", lambda e: e.scalar_tensor_tensor(out=tm[:, 0:tl], in0=hi[:, c, 0:tl], scalar=modA[:, c, j:j + 1], in1=rstd[:, 0:tl],
                                                                  op0=ALU.mult, op1=ALU.mult),
                          reads=[hi, modA, rstd], writes=[tm])
                    kb.op("act", lambda e: e.activation(out=u[:, c, 0:tl], in_=tm[:, 0:tl], func=AF.Identity, bias=modS[:, c, j:j + 1]),
                          reads=[tm, modS], writes=[u], waw=False)
                kb.dma("pool", UTv[:, :, g0:g0 + tl], u[:, :, 0:tl], reads=[u], writes=[UTd])
    with Phase(kb) as P:
        UT = P.sb("UT", [128, 8, NT], BF16)
        wst = [P.sb("wst%d" % i, [128, 8, 128]) for i in range(2)]
        wb = [P.sb("wb%d" % i, [128, 8, 128], BF16) for i in range(2)]
        stage = [P.sb("stage%d" % i, [128, NT]) for i in range(2)]
        for c in range(8):
            kb.dma("sp", UT[:, c, :], UTv[:, c, :], reads=[UTd], writes=[UT])
        ev = 0
        for blk in range(NBLK):
            ws, w, st = wst[blk % 2], wb[blk % 2], stage[blk % 2]
            kb.dma("sp", ws[:], inw_in[l, blk, :, :, :], writes=[ws])
            kb.op("pool", lambda e: e.tensor_copy(out=w[:], in_=ws[:]), reads=[ws], writes=[w])
            for b in range(NB):
                for (t0, tl) in TILES:
                    g0 = gtok(b, t0)
                    ps = kb.ps()
                    for c in range(8):
                        kb.op("pe", lambda e: e.matmul(ps[:, 0:tl], lhsT=w[:, c, :], rhs=UT[:, c, g0:g0 + tl], start=(c == 0), stop=(c == 7)),
                              reads=[w, UT], writes=[ps])
                    if ev % 2:
                        kb.op("act", lambda e: e.copy(out=st[:, g0:g0 + tl], in_=ps[:, 0:tl]), reads=[ps], writes=[st], waw=False)
                    else:
                        kb.op("dve", lambda e: e.tensor_copy(out=st[:, g0:g0 + tl], in_=ps[:, 0:tl]), reads=[ps], writes=[st], waw=False)
                    ev += 1
            kb.dma("pool", PT[blk * 128:(blk + 1) * 128, :], st[:], reads=[st], writes=[PT])


def layer_merge(kb, l, L):
    mergew_in, mergeb_in, branchw_in, outw_in = (L[k] for k in ("mergew_in", "mergeb_in", "branchw_in", "outw_in"))
    modG, HTv, UTv, YSv, HT, UTd, YST = (L[k] for k in ("modG", "HTv", "UTv", "YSv", "HT", "UTd", "YST"))
    MWB, BWB, OWB = L["MWB"], L["BWB"], L["OWB"]
    with Phase(kb) as P:
        ws8 = [P.sb("ws8_%d" % i, [128, 8, 128]) for i in range(3)]
        wb8 = [P.sb("wb8_%d" % i, [128, 8, 128], BF16) for i in range(3)]
        k = 0
        for (src, dst, n_, cc) in ((mergew_in, MWB, 32, 8), (branchw_in, BWB, 32, 4), (outw_in, OWB, 8, 8)):
            for j in range(n_):
                a_, b_ = ws8[k % 3], wb8[k % 3]
                kb.dma("sp", a_[:, 0:cc, :], src[l, j, :, :, :], writes=[a_])
                if k % 2:
                    kb.op("act", lambda e: e.copy(out=b_[:, 0:cc, :], in_=a_[:, 0:cc, :]), reads=[a_], writes=[b_])
                else:
                    kb.op("dve", lambda e: e.tensor_copy(out=b_[:, 0:cc, :], in_=a_[:, 0:cc, :]), reads=[a_], writes=[b_])
                kb.dma("pool", dst[j, :, :, :], b_[:, 0:cc, :], reads=[b_], writes=[dst])
                k += 1
    with Phase(kb) as P:
        mb = P.sb("mb", [128, 32])
        kb.dma("sp", mb[:], mergeb_in[l, :, :], writes=[mb])
        ut = [P.sb("ut%d" % i, [128, 8, 512], BF16) for i in range(2)]
        ys = [P.sb("ys%d" % i, [128, 4, 4, 512], BF16) for i in range(2)]
        hin = [P.sb("hin%d" % i, [128, 8, 512]) for i in range(2)]
        hout = [P.sb("hout%d" % i, [128, 8, 512]) for i in range(2)]
        acc = P.sb("acc", [128, 8, 512])
        accb = P.sb("accb", [128, 8, 512], BF16)
        gt = [P.sb("gt%d" % i, [128, 512]) for i in range(2)]
        tm = [P.sb("tm%d" % i, [128, 512]) for i in range(2)]
        mwb = [P.sb("mwb%d" % i, [128, 8, 128], BF16) for i in range(3)]
        bwb = [P.sb("bwb%d" % i, [128, 4, 128], BF16) for i in range(3)]
        it = 0
        wi = 0
        for b in range(NB):
            for (t0, tl) in TILES:
                u, y, hi, ho = ut[it % 2], ys[it % 2], hin[it % 2], hout[it % 2]
                it += 1
                jc = 2 if t0 < TC else b
                g0 = gtok(b, t0)
                kb.dma("sp", u[:, :, 0:tl], UTv[:, :, g0:g0 + tl], reads=[UTd], writes=[u])
                for m in range(4):
                    kb.dma("sp", y[:, m, :, 0:tl], YSv[:, m, :, g0:g0 + tl], reads=[YST], writes=[y])
                kb.dma("sp", hi[:, :, 0:tl], HTv[:, :, g0:g0 + tl], reads=[HT], writes=[hi])
                for m in range(4):
                    for j in range(8):
                        mj = m * 8 + j
                        wb_, bb = mwb[wi % 3], bwb[wi % 3]
                        g, t_ = gt[wi % 2], tm[wi % 2]
                        wi += 1
                        kb.dma("sp", wb_[:], MWB[mj, :, :, :], reads=[MWB], writes=[wb_])
                        kb.dma("sp", bb[:], BWB[mj, :, :, :], reads=[BWB], writes=[bb])
                        psg = kb.ps()
                        for c in range(8):
                            kb.op("pe", lambda e: e.matmul(psg[:, 0:tl], lhsT=wb_[:, c, :], rhs=u[:, c, 0:tl], start=(c == 0), stop=(c == 7)),
                                  reads=[wb_, u], writes=[psg])
                        psb = kb.ps()
                        for c in range(4):
                            kb.op("pe", lambda e: e.matmul(psb[:, 0:tl], lhsT=bb[:, c, :], rhs=y[:, m, c, 0:tl], start=(c == 0), stop=(c == 3)),
                                  reads=[bb, y], writes=[psb])
                        kb.op("act", lambda e: e.activation(out=g[:, 0:tl], in_=psg[:, 0:tl], func=AF.Sigmoid, bias=mb[:, mj:mj + 1]),
                              reads=[psg, mb], writes=[g])
                        if m == 0:
                            kb.op("dve", lambda e: e.tensor_tensor(out=acc[:, j, 0:tl], in0=psb[:, 0:tl], in1=g[:, 0:tl], op=ALU.mult),
                                  reads=[psb, g], writes=[acc], waw=False)
                        else:
                            kb.op("dve", lambda e: e.tensor_tensor(out=t_[:, 0:tl], in0=psb[:, 0:tl], in1=g[:, 0:tl], op=ALU.mult),
                                  reads=[psb, g], writes=[t_])
                            kb.op("dve", lambda e: e.tensor_tensor(out=acc[:, j, 0:tl], in0=acc[:, j, 0:tl], in1=t_[:, 0:tl], op=ALU.add),
                                  reads=[t_, acc], writes=[acc])
                kb.op("act", lambda e: e.copy(out=accb[:, :, 0:tl], in_=acc[:, :, 0:tl]), reads=[acc], writes=[accb])
                for j2 in range(8):
                    wb_ = mwb[wi % 3]
                    wi += 1
                    kb.dma("sp", wb_[:], OWB[j2, :, :, :], reads=[OWB], writes=[wb_])
                    pso = kb.ps()
                    for c in range(8):
                        kb.op("pe", lambda e: e.matmul(pso[:, 0:tl], lhsT=wb_[:, c, :], rhs=accb[:, c, 0:tl], start=(c == 0), stop=(c == 7)),
                              reads=[wb_, accb], writes=[pso])
                    kb.op("dve", lambda e: e.scalar_tensor_tensor(out=ho[:, j2, 0:tl], in0=pso[:, 0:tl], scalar=modG[:, j2, jc:jc + 1], in1=hi[:, j2, 0:tl],
                                                                  op0=ALU.mult, op1=ALU.add),
                          reads=[pso, modG, hi], writes=[ho], waw=False)
                kb.dma("pool", HTv[:, :, g0:g0 + tl], ho[:, :, 0:tl], reads=[ho], writes=[HT])


def make_A(kb, P, L, b, sT, negA):
    AT, ident = L["AT"], L["ident"]
    onesr = P.sb("onesr", [16, TT])
    Pf = P.sb("Pf", [16, TT])
    Ab = P.sb("Ab", [16, TT])
    cc = P.sb("cc", [16, 2])
    kb.op("dve", lambda e: e.memset(onesr[:], 1.0), writes=[onesr])
    kb.op("dve", lambda e: e.tensor_tensor_scan(out=Pf[:], data0=onesr[:], data1=sT[:], initial=0.0, op0=ALU.mult, op1=ALU.add),
          reads=[onesr, sT], writes=[Pf])
    kb.op("dve", lambda e: e.tensor_copy(out=cc[:, 0:1], in_=Pf[:, TC - 1:TC]), reads=[Pf], writes=[cc])
    kb.op("dve", lambda e: e.tensor_tensor(out=cc[:, 1:2], in0=Pf[:, TC - 1:TC], in1=Pf[:, TT - 1:TT], op=ALU.add), reads=[Pf, cc], writes=[cc])
    kb.op("dve", lambda e: e.scalar_tensor_tensor(out=Ab[:, 0:TC], in0=sT[:, 0:TC], scalar=cc[:, 0:1], in1=Pf[:, 0:TC], op0=ALU.add, op1=ALU.subtract),
          reads=[sT, cc, Pf], writes=[Ab])
    kb.op("dve", lambda e: e.scalar_tensor_tensor(out=Ab[:, TC:TT], in0=sT[:, TC:TT], scalar=cc[:, 1:2], in1=Pf[:, TC:TT], op0=ALU.add, op1=ALU.subtract),
          reads=[sT, cc, Pf], writes=[Ab])
    kb.op("dve", lambda e: e.tensor_copy(out=Ab[0:8, :], in_=Pf[0:8, :]), reads=[Pf, Ab], writes=[Ab])
    kb.dma("sp", AT[b, :, :], Ab[:], reads=[Ab], writes=[AT])
    for i in range(TT // 128):
        ps = kb.ps6()
        kb.op("pe", lambda e: e.transpose(out=ps[:, 0:16], in_=Ab[:, i * 128:(i + 1) * 128], identity=ident[0:16, 0:16]),
              reads=[Ab, ident], writes=[ps])
        kb.op("dve", lambda e: e.tensor_scalar(out=negA[:, i, :], in0=ps[:, 0:16], scalar1=-1.0, scalar2=None, op0=ALU.mult),
              reads=[ps], writes=[negA], waw=False)


def tok_transpose(kb, L, srcT, dst, nchunk, scale=None):
    ident = L["ident"]
    k = 0
    for i in range(TT // 128):
        for c in range(nchunk):
            ps = kb.ps6()
            kb.op("pe", lambda e: e.transpose(out=ps[:, 0:128], in_=srcT[:, c, i * 128:(i + 1) * 128], identity=ident[:]),
                  reads=[srcT, ident], writes=[ps])
            if k % 2:
                kb.op("act", lambda e: e.copy(out=dst[:, i, c * 128:(c + 1) * 128], in_=ps[:, 0:128]), reads=[ps], writes=[dst], waw=False)
            else:
                kb.op("dve", lambda e: e.tensor_copy(out=dst[:, i, c * 128:(c + 1) * 128], in_=ps[:, 0:128]), reads=[ps], writes=[dst], waw=False)
            k += 1


def decay_attn(kb, P, L, b, kq, Vtok, cscal, negA, wk):
    AT, YRAW, maskF, maskB = L["AT"], L["YRAW"], L["maskF"], L["maskB"]
    abc, Et, Tt, Wt, yst = wk
    accb = [kb.psum[6], kb.psum[7]]
    U = []
    gi = 0
    for h in range(8):
        for (t0, tl) in TILES:
            c0, c1 = t0 // 128, (t0 + tl) // 128
            isctx_q = t0 < TC
            units = []
            for i in range(TT // 128):
                isctx_k = i < 2
                for d in range(2):
                    if c0 <= i < c1:
                        units.append((i, d, i - c0))
                    elif d == 0 and i < c0:
                        units.append((i, d, None))
                    elif d == 1 and ((i >= c1 and not isctx_q) or (isctx_k and not isctx_q)):
                        units.append((i, d, None))
            for k, (i, d, off) in enumerate(units):
                U.append(dict(h=h, t0=t0, tl=tl, i=i, d=d, off=off, first=(k == 0), last=(k == len(units) - 1), g=gi,
                              newi=(k == 0 or units[k - 1][0] != i), newh=(k == 0 and t0 == 0)))
            gi += 1
    state = {"st": None, "tn": 0}

    def stA(n, u):
        h, t0, tl, i, d, off = u["h"], u["t0"], u["tl"], u["i"], u["d"], u["off"]
        ab = abc[h % 2]
        if u["newh"]:
            for dd in range(2):
                kb.dma("sp", ab[:, dd, :], AT[b, dd * 8 + h:dd * 8 + h + 1, :].to_broadcast([128, TT]), reads=[AT], writes=[ab])
        lf, rf, bufs = kq(h)
        if u["newi"]:
            st = kb.ps6()
            kb.op("pe", lambda e: e.matmul(st[:, 0:tl], lhsT=lf(i), rhs=rf(t0, t0 + tl), start=True, stop=True), reads=bufs, writes=[st])
            state["st"] = st
        u["st"] = state["st"]
        E = Et[n % 3]
        col = d * 8 + h
        if off is None:
            kb.op("act", lambda e: e.activation(out=E[:, 0:tl], in_=ab[:, d, t0:t0 + tl], func=AF.Exp, bias=negA[:, i, col:col + 1]),
                  reads=[ab, negA], writes=[E])
        else:
            T_ = Tt[state["tn"] % 2]
            state["tn"] += 1
            mk = maskF if d == 0 else maskB
            kb.op("dve", lambda e: e.tensor_scalar(out=T_[:, 0:tl], in0=ab[:, d, t0:t0 + tl], scalar1=negA[:, i, col:col + 1], scalar2=0.0,
                                                   op0=ALU.add, op1=ALU.min), reads=[ab, negA], writes=[T_])
            kb.op("dve", lambda e: e.tensor_tensor(out=T_[:, 0:tl], in0=T_[:, 0:tl], in1=mk[:, off, 0:tl], op=ALU.add), reads=[T_, mk], writes=[T_])
            kb.op("act", lambda e: e.activation(out=E[:, 0:tl], in_=T_[:, 0:tl], func=AF.Exp), reads=[T_], writes=[E])

    def stB(n, u):
        tl = u["tl"]
        E, W, st = Et[n % 3], Wt[n % 3], u["st"]
        cs, csb = cscal(u["i"], u["h"], u["d"])
        kb.op("dve", lambda e: e.scalar_tensor_tensor(out=W[:, 0:tl], in0=st[:, 0:tl], scalar=cs, in1=E[:, 0:tl], op0=ALU.mult, op1=ALU.mult),
              reads=[st, E] + csb, writes=[W])

    def stC(n, u):
        h, t0, tl, i = u["h"], u["t0"], u["tl"], u["i"]
        W = Wt[n % 3]
        acc = accb[u["g"] % 2]
        kb.op("pe", lambda e: e.matmul(acc[0:64, 0:tl], lhsT=Vtok[:, i, h * 64:(h + 1) * 64], rhs=W[:, 0:tl], start=u["first"], stop=u["last"]),
              reads=[Vtok, W], writes=[acc])
        if u["last"]:
            ys_ = yst[u["g"] % 2]
            kb.op("act", lambda e: e.copy(out=ys_[0:64, 0:tl], in_=acc[0:64, 0:tl]), reads=[acc], writes=[ys_])
            g0 = gtok(b, t0)
            kb.dma("sp", YRAW[h * 64:(h + 1) * 64, g0:g0 + tl], ys_[0:64, 0:tl], reads=[ys_], writes=[YRAW])

    N = len(U)
    for t in range(N + 2):
        if t < N:
            stA(t, U[t])
        if 0 <= t - 1 < N:
            stB(t - 1, U[t - 1])
        if 0 <= t - 2 < N:
            stC(t - 2, U[t - 2])


def attn_work(P):
    abc = [P.sb("abc%d" % i, [128, 2, TT]) for i in range(2)]
    Et = [P.sb("E%d" % i, [128, 512]) for i in range(3)]
    Tt = [P.sb("T%d" % i, [128, 512]) for i in range(2)]
    Wt = [P.sb("W%d" % i, [128, 512], BF16) for i in range(3)]
    yst = [P.sb("yst%d" % i, [64, 512]) for i in range(2)]
    return abc, Et, Tt, Wt, yst


def dwconv3(kb, x, acc, w, segs):
    xb, xa = x
    ab, aa = acc
    wb_, wa = w
    kb.op("dve", lambda e: e.tensor_scalar(out=aa, in0=xa, scalar1=wa[:, 1:2], scalar2=None, op0=ALU.mult), reads=[xb, wb_], writes=[ab])
    for (s0, s1) in segs:
        kb.op("dve", lambda e: e.scalar_tensor_tensor(out=aa[:, s0 + 1:s1], in0=xa[:, s0:s1 - 1], scalar=wa[:, 0:1], in1=aa[:, s0 + 1:s1],
                                                      op0=ALU.mult, op1=ALU.add), reads=[xb, wb_, ab], writes=[ab])
        kb.op("dve", lambda e: e.scalar_tensor_tensor(out=aa[:, s0:s1 - 1], in0=xa[:, s0 + 1:s1], scalar=wa[:, 2:3], in1=aa[:, s0:s1 - 1],
                                                      op0=ALU.mult, op1=ALU.add), reads=[xb, wb_, ab], writes=[ab])


SEGS = [(0, TC), (TC, TT)]


def layer_ssd(kb, l, L):
    PT, XS, YRAW, YST, ones = (L[k] for k in ("PT", "XS", "YRAW", "YST", "ones"))
    for b in range(NB):
        with Phase(kb) as P:
            g0 = gtok(b, 0)
            cw = P.sb("cw", [128, 8, 3])
            cb = P.sb("cb", [128, 8])
            kb.dma("sp", cw[:], L["ssdcw_in"][l, :, :, :], writes=[cw])
            kb.dma("sp", cb[:], L["ssdcb_in"][l, :, :], writes=[cb])
            BT = P.sb("BT", [128, 2, TT], BF16)
            CT = P.sb("CT", [128, 2, TT], BF16)
            Vtok = P.sb("Vtok", [128, 18, 512], BF16)
            negA = P.sb("negA", [128, 18, 16])
            dttok = P.sb("dttok", [128, 18, 16])
            with Phase(kb) as Q:
                xin = [Q.sb("xin%d" % i, [128, TT]) for i in range(2)]
                cacc = [Q.sb("cacc%d" % i, [128, TT]) for i in range(2)]
                xs32 = Q.sb("xs32", [128, 4, TT])
                for c in range(8):
                    xi, ca = xin[c % 2], cacc[c % 2]
                    r0 = C_XBC + c * 128
                    kb.dma("sp", xi[:], PT[r0:r0 + 128, g0:g0 + TT], reads=[PT], writes=[xi])
                    dwconv3(kb, (xi, xi[:]), (ca, ca[:]), (cw, cw[:, c, :]), SEGS)
                    if c < 4:
                        kb.op("act", lambda e: e.activation(out=xs32[:, c, :], in_=ca[:], func=AF.Silu, bias=cb[:, c:c + 1]),
                              reads=[ca, cb], writes=[xs32], waw=False)
                        kb.dma("sp", XS[c * 128:(c + 1) * 128, g0:g0 + TT], xs32[:, c, :], reads=[xs32], writes=[XS])
                    else:
                        dst = BT if c < 6 else CT
                        kb.op("act", lambda e: e.activation(out=dst[:, c % 2, :], in_=ca[:], func=AF.Silu, bias=cb[:, c:c + 1]),
                              reads=[ca, cb], writes=[dst], waw=False)
                tok_transpose(kb, L, xs32, Vtok, 4)
            with Phase(kb) as Q:
                dtb = Q.sb("dtb", [16, 1])
                alog = Q.sb("alog", [16, 1])
                dtT = Q.sb("dtT", [16, TT])
                sT = Q.sb("sT", [16, TT])
                kb.dma("sp", dtb[:], L["ssddtb_in"][l, :, :], writes=[dtb])
                kb.dma("sp", alog[:], L["ssdalog_in"][l, :, :], writes=[alog])
                kb.dma("sp", dtT[:], PT[C_DT:C_DT + 16, g0:g0 + TT], reads=[PT], writes=[dtT])
                kb.op("act", lambda e: e.activation(out=dtT[:], in_=dtT[:], func=AF.Exp, bias=dtb[:, 0:1]), reads=[dtT, dtb], writes=[dtT])
                kb.op("act", lambda e: e.activation(out=dtT[:], in_=dtT[:], func=AF.Ln, bias=1.0), reads=[dtT], writes=[dtT])
                kb.op("act", lambda e: e.activation(out=alog[:], in_=alog[:], func=AF.Exp), reads=[alog], writes=[alog])
                kb.op("dve", lambda e: e.tensor_scalar(out=alog[:], in0=alog[:], scalar1=-1.0, scalar2=None, op0=ALU.mult), reads=[alog], writes=[alog])
                kb.op("dve", lambda e: e.tensor_scalar(out=sT[:], in0=dtT[:], scalar1=alog[:, 0:1], scalar2=None, op0=ALU.mult),
                      reads=[dtT, alog], writes=[sT])
                make_A(kb, Q, L, b, sT, negA)
                for i in range(TT // 128):
                    ps = kb.ps6()
                    kb.op("pe", lambda e: e.transpose(out=ps[:, 0:16], in_=dtT[:, i * 128:(i + 1) * 128], identity=L["ident"][0:16, 0:16]),
                          reads=[dtT, L["ident"]], writes=[ps])
                    kb.op("dve", lambda e: e.tensor_copy(out=dttok[:, i, :], in_=ps[:, 0:16]), reads=[ps], writes=[dttok], waw=False)
            with Phase(kb) as Q:
                wk = attn_work(Q)

                def kq(h):
                    g = h // 4
                    return (lambda i: BT[:, g, i * 128:(i + 1) * 128]), (lambda n0, n1: CT[:, g, n0:n1]), [BT, CT]

                def cscal(i, h, d):
                    col = d * 8 + h
                    return dttok[:, i, col:col + 1], [dttok]
                decay_attn(kb, Q, L, b, kq, Vtok, cscal, negA, wk)
    with Phase(kb) as P:
        Dv = P.sb("Dv", [128, 4])
        nw = P.sb("nw", [128, 4])
        kb.dma("sp", Dv[:], L["ssdD_in"][l, :, :], writes=[Dv])
        kb.dma("sp", nw[:], L["ssdnw_in"][l, :, :], writes=[nw])
        yb = [P.sb("yb%d" % i, [128, 4, 512]) for i in range(2)]
        xb = [P.sb("xb%d" % i, [128, 4, 512]) for i in range(2)]
        zb = [P.sb("zb%d" % i, [128, 4, 512]) for i in range(2)]
        sq = P.sb("sq", [128, 4, 512])
        rstd = P.sb("rstd", [128, 512])
        ob = [P.sb("ob%d" % i, [128, 4, 512], BF16) for i in range(2)]
        it = 0
        for b in range(NB):
            for (t0, tl) in TILES:
                y, x, z, o = yb[it % 2], xb[it % 2], zb[it % 2], ob[it % 2]
                it += 1
                g0 = gtok(b, t0)
                kb.dma("sp", y[:, :, 0:tl], YRAW.t.rearrange("(c p) g -> p c g", p=128)[:, :, g0:g0 + tl], reads=[YRAW], writes=[y])
                kb.dma("sp", x[:, :, 0:tl], XS.t.rearrange("(c p) g -> p c g", p=128)[:, :, g0:g0 + tl], reads=[XS], writes=[x])
                kb.dma("sp", z[:, :, 0:tl], PT[C_GZ:C_GZ + 512, :].rearrange("(c p) g -> p c g", p=128)[:, :, g0:g0 + tl], reads=[PT], writes=[z])
                for c in range(4):
                    kb.op("dve", lambda e: e.scalar_tensor_tensor(out=y[:, c, 0:tl], in0=x[:, c, 0:tl], scalar=Dv[:, c:c + 1], in1=y[:, c, 0:tl],
                                                                  op0=ALU.mult, op1=ALU.add), reads=[x, Dv, y], writes=[y])
                kb.op("act", lambda e: e.activation(out=z[:, :, 0:tl], in_=z[:, :, 0:tl], func=AF.Silu), reads=[z], writes=[z])
                kb.op("dve", lambda e: e.tensor_tensor(out=y[:, :, 0:tl], in0=y[:, :, 0:tl], in1=z[:, :, 0:tl], op=ALU.mult), reads=[y, z], writes=[y])
                kb.op("act", lambda e: e.activation(out=sq[:, :, 0:tl], in_=y[:, :, 0:tl], func=AF.Square), reads=[y], writes=[sq])
                ps = kb.ps()
                for c in range(4):
                    kb.op("pe", lambda e: e.matmul(ps[:, 0:tl], lhsT=ones[:], rhs=sq[:, c, 0:tl], start=(c == 0), stop=(c == 3)),
                          reads=[ones, sq], writes=[ps])
                rsqrt(kb, rstd, rstd[:, 0:tl], ps, ps[:, 0:tl], 1.0 / DBR, 1e-6)
                for c in range(4):
                    kb.op("dve", lambda e: e.scalar_tensor_tensor(out=o[:, c, 0:tl], in0=y[:, c, 0:tl], scalar=nw[:, c:c + 1], in1=rstd[:, 0:tl],
                                                                  op0=ALU.mult, op1=ALU.mult), reads=[y, nw, rstd], writes=[o], waw=False)
                kb.dma("pool", L["YSv"][:, 1, :, g0:g0 + tl], o[:, :, 0:tl], reads=[o], writes=[YST])


def layer_ret(kb, l, L):
    PT, YRAW, YST, blk64 = (L[k] for k in ("PT", "YRAW", "YST", "blk64"))
    for b in range(NB):
        with Phase(kb) as P:
            g0 = gtok(b, 0)
            QT = P.sb("QT", [128, 4, TT], BF16)
            KT = P.sb("KT", [128, 4, TT], BF16)
            Vtok = P.sb("Vtok", [128, 18, 512], BF16)
            negA = P.sb("negA", [128, 18, 16])
            with Phase(kb) as Q:
                v32 = Q.sb("v32", [128, 4, TT])
                for c in range(4):
                    r0 = C_QKV + 1024 + c * 128
                    kb.dma("sp", v32[:, c, :], PT[r0:r0 + 128, g0:g0 + TT], reads=[PT], writes=[v32])
                tok_transpose(kb, L, v32, Vtok, 4)
            with Phase(kb) as Q:
                lg = Q.sb("lg", [16, 1])
                sT = Q.sb("sT", [16, TT])
                kb.dma("sp", lg[:], L["retlg_in"][l, :, :], writes=[lg])
                kb.op("dve", lambda e: e.memset(sT[:], 1.0), writes=[sT])
                kb.op("dve", lambda e: e.tensor_scalar(out=sT[:], in0=sT[:], scalar1=lg[:, 0:1], scalar2=None, op0=ALU.mult), reads=[sT, lg], writes=[sT])
                make_A(kb, Q, L, b, sT, negA)
            with Phase(kb) as Q:
                cosT = Q.sb("cosT", [128, TL])
                sinT = Q.sb("sinT", [128, TL])
                kb.dma("sp", cosT[:], L["ropec_in"][:, :], writes=[cosT])
                kb.dma("sp", sinT[:], L["ropes_in"][:, :], writes=[sinT])
                xin = [Q.sb("xin%d" % i, [128, TT]) for i in range(2)]
                xsw = [Q.sb("xsw%d" % i, [128, TT]) for i in range(2)]
                k = 0
                for (dst, r0, rs) in ((QT, C_QKV, C_QSW), (KT, C_QKV + 512, C_KSW)):
                    for c in range(4):
                        xi, xw = xin[k % 2], xsw[k % 2]
                        k += 1
                        kb.dma("sp", xi[:], PT[r0 + c * 128:r0 + (c + 1) * 128, g0:g0 + TT], reads=[PT], writes=[xi])
                        kb.dma("sp", xw[:, TC:TT], PT[rs + c * 128:rs + (c + 1) * 128, g0 + TC:g0 + TT], reads=[PT], writes=[xw])
                        kb.op("act", lambda e: e.copy(out=dst[:, c, 0:TC], in_=xi[:, 0:TC]), reads=[xi], writes=[dst], waw=False)
                        kb.op("dve", lambda e: e.tensor_tensor(out=xi[:, TC:TT], in0=xi[:, TC:TT], in1=cosT[:], op=ALU.mult), reads=[xi, cosT], writes=[xi])
                        kb.op("dve", lambda e: e.tensor_tensor(out=xw[:, TC:TT], in0=xw[:, TC:TT], in1=sinT[:], op=ALU.mult), reads=[xw, sinT], writes=[xw])
                        kb.op("dve", lambda e: e.tensor_tensor(out=dst[:, c, TC:TT], in0=xi[:, TC:TT], in1=xw[:, TC:TT], op=ALU.add),
                              reads=[xi, xw], writes=[dst], waw=False)
            with Phase(kb) as Q:
                wk = attn_work(Q)

                def kq(h):
                    c, p0 = h // 2, (h % 2) * 64
                    return (lambda i: KT[p0:p0 + 64, c, i * 128:(i + 1) * 128]), (lambda n0, n1: QT[p0:p0 + 64, c, n0:n1]), [KT, QT]

                def cscal(i, h, d):
                    return 0.125, []
                decay_attn(kb, Q, L, b, kq, Vtok, cscal, negA, wk)
    with Phase(kb) as P:
        gw = P.sb("gw", [128, 4])
        gb = P.sb("gb", [128, 4])
        kb.dma("sp", gw[:], L["retgw_in"][l, :, :], writes=[gw])
        kb.dma("sp", gb[:], L["retgb_in"][l, :, :], writes=[gb])
        groupnorm_gate_readout(kb, P, L, YRAW, C_GR, gw, gb, 1e-5, 3, None)


def groupnorm_gate_readout(kb, P, L, YR, c_gate, gw, gb, eps, mslot, extra):
    PT, YST, blk64 = L["PT"], L["YST"], L["blk64"]
    yb = [P.sb("yb%d" % i, [128, 4, 512]) for i in range(2)]
    zb = [P.sb("zb%d" % i, [128, 4, 512]) for i in range(2)]
    eb = [P.sb("eb%d" % i, [128, 4, 512]) for i in range(2)] if extra is not None else None
    dd = [P.sb("dd%d" % i, [128, 512]) for i in range(2)]
    sq = [P.sb("sq%d" % i, [128, 512]) for i in range(2)]
    rs = [P.sb("rs%d" % i, [128, 512]) for i in range(2)]
    ob = [P.sb("ob%d" % i, [128, 4, 512], BF16) for i in range(2)]
    it = 0
    k = 0
    for b in range(NB):
        for (t0, tl) in TILES:
            y, z, o = yb[it % 2], zb[it % 2], ob[it % 2]
            ex = eb[it % 2] if extra is not None else None
            it += 1
            g0 = gtok(b, t0)
            kb.dma("sp", y[:, :, 0:tl], YR.t.rearrange("(c p) g -> p c g", p=128)[:, :, g0:g0 + tl], reads=[YR], writes=[y])
            kb.dma("sp", z[:, :, 0:tl], PT[c_gate:c_gate + 512, :].rearrange("(c p) g -> p c g", p=128)[:, :, g0:g0 + tl], reads=[PT], writes=[z])
            if extra is not None:
                kb.dma("sp", ex[:, :, 0:tl], extra.t.rearrange("(c p) g -> p c g", p=128)[:, :, g0:g0 + tl], reads=[extra], writes=[ex])
            kb.op("act", lambda e: e.activation(out=z[:, :, 0:tl], in_=z[:, :, 0:tl], func=AF.Silu), reads=[z], writes=[z])
            for c in range(4):
                d_, s_, r_ = dd[k % 2], sq[k % 2], rs[k % 2]
                k += 1
                pm = kb.ps()
                kb.op("pe", lambda e: e.matmul(pm[:, 0:tl], lhsT=blk64[:], rhs=y[:, c, 0:tl], start=True, stop=True), reads=[blk64, y], writes=[pm])
                kb.op("dve", lambda e: e.tensor_tensor(out=d_[:, 0:tl], in0=y[:, c, 0:tl], in1=pm[:, 0:tl], op=ALU.subtract), reads=[y, pm], writes=[d_])
                kb.op("act", lambda e: e.activation(out=s_[:, 0:tl], in_=d_[:, 0:tl], func=AF.Square), reads=[d_], writes=[s_])
                pv = kb.ps()
                kb.op("pe", lambda e: e.matmul(pv[:, 0:tl], lhsT=blk64[:], rhs=s_[:, 0:tl], start=True, stop=True), reads=[blk64, s_], writes=[pv])
                rsqrt(kb, r_, r_[:, 0:tl], pv, pv[:, 0:tl], 1.0, eps)
                kb.op("dve", lambda e: e.tensor_tensor(out=d_[:, 0:tl], in0=d_[:, 0:tl], in1=r_[:, 0:tl], op=ALU.mult), reads=[d_, r_], writes=[d_])
                kb.op("act", lambda e: e.activation(out=d_[:, 0:tl], in_=d_[:, 0:tl], func=AF.Identity, scale=gw[:, c:c + 1], bias=gb[:, c:c + 1]),
                      reads=[d_, gw, gb], writes=[d_])
                if extra is not None:
                    kb.op("dve", lambda e: e.tensor_tensor(out=d_[:, 0:tl], in0=d_[:, 0:tl], in1=ex[:, c, 0:tl], op=ALU.add), reads=[d_, ex], writes=[d_])
                kb.op("dve", lambda e: e.tensor_tensor(out=o[:, c, 0:tl], in0=d_[:, 0:tl], in1=z[:, c, 0:tl], op=ALU.mult), reads=[d_, z], writes=[o], waw=False)
            kb.dma("pool", L["YSv"][:, mslot, :, g0:g0 + tl], o[:, :, 0:tl], reads=[o], writes=[YST])


TWO_PI = 2.0 * math.pi


def sin_inplace(kb, xb, xa, kib, kia, kfb, kfa, mpi):
    kb.op("dve", lambda e: e.tensor_scalar(out=xa, in0=xa, scalar1=17.0 * math.pi, scalar2=None, op0=ALU.add), reads=[xb], writes=[xb])
    kb.op("dve", lambda e: e.tensor_scalar(out=kfa, in0=xa, scalar1=1.0 / TWO_PI, scalar2=None, op0=ALU.mult), reads=[xb], writes=[kfb])
    kb.op("dve", lambda e: e.tensor_copy(out=kia, in_=kfa), reads=[kfb], writes=[kib])
    kb.op("dve", lambda e: e.tensor_copy(out=kfa, in_=kia), reads=[kib], writes=[kfb])
    kb.op("dve", lambda e: e.scalar_tensor_tensor(out=xa, in0=kfa, scalar=-TWO_PI, in1=xa, op0=ALU.mult, op1=ALU.add), reads=[kfb, xb], writes=[xb])
    kb.op("dve", lambda e: e.tensor_scalar(out=kfa, in0=xa, scalar1=0.0, scalar2=TWO_PI, op0=ALU.is_lt, op1=ALU.mult), reads=[xb], writes=[kfb])
    kb.op("dve", lambda e: e.tensor_tensor(out=xa, in0=xa, in1=kfa, op=ALU.add), reads=[xb, kfb], writes=[xb])
    kb.op("act", lambda e: e.activation(out=xa, in_=xa, func=AF.Sin, bias=mpi[0:xa.shape[0], 0:1]), reads=[xb, mpi], writes=[xb])


def fwd_dft(kb, P, n, tab, load_rhs, consume):
    nch = n // 128
    fcs = [P.sb("fc%d" % i, [128, nch, 128], BF16) for i in range(2)]
    fss = [P.sb("fs%d" % i, [128, nch, 128], BF16) for i in range(2)]
    k = 0
    pend = None
    for cc in range(4):
        rt = load_rhs(cc)
        for fch in range(nch):
            fc, fs = fcs[k % 2], fss[k % 2]
            k += 1
            kb.dma("sp", fc[:], tab["Fc"][fch, :, :, :], writes=[fc])
            kb.dma("sp", fs[:], tab["Fs"][fch, :, :, :], writes=[fs])
            pre, pim = kb.ps(), kb.ps()
            for j in range(nch):
                kb.op("pe", lambda e: e.matmul(pre[:, 0:256], lhsT=fc[:, j, :], rhs=rt[:, j, :], start=(j == 0), stop=(j == nch - 1)),
                      reads=[fc, rt], writes=[pre])
            for j in range(nch):
                kb.op("pe", lambda e: e.matmul(pim[:, 0:256], lhsT=fs[:, j, :], rhs=rt[:, j, :], start=(j == 0), stop=(j == nch - 1)),
                      reads=[fs, rt], writes=[pim])
            if pend is not None:
                consume(*pend)
            pend = (cc, fch, pre, pim)
            if fch == nch - 1:
                consume(*pend)
                pend = None


def layer_hyena(kb, l, L):
    PT, HYC, UTOK, Z1T, YST, ident = (L[k] for k in ("PT", "HYC", "UTOK", "Z1T", "YST", "ident"))
    for b in range(NB):
        with Phase(kb) as P:
            g0 = gtok(b, 0)
            cw = P.sb("cw", [128, 12, 3])
            cb = P.sb("cb", [128, 12])
            kb.dma("sp", cw[:], L["hycw_in"][l, :, :, :], writes=[cw])
            kb.dma("sp", cb[:], L["hycb_in"][l, :, :], writes=[cb])
            xin = [P.sb("xin%d" % i, [128, TT]) for i in range(2)]
            cacc = [P.sb("cacc%d" % i, [128, TT]) for i in range(2)]
            v32 = P.sb("v32", [128, 4, TT])
            vt = P.sb("vt", [128, 18, 512], BF16)
            for c in range(12):
                xi, ca = xin[c % 2], cacc[c % 2]
                r0 = C_HY + c * 128
                kb.dma("sp", xi[:], PT[r0:r0 + 128, g0:g0 + TT], reads=[PT], writes=[xi])
                dwconv3(kb, (xi, xi[:]), (ca, ca[:]), (cw, cw[:, c, :]), SEGS)
                if c < 4:
                    kb.op("act", lambda e: e.activation(out=v32[:, c, :], in_=ca[:], func=AF.Identity, bias=cb[:, c:c + 1]),
                          reads=[ca, cb], writes=[v32], waw=False)
                    kb.dma("pool", HYC[c * 128:(c + 1) * 128, g0:g0 + TT], v32[:, c, :], reads=[v32], writes=[HYC])
                else:
                    kb.op("act", lambda e: e.activation(out=ca[:], in_=ca[:], func=AF.Identity, bias=cb[:, c:c + 1]), reads=[ca, cb], writes=[ca])
                    kb.dma("pool", HYC[c * 128:(c + 1) * 128, g0:g0 + TT], ca[:], reads=[ca], writes=[HYC])
            tok_transpose(kb, L, v32, vt, 4)
            kb.dma("pool", UTOK[0][g0:g0 + TT, :].rearrange("(i p) c -> p i c", p=128), vt[:], reads=[vt], writes=[UTOK[0]])
    for (n, t0s) in ((TL, TC), (TC, 0)):
        nch = n // 128
        tab = L["hytab"][n]
        HTOKn, HFn = L["HTOK"][n], L["HF"][n]
        with Phase(kb) as P:
            feats = P.sb("feats", [33, n])
            w1 = P.sb("w1", [33, 64])
            w2 = P.sb("w2", [64, 64])
            w3 = P.sb("w3", [64, 1024])
            b1 = P.sb("b1", [64, 1])
            b2 = P.sb("b2", [64, 1])
            fr = P.sb("fr", [64, 1])
            mpi = P.sb("mpi", [128, 1])
            hid1 = P.sb("hid1", [64, n])
            hid2 = P.sb("hid2", [64, n])
            ki = P.sb("ki", [64, n], mybir.dt.int32)
            kf = P.sb("kf", [64, n])
            win = P.sb("win", [128, 4, n])
            hw = [P.sb("hw%d" % i, [128, n]) for i in range(2)]
            asum = P.sb("asum", [128, 2])
            ht = [P.sb("ht%d" % i, [128, nch, 128], BF16) for i in range(2)]
            for (dst, src) in ((feats, tab["feats"][:, :]), (w1, L["hyw1_in"][l, :, :]), (w2, L["hyw2_in"][l, :, :]), (w3, L["hyw3_in"][l, :, :]),
                               (b1, L["hyb1_in"][l, :, :]), (b2, L["hyb2_in"][l, :, :]), (fr, L["hyfreq_in"][l, :, :]), (win, tab["window"][:, :, :])):
                kb.dma("sp", dst[:], src, writes=[dst])
            kb.op("dve", lambda e: e.memset(mpi[:], -math.pi), writes=[mpi])
            kb.op("dve", lambda e: e.tensor_tensor(out=b1[:], in0=b1[:], in1=fr[:], op=ALU.mult), reads=[b1, fr], writes=[b1])
            kb.op("dve", lambda e: e.tensor_tensor(out=b2[:], in0=b2[:], in1=fr[:], op=ALU.mult), reads=[b2, fr], writes=[b2])
            tiles = [(t, min(512, n - t)) for t in range(0, n, 512)]
            for (hid, w, K_, src, bb) in ((hid1, w1, 33, feats, b1), (hid2, w2, 64, hid1, b2)):
                for (t, wd) in tiles:
                    ps = kb.ps()
                    kb.op("pe", lambda e: e.matmul(ps[0:64, 0:wd], lhsT=w[0:K_, :], rhs=src[0:K_, t:t + wd], start=True, stop=True),
                          reads=[w, src], writes=[ps])
                    kb.op("dve", lambda e: e.tensor_scalar(out=hid[:, t:t + wd], in0=ps[0:64, 0:wd], scalar1=fr[:, 0:1], scalar2=bb[:, 0:1],
                                                           op0=ALU.mult, op1=ALU.add), reads=[ps, fr, bb], writes=[hid], waw=False)
                sin_inplace(kb, hid, hid[:], ki, ki[:], kf, kf[:], mpi)
            for j8 in range(8):
                h_ = hw[j8 % 2]
                hs = ht[j8 % 2]
                for (t, wd) in tiles:
                    ps = kb.ps()
                    kb.op("pe", lambda e: e.matmul(ps[:, 0:wd], lhsT=w3[:, j8 * 128:(j8 + 1) * 128], rhs=hid2[:, t:t + wd], start=True, stop=True),
                          reads=[w3, hid2], writes=[ps])
                    kb.op("dve", lambda e: e.tensor_tensor(out=h_[:, t:t + wd], in0=ps[:, 0:wd], in1=win[:, j8 % 4, t:t + wd], op=ALU.mult),
                          reads=[ps, win], writes=[h_], waw=False)
                kb.op("dve", lambda e: e.tensor_reduce(out=asum[:, 0:1], in_=h_[:], axis=AX.X, op=ALU.add, apply_absolute_value=True),
                      reads=[h_], writes=[asum])
                kb.op("dve", lambda e: e.reciprocal(out=asum[:, 1:2], in_=asum[:, 0:1]), reads=[asum], writes=[asum])
                kb.op("dve", lambda e: e.tensor_scalar(out=h_[:], in0=h_[:], scalar1=asum[:, 1:2], scalar2=None, op0=ALU.mult), reads=[h_, asum], writes=[h_])
                for i in range(nch):
                    ps = kb.ps()
                    kb.op("pe", lambda e: e.transpose(out=ps[:, 0:128], in_=h_[:, i * 128:(i + 1) * 128], identity=ident[:]), reads=[h_, ident], writes=[ps])
                    if i % 2:
                        kb.op("act", lambda e: e.copy(out=hs[:, i, :], in_=ps[:, 0:128]), reads=[ps], writes=[hs], waw=False)
                    else:
                        kb.op("dve", lambda e: e.tensor_copy(out=hs[:, i, :], in_=ps[:, 0:128]), reads=[ps], writes=[hs], waw=False)
                kb.dma("pool", HTOKn[:, j8 * 128:(j8 + 1) * 128].rearrange("(i p) c -> p i c", p=128), hs[:], reads=[hs], writes=[HTOKn])
        with Phase(kb) as P:
            rts = [P.sb("rt%d" % i, [128, nch, 256], BF16) for i in range(2)]
            ho = [P.sb("ho%d" % i, [128, 2, 256]) for i in range(2)]
            cnt = [0, 0]

            def load_rhs(cc):
                rt = rts[cnt[0] % 2]
                cnt[0] += 1
                for o in range(2):
                    c0 = o * 512 + cc * 128
                    kb.dma("sp", rt[:, :, o * 128:(o + 1) * 128], HTOKn[:, c0:c0 + 128].rearrange("(i p) c -> p i c", p=128), reads=[HTOKn], writes=[rt])
                return rt

            def consume(cc, fch, pre, pim):
                h_ = ho[cnt[1] % 2]
                cnt[1] += 1
                kb.op("act", lambda e: e.copy(out=h_[:, 0, :], in_=pre[:, 0:256]), reads=[pre], writes=[h_], waw=False)
                kb.op("dve", lambda e: e.tensor_copy(out=h_[:, 1, :], in_=pim[:, 0:256]), reads=[pim], writes=[h_], waw=False)
                for ri in range(2):
                    kb.dma("pool", HFn[ri, fch, :, cc, :], h_[:, ri, :], reads=[h_], writes=[HFn])
            fwd_dft(kb, P, n, tab, load_rhs, consume)
        for o in range(2):
            with Phase(kb) as P:
                hb = P.sb("hb", [128, 2, 4])
                kb.dma("sp", hb[:], L["hybias_in"][l, :, :, :], writes=[hb])
                rts = [P.sb("rt%d" % i, [128, nch, 256], BF16) for i in range(2)]
                Y = P.sb("Y", [128, nch, 2, 256], BF16)
                hf = [P.sb("hf%d" % i, [128, 2, 128]) for i in range(2)]
                ta = [P.sb("ta%d" % i, [128, 256]) for i in range(2)]
                tb = [P.sb("tb%d" % i, [128, 256]) for i in range(2)]
                gre = P.sb("gre", [128, nch, 256], BF16)
                gim = P.sb("gim", [128, nch, 256], BF16)
                ut = [P.sb("ut%d" % i, [128, 256]) for i in range(2)]
                xt = [P.sb("xt%d" % i, [128, 256]) for i in range(2)]
                gt = [P.sb("gt%d" % i, [128, 256]) for i in range(2)]
                zt = [P.sb("zt%d" % i, [128, 256]) for i in range(2)]
                zo = [P.sb("zo%d" % i, [128, 256], BF16) for i in range(2)]
                ztok = [P.sb("ztok%d" % i, [128, 2, 128], BF16) for i in range(2)]
                cnt = [0, 0, 0]

                def load_rhs(cc):
                    rt = rts[cnt[0] % 2]
                    cnt[0] += 1
                    for b in range(NB):
                        g0 = gtok(b, t0s)
                        kb.dma("sp", rt[:, :, b * 128:(b + 1) * 128], UTOK[o][g0:g0 + n, cc * 128:(cc + 1) * 128].rearrange("(i p) c -> p i c", p=128),
                               reads=[UTOK[o]], writes=[rt])
                    return rt

                def consume(cc, fch, pre, pim):
                    h_ = hf[cnt[1] % 2]
                    a_, b_ = ta[cnt[1] % 2], tb[cnt[1] % 2]
                    cnt[1] += 1
                    for ri in range(2):
                        kb.dma("sp", h_[:, ri, :], HFn[ri, fch, :, cc, o * 128:(o + 1) * 128], reads=[HFn], writes=[h_])
                    hre = h_[:, 0, :].unsqueeze(1).to_broadcast([128, 2, 128])
                    him = h_[:, 1, :].unsqueeze(1).to_broadcast([128, 2, 128])
                    v3 = lambda ap: ap.rearrange("p (b c) -> p b c", b=2)
                    kb.op("dve", lambda e: e.tensor_tensor(out=v3(a_[:]), in0=v3(pre[:, 0:256]), in1=hre, op=ALU.mult), reads=[pre, h_], writes=[a_])
                    kb.op("dve", lambda e: e.tensor_tensor(out=v3(b_[:]), in0=v3(pim[:, 0:256]), in1=him, op=ALU.mult), reads=[pim, h_], writes=[b_])
                    kb.op("dve", lambda e: e.tensor_tensor(out=Y[:, fch, 0, :], in0=a_[:], in1=b_[:], op=ALU.subtract), reads=[a_, b_], writes=[Y], waw=False)
                    if fch == 0:
                        kb.op("dve", lambda e: e.tensor_copy(out=Y[0:1, 0, 0, :], in_=a_[0:1, :]), reads=[a_, Y], writes=[Y])
                    a2, b2_ = ta[cnt[1] % 2], tb[cnt[1] % 2]
                    cnt[1] += 1
                    kb.op("dve", lambda e: e.tensor_tensor(out=v3(a2[:]), in0=v3(pre[:, 0:256]), in1=him, op=ALU.mult), reads=[pre, h_], writes=[a2])
                    kb.op("dve", lambda e: e.tensor_tensor(out=v3(b2_[:]), in0=v3(pim[:, 0:256]), in1=hre, op=ALU.mult), reads=[pim, h_], writes=[b2_])
                    kb.op("dve", lambda e: e.tensor_tensor(out=Y[:, fch, 1, :], in0=a2[:], in1=b2_[:], op=ALU.add), reads=[a2, b2_], writes=[Y], waw=False)
                    if fch == 0:
                        kb.op("dve", lambda e: e.tensor_copy(out=Y[0:1, 0, 1, :], in_=b_[0:1, :]), reads=[b_, Y], writes=[Y])
                    if fch == nch - 1:
                        inverse(cc)

                def inverse(cc):
                    for tt in range(n // 256):
                        kb.dma("sp", gre[:], tab["Gre"][tt, :, :, :], writes=[gre])
                        kb.dma("sp", gim[:], tab["Gim"][tt, :, :, :], writes=[gim])
                        for b in range(NB):
                            k = cnt[2]
                            cnt[2] += 1
                            g = gtok(b, t0s + tt * 256)
                            u_, x_, g_, z_, zo_, zk = ut[k % 2], xt[k % 2], gt[k % 2], zt[k % 2], zo[k % 2], ztok[k % 2]
                            usrc = HYC if o == 0 else Z1T
                            kb.dma("sp", u_[:], usrc[cc * 128:(cc + 1) * 128, g:g + 256], reads=[usrc], writes=[u_])
                            xr = 512 * (o + 1) + cc * 128
                            kb.dma("sp", x_[:], HYC[xr:xr + 128, g:g + 256], reads=[HYC], writes=[x_])
                            if o == 1:
                                kb.dma("sp", g_[:], PT[C_GH + cc * 128:C_GH + (cc + 1) * 128, g:g + 256], reads=[PT], writes=[g_])
                                kb.op("act", lambda e: e.activation(out=g_[:], in_=g_[:], func=AF.Silu), reads=[g_], writes=[g_])
                            ps = kb.ps()
                            for fch in range(nch):
                                kb.op("pe", lambda e: e.matmul(ps[:, 0:256], lhsT=Y[:, fch, 0, b * 128:(b + 1) * 128], rhs=gre[:, fch, :],
                                                               start=(fch == 0), stop=False), reads=[Y, gre], writes=[ps])
                                kb.op("pe", lambda e: e.matmul(ps[:, 0:256], lhsT=Y[:, fch, 1, b * 128:(b + 1) * 128], rhs=gim[:, fch, :],
                                                               start=False, stop=(fch == nch - 1)), reads=[Y, gim], writes=[ps])
                            kb.op("dve", lambda e: e.scalar_tensor_tensor(out=z_[:], in0=u_[:], scalar=hb[:, o, cc:cc + 1], in1=ps[:, 0:256],
                                                                          op0=ALU.mult, op1=ALU.add), reads=[u_, hb, ps], writes=[z_])
                            if o == 0:
                                kb.op("dve", lambda e: e.tensor_tensor(out=z_[:], in0=z_[:], in1=x_[:], op=ALU.mult), reads=[z_, x_], writes=[z_])
                                kb.dma("pool", Z1T[cc * 128:(cc + 1) * 128, g:g + 256], z_[:], reads=[z_], writes=[Z1T])
                                for hh in range(2):
                                    p2 = kb.ps()
                                    kb.op("pe", lambda e: e.transpose(out=p2[:, 0:128], in_=z_[:, hh * 128:(hh + 1) * 128], identity=ident[:]),
                                          reads=[z_, ident], writes=[p2])
                                    kb.op("act", lambda e: e.copy(out=zk[:, hh, :], in_=p2[:, 0:128]), reads=[p2], writes=[zk], waw=False)
                                kb.dma("pool", UTOK[1][g:g + 256, cc * 128:(cc + 1) * 128].rearrange("(i p) c -> p i c", p=128), zk[:],
                                       reads=[zk], writes=[UTOK[1]])
                            else:
                                kb.op("dve", lambda e: e.tensor_tensor(out=z_[:], in0=z_[:], in1=x_[:], op=ALU.mult), reads=[z_, x_], writes=[z_])
                                kb.op("dve", lambda e: e.tensor_tensor(out=zo_[:], in0=z_[:], in1=g_[:], op=ALU.mult), reads=[z_, g_], writes=[zo_])
                                kb.dma("pool", L["YSv"][:, 2, cc, g:g + 256], zo_[:], reads=[zo_], writes=[YST])
                fwd_dft(kb, P, n, tab, load_rhs, consume)


def tview(ap, d, tok_lo, n):
    return ap[:, tok_lo:tok_lo + n, :]


def fm_transpose(kb, L, src, dst):
    for c in range(4):
        ps = kb.ps()
        kb.op("pe", lambda e: e.transpose(out=ps[:, 0:128], in_=src[:, c * 128:(c + 1) * 128], identity=L["ident"][:]), reads=[src, L["ident"]], writes=[ps])
        if c % 2:
            kb.op("act", lambda e: e.copy(out=dst[:, c, :], in_=ps[:, 0:128]), reads=[ps], writes=[dst], waw=False)
        else:
            kb.op("dve", lambda e: e.tensor_copy(out=dst[:, c, :], in_=ps[:, 0:128]), reads=[ps], writes=[dst], waw=False)


TBLK = 8


def rwkv_prep(kb, l, L):
    PT, RKVTOK, XD, YSC, BONUST, YRAW = (L[k] for k in ("PT", "RKVTOK", "XD", "YSC", "BONUST", "YRAW"))
    for b in range(NB):
        with Phase(kb) as P:
            g0 = gtok(b, 0)
            mu = P.sb("mu", [128, 12])
            cw = P.sb("cw", [128, 12, 3])
            kb.dma("sp", mu[:], L["rwmu_in"][l, :, :], writes=[mu])
            kb.op("dve", lambda e: e.tensor_scalar(out=cw[:, :, 0], in0=mu[:], scalar1=0.5, scalar2=None, op0=ALU.mult), reads=[mu], writes=[cw], waw=False)
            kb.op("dve", lambda e: e.tensor_scalar(out=cw[:, :, 2], in0=mu[:], scalar1=0.5, scalar2=None, op0=ALU.mult), reads=[mu], writes=[cw], waw=False)
            kb.op("dve", lambda e: e.tensor_scalar(out=cw[:, :, 1], in0=mu[:], scalar1=-1.0, scalar2=1.0, op0=ALU.mult, op1=ALU.add), reads=[mu], writes=[cw], waw=False)
            xin = [P.sb("xin%d" % i, [128, TT]) for i in range(2)]
            v32 = P.sb("v32", [128, 4, TT])
            vt = P.sb("vt", [128, 18, 512])
            for grp in range(3):
                for c4 in range(4):
                    c = grp * 4 + c4
                    xi = xin[c % 2]
                    kb.dma("sp", xi[:], PT[C_RKV + c * 128:C_RKV + (c + 1) * 128, g0:g0 + TT], reads=[PT], writes=[xi])
                    dwconv3(kb, (xi, xi[:]), (v32, v32[:, c4, :]), (cw, cw[:, c, :]), SEGS)
                tok_transpose(kb, L, v32, vt, 4)
                kb.dma("pool", RKVTOK[grp][g0:g0 + TT, :].rearrange("(i p) c -> p i c", p=128), vt[:], reads=[vt], writes=[RKVTOK[grp]])
    with Phase(kb) as P:
        def bc(name, src):
            t = P.sb(name, [128, 512])
            kb.dma("sp", t[:], src.to_broadcast([128, 512]), writes=[t])
            return t
        w0 = [bc("w0%d" % d, L["rww0_in"][l, d:d + 1, :]) for d in range(2)]
        a0 = [bc("a0%d" % d, L["rwa0_in"][l, d:d + 1, :]) for d in range(2)]
        k_k = bc("k_k", L["rwkk_in"][l, 0:1, :])
        k_a = bc("k_a", L["rwka_in"][l, 0:1, :])
        r_k = bc("r_k", L["rwrk_in"][l, 0:1, :])
        w2 = P.sb("w2", [128, 512])
        a2 = P.sb("a2", [128, 512])
        kb.dma("sp", w2[:], L["rww2_in"][l, :, :], writes=[w2])
        kb.dma("sp", a2[:], L["rwa2_in"][l, :, :], writes=[a2])
        zw = P.sb("zw", [128, TT])
        za = P.sb("za", [128, TT])
        NTL = 22
        tl_ = [[P.sb("t%d_%d" % (j, i), [128, 512]) for i in range(NTL)] for j in range(2)]
        sm = [P.sb("sm%d" % i, [128, 8]) for i in range(2)]
        bst = [P.sb("bst%d" % i, [128, 4, 128]) for i in range(2)]
        v3 = lambda ap: ap.rearrange("p (h k) -> p h k", k=64)
        it = 0
        for b in range(NB):
            g0 = gtok(b, 0)
            kb.dma("sp", zw[:], PT[C_ZW:C_ZW + 128, g0:g0 + TT], reads=[PT], writes=[zw])
            kb.dma("sp", za[:], PT[C_ZA:C_ZA + 128, g0:g0 + TT], reads=[PT], writes=[za])
            kb.op("act", lambda e: e.activation(out=zw[:], in_=zw[:], func=AF.Tanh), reads=[zw], writes=[zw])
            for i in range(TT // 128):
                T = tl_[it % 2]
                s_ = sm[it % 2]
                bs_ = bst[it % 2]
                it += 1
                g = g0 + i * 128
                r, k, v = T[0], T[1], T[2]
                for (dst, src) in ((r, RKVTOK[0]), (k, RKVTOK[1]), (v, RKVTOK[2])):
                    kb.dma("sp", dst[:], src[g:g + 128, :], reads=[src], writes=[dst])
                kx, sq, kk = T[3], T[4], T[5]
                kb.op("dve", lambda e: e.tensor_tensor(out=kx[:], in0=k[:], in1=k_k[:], op=ALU.mult), reads=[k, k_k], writes=[kx])
                kb.op("act", lambda e: e.activation(out=sq[:], in_=kx[:], func=AF.Square), reads=[kx], writes=[sq])
                kb.op("dve", lambda e: e.tensor_reduce(out=s_[:], in_=v3(sq[:]), axis=AX.X, op=ALU.add), reads=[sq], writes=[s_])
                kb.op("dve", lambda e: e.tensor_scalar(out=s_[:], in0=s_[:], scalar1=1e-12, scalar2=None, op0=ALU.max), reads=[s_], writes=[s_])
                kb.op("act", lambda e: e.activation(out=s_[:], in_=s_[:], func=AF.Sqrt), reads=[s_], writes=[s_])
                kb.op("dve", lambda e: e.reciprocal(out=s_[:], in_=s_[:]), reads=[s_], writes=[s_])
                kb.op("dve", lambda e: e.tensor_tensor(out=v3(kk[:]), in0=v3(kx[:]), in1=s_[:].unsqueeze(2).to_broadcast([128, 8, 64]), op=ALU.mult),
                      reads=[kx, s_], writes=[kk])
                kds = []
                for d in range(2):
                    wr, dec, aa, kd, bd = T[6 + d * 5], T[7 + d * 5], T[8 + d * 5], T[9 + d * 5], T[10 + d * 5]
                    pw = kb.ps()
                    kb.op("pe", lambda e: e.matmul(pw[:, :], lhsT=zw[d * 64:(d + 1) * 64, i * 128:(i + 1) * 128], rhs=w2[d * 64:(d + 1) * 64, :], start=True, stop=True),
                          reads=[zw, w2], writes=[pw])
                    pa = kb.ps()
                    kb.op("pe", lambda e: e.matmul(pa[:, :], lhsT=za[d * 64:(d + 1) * 64, i * 128:(i + 1) * 128], rhs=a2[d * 64:(d + 1) * 64, :], start=True, stop=True),
                          reads=[za, a2], writes=[pa])
                    kb.op("dve", lambda e: e.tensor_tensor(out=wr[:], in0=pw[:, :], in1=w0[d][:], op=ALU.add), reads=[pw, w0[d]], writes=[wr])
                    kb.op("act", lambda e: e.activation(out=wr[:], in_=wr[:], func=AF.Sigmoid), reads=[wr], writes=[wr])
                    kb.op("act", lambda e: e.activation(out=dec[:], in_=wr[:], func=AF.Exp, scale=-math.exp(-0.5)), reads=[wr], writes=[dec])
                    kb.op("dve", lambda e: e.tensor_tensor(out=aa[:], in0=pa[:, :], in1=a0[d][:], op=ALU.add), reads=[pa, a0[d]], writes=[aa])
                    kb.op("act", lambda e: e.activation(out=aa[:], in_=aa[:], func=AF.Sigmoid), reads=[aa], writes=[aa])
                    kb.op("dve", lambda e: e.tensor_tensor(out=bd[:], in0=kk[:], in1=aa[:], op=ALU.mult), reads=[kk, aa], writes=[bd])
                    kb.op("dve", lambda e: e.scalar_tensor_tensor(out=kd[:], in0=aa[:], scalar=-1.0, in1=k_a[:], op0=ALU.add, op1=ALU.mult), reads=[aa, k_a], writes=[kd])
                    kb.op("dve", lambda e: e.scalar_tensor_tensor(out=kd[:], in0=kd[:], scalar=1.0, in1=k[:], op0=ALU.add, op1=ALU.mult), reads=[kd, k], writes=[kd])
                    kds.append(kd)
                    for fi, src in enumerate((dec, kk, bd, kd, r)):
                        kb.dma("pool", tview(XD[d, b * 8:(b + 1) * 8, :, fi * 64:(fi + 1) * 64], d, i * 128, 128).rearrange("h t k -> t h k"),
                               v3(src[:]), reads=[src], writes=[XD])
                    v4 = v[:].rearrange("p (h g i) -> p h g i", h=8, g=8)
                    for vg in range(8):
                        kb.dma("pool", tview(L["XV"][d, vg, b * 8:(b + 1) * 8, :, :], d, i * 128, 128).rearrange("h t i -> t h i"),
                               v4[:, :, vg, :], reads=[v], writes=[L["XV"]])
                rk, ks, bo = T[16], T[17], T[18]
                kb.op("dve", lambda e: e.tensor_tensor(out=rk[:], in0=r[:], in1=r_k[:], op=ALU.mult), reads=[r, r_k], writes=[rk])
                kb.op("dve", lambda e: e.tensor_tensor(out=ks[:], in0=kds[0][:], in1=kds[1][:], op=ALU.add), reads=[kds[0], kds[1]], writes=[ks])
                kb.op("dve", lambda e: e.tensor_tensor(out=ks[:], in0=ks[:], in1=rk[:], op=ALU.mult), reads=[ks, rk], writes=[ks])
                kb.op("dve", lambda e: e.tensor_reduce(out=s_[:], in_=v3(ks[:]), axis=AX.X, op=ALU.add), reads=[ks, s_], writes=[s_])
                kb.op("dve", lambda e: e.tensor_tensor(out=v3(bo[:]), in0=v3(v[:]), in1=s_[:].unsqueeze(2).to_broadcast([128, 8, 64]), op=ALU.mult),
                      reads=[v, s_], writes=[bo])
                fm_transpose(kb, L, bo, bs_)
                kb.dma("pool", BONUST.t.rearrange("(c p) g -> p c g", p=128)[:, :, g:g + 128], bs_[:], reads=[bs_], writes=[BONUST])


def rwkv_scan(kb, l, L, others, budget):
    XD, YSC = L["XD"], L["YSC"]
    with Phase(kb) as P:
        S = P.sb("S", [128, 2, 8, 64])
        T1 = P.sb("T1", [128, 2, 8, 64])
        sa = P.sb("sa", [128, 2, 8])
        X = [P.sb("X%d" % i, [128, 2, TBLK, 320]) for i in range(2)]
        V = [P.sb("V%d" % i, [128, 2, TBLK, 8]) for i in range(2)]
        Yb = [P.sb("Y%d" % i, [128, 2, TBLK, 8]) for i in range(2)]
        T3 = [P.sb("T3_%d" % i, [128, 2, 8, 64]) for i in range(4)]
        T4 = [P.sb("T4_%d" % i, [128, 2, 8, 64]) for i in range(2)]
        S_alt = P.sb("S_alt", [128, 2, 8, 64])
        S2 = [S, S_alt]
        kb.op("dve", lambda e: e.memset(S[:], 0.0), writes=[S])
        nblk = TT // TBLK
        SH = [128, 2, 8, 64]
        bk = lambda ap: ap.unsqueeze(2).to_broadcast(SH)
        bv = lambda ap: ap.unsqueeze(3).to_broadcast(SH)

        XV = L["XV"]

        def tok0_b(j):
            s0 = j * TBLK
            if s0 < TC:
                return TC - s0 - TBLK
            return TT - (s0 - TC) - TBLK

        def load(j):
            x, v = X[j % 2], V[j % 2]
            s0 = j * TBLK
            tb = tok0_b(j)
            kb.dma("pool", x[:, 0, :, :].rearrange("p t k -> p (t k)"),
                   XD[0, :, s0:s0 + TBLK, :].rearrange("q t k -> q (t k)").unsqueeze(0).to_broadcast([8, 16, TBLK * 320]), reads=[XD], writes=[x])
            for vg in range(8):
                kb.dma("pool", x[vg * 16:(vg + 1) * 16, 1, :, :], XD[1, :, tb:tb + TBLK, :][:, ::-1, :], reads=[XD], writes=[x])
            kb.dma("pool", v[:, 0, :, :].rearrange("p t i -> p (t i)"),
                   XV[0, :, :, s0:s0 + TBLK, :].rearrange("g q t i -> (g q) (t i)"), reads=[XV], writes=[v])
            kb.dma("pool", v[:, 1, :, :], XV[1, :, :, tb:tb + TBLK, :].rearrange("g q t i -> (g q) t i")[:, ::-1, :], reads=[XV], writes=[v])
        def store(j):
            y = Yb[j % 2]
            kb.dma("pool", YSC[0, :, :, j * TBLK:(j + 1) * TBLK, :].rearrange("g q t i -> (g q) (t i)"),
                   y[:, 0, :, :].rearrange("p t i -> p (t i)"), reads=[y], writes=[YSC])
            tb = tok0_b(j)
            kb.dma("pool", YSC[1, :, :, tb:tb + TBLK, :].rearrange("g q t i -> (g q) t i")[:, ::-1, :], y[:, 1, :, :], reads=[y], writes=[YSC])

        load(0)
        load(1)
        co = Interleave(others) if others is not None else None
        kb.co = co
        deferred = []
        k3 = 0
        pend_red = [None]
        for j in range(nblk):
            x, v, y = X[j % 2], V[j % 2], Yb[j % 2]
            for c in range(TBLK):
                t3 = T3[k3 % 4]
                k3 += 1
                kb.op("pool", lambda e: e.tensor_tensor(out=t3[:], in0=bv(v[:, :, c, :]), in1=bk(x[:, :, c, 192:256]), op=ALU.mult), reads=[v, x], writes=[t3])
                if c == 2:
                    for f in deferred:
                        f()
                    deferred = []
                Sa, Sb = S2[(k3 - 1) % 2], S2[k3 % 2]
                t4 = T4[(k3 - 1) % 2]
                kb.op("dve", lambda e: e.tensor_tensor(out=T1[:], in0=Sa[:], in1=bk(x[:, :, c, 64:128]), op=ALU.mult), reads=[Sa, x], writes=[T1])
                kb.op("dve", lambda e: e.tensor_reduce(out=sa[:], in_=T1[:], axis=AX.X, op=ALU.add), reads=[T1], writes=[sa])
                kb.op("dve", lambda e: e.tensor_tensor(out=Sb[:], in0=Sa[:], in1=bk(x[:, :, c, 0:64]), op=ALU.mult), reads=[Sa, x], writes=[Sb])
                kb.op("dve", lambda e: e.tensor_tensor(out=T1[:], in0=bv(sa[:]), in1=bk(x[:, :, c, 128:192]), op=ALU.mult), reads=[sa, x], writes=[T1])
                kb.op("dve", lambda e: e.tensor_tensor(out=Sb[:], in0=Sb[:], in1=T1[:], op=ALU.subtract), reads=[Sb, T1], writes=[Sb])
                kb.op("dve", lambda e: e.tensor_tensor(out=Sb[:], in0=Sb[:], in1=t3[:], op=ALU.add), reads=[Sb, t3], writes=[Sb])
                kb.op("pool", lambda e: e.tensor_tensor(out=t4[:], in0=Sb[:], in1=bk(x[:, :, c, 256:320]), op=ALU.mult), reads=[Sb, x], writes=[t4])
                if pend_red[0] is not None:
                    p4, py, pc = pend_red[0]
                    kb.op("dve", lambda e: e.tensor_reduce(out=py[:, :, pc, :], in_=p4[:], axis=AX.X, op=ALU.add), reads=[p4], writes=[py], waw=False)
                pend_red[0] = (t4, y, c)
            deferred.append(lambda j=j: store(j))
            if j + 2 < nblk:
                deferred.append(lambda j=j: load(j + 2))
            if co is not None:
                co.give(budget)
        if pend_red[0] is not None:
            p4, py, pc = pend_red[0]
            kb.op("dve", lambda e: e.tensor_reduce(out=py[:, :, pc, :], in_=p4[:], axis=AX.X, op=ALU.add), reads=[p4], writes=[py], waw=False)
        for f in deferred:
            f()
        if co is not None:
            if not co.done:
                print("[build] interleave: worker still had work after the scan (emitted %d ops so far)" % co.nops)
            co.finish()
            print("[build] interleave: worker ops=%d, scan blocks=%d, budget=%d" % (co.nops, nblk, budget))
        kb.co = None


def rwkv_readout(kb, l, L):
    PT, RKVTOK, XD, YSC, BONUST, YRAW = (L[k] for k in ("PT", "RKVTOK", "XD", "YSC", "BONUST", "YRAW"))
    with Phase(kb) as P:
        ya = [P.sb("ya%d" % i, [128, 512]) for i in range(2)]
        yb_ = [P.sb("yb%d" % i, [128, 512]) for i in range(2)]
        st = [P.sb("st%d" % i, [128, 4, 128]) for i in range(2)]
        v3 = lambda ap: ap.rearrange("p (h k) -> p h k", k=64)
        it = 0
        for b in range(NB):
            for i in range(TT // 128):
                a_, b_, s_ = ya[it % 2], yb_[it % 2], st[it % 2]
                it += 1
                g = gtok(b, i * 128)
                for d, dst in ((0, a_), (1, b_)):
                    d4 = dst[:].rearrange("p (h g i) -> p h g i", h=8, g=8)
                    for vg in range(8):
                        kb.dma("sp", d4[:, :, vg, :], tview(YSC[d, vg, b * 8:(b + 1) * 8, :, :], d, i * 128, 128).rearrange("h t i -> t h i"),
                               reads=[YSC], writes=[dst])
                kb.op("dve", lambda e: e.tensor_tensor(out=a_[:], in0=a_[:], in1=b_[:], op=ALU.add), reads=[a_, b_], writes=[a_])
                fm_transpose(kb, L, a_, s_)
                kb.dma("pool", YRAW.t.rearrange("(c p) g -> p c g", p=128)[:, :, g:g + 128], s_[:], reads=[s_], writes=[YRAW])
    with Phase(kb) as P:
        gw = P.sb("gw", [128, 4])
        gb = P.sb("gb", [128, 4])
        kb.dma("sp", gw[:], L["rwgw_in"][l, :, :], writes=[gw])
        kb.dma("sp", gb[:], L["rwgb_in"][l, :, :], writes=[gb])
        groupnorm_gate_readout(kb, P, L, YRAW, C_GA, gw, gb, 64e-5, 0, BONUST)


def gtok(b, t0):
    return b * TT + t0


def build_program(depth=DEPTH, debug=(), ext_in=(), skip=(), budget=125):
    nc = bass.Bass("TRN2", target_bir_lowering=False)
    kb = KB(nc)

    def din(name, shape, dt=F32):
        return Buf(nc.dram_tensor(name, list(shape), dt, kind="ExternalInput").ap(), name)

    x_in = din("x", [NB, TL, D])
    ctx_in = din("ctx", [NB, TC, D])
    fnw_in = din("final_norm_w", [128, 8])
    ident_in = din("ident", [128, 128])
    out_d = Buf(nc.dram_tensor("out", [NB, TL, D], F32, kind="ExternalOutput").ap(), "out")

    cond_in = din("condT", [128, 8, 3])
    adaw_in = din("ada_w", [DEPTH, 128, 8, 3 * D])
    adab_in = din("ada_b", [DEPTH, 128, 24])
    normw_in = din("norm_w", [DEPTH, 128, 8])
    inw_in = din("in_w", [DEPTH, NBLK, 128, 8, 128])

    mergew_in = din("merge_w", [DEPTH, 32, 128, 8, 128])
    mergeb_in = din("merge_b", [DEPTH, 128, 32])
    branchw_in = din("branch_w", [DEPTH, 32, 128, 4, 128])
    outw_in = din("out_w", [DEPTH, 8, 128, 8, 128])

    maskF_in = din("maskF", [128, 4, 512])
    maskB_in = din("maskB", [128, 4, 512])
    blk64_in = din("blk64", [128, 128])
    ropec_in = din("rope_cos", [128, TL])
    ropes_in = din("rope_sin", [128, TL])
    ssdcw_in = din("ssd_conv_w", [DEPTH, 128, 8, 3])
    ssdcb_in = din("ssd_conv_b", [DEPTH, 128, 8])
    ssddtb_in = din("ssd_dt_bias", [DEPTH, 16, 1])
    ssdalog_in = din("ssd_a_log", [DEPTH, 16, 1])
    ssdD_in = din("ssd_d", [DEPTH, 128, 4])
    ssdnw_in = din("ssd_norm_w", [DEPTH, 128, 4])
    retlg_in = din("ret_log_decay", [DEPTH, 16, 1])
    retgw_in = din("ret_gn_w", [DEPTH, 128, 4])
    retgb_in = din("ret_gn_b", [DEPTH, 128, 4])

    hycw_in = din("hy_conv_w", [DEPTH, 128, 12, 3])
    hycb_in = din("hy_conv_b", [DEPTH, 128, 12])
    hyw1_in = din("hy_w1", [DEPTH, 33, 64])
    hyb1_in = din("hy_b1", [DEPTH, 64, 1])
    hyw2_in = din("hy_w2", [DEPTH, 64, 64])
    hyb2_in = din("hy_b2", [DEPTH, 64, 1])
    hyw3_in = din("hy_w3", [DEPTH, 64, 1024])
    hyfreq_in = din("hy_freq", [DEPTH, 64, 1])
    hybias_in = din("hy_bias", [DEPTH, 128, 2, 4])
    hytab = {}
    for n_ in (TL, TC):
        k_ = n_ // 128
        hytab[n_] = dict(feats=din("hy_feats%d" % n_, [33, n_]), window=din("hy_window%d" % n_, [128, 4, n_]),
                         Fc=din("hy_Fc%d" % n_, [k_, 128, k_, 128], BF16), Fs=din("hy_Fs%d" % n_, [k_, 128, k_, 128], BF16),
                         Gre=din("hy_Gre%d" % n_, [n_ // 256, 128, k_, 256], BF16), Gim=din("hy_Gim%d" % n_, [n_ // 256, 128, k_, 256], BF16))

    rwmu_in = din("rwkv_mu", [DEPTH, 128, 12])
    rww0_in = din("rwkv_w0", [DEPTH, 2, 512])
    rww2_in = din("rwkv_w2", [DEPTH, 128, 512])
    rwa0_in = din("rwkv_a0", [DEPTH, 2, 512])
    rwa2_in = din("rwkv_a2", [DEPTH, 128, 512])
    rwkk_in = din("rwkv_k_k", [DEPTH, 1, 512])
    rwka_in = din("rwkv_k_a", [DEPTH, 1, 512])
    rwrk_in = din("rwkv_r_k", [DEPTH, 1, 512])
    rwgw_in = din("rwkv_gn_w", [DEPTH, 128, 4])
    rwgb_in = din("rwkv_gn_b", [DEPTH, 128, 4])

    def scratch(name, shape, dt=F32):
        if name in ext_in:
            return Buf(nc.dram_tensor(name, list(shape), dt, kind="ExternalInput").ap(), name)
        if name in debug:
            return Buf(nc.dram_tensor(name, list(shape), dt, kind="ExternalOutput").ap(), name)
        return kb.dram(name, shape, dt)

    HT = scratch("HT", [D, NT])
    UTd = scratch("UTd", [D, NT], BF16)
    PT = scratch("PT", [NCOLS, NT])
    MWB = scratch("MWB", [32, 128, 8, 128], BF16)
    BWB = scratch("BWB", [32, 128, 4, 128], BF16)
    OWB = scratch("OWB", [8, 128, 8, 128], BF16)
    YST = scratch("YST", [4 * DBR, NT], BF16)
    AT = scratch("AT", [NB, 16, TT])
    YRAW = scratch("YRAW", [DBR, NT])
    XS = scratch("XS", [DBR, NT])
    HYC = scratch("HYC", [3 * DBR, NT])
    UTOK = [scratch("UTOK%d" % i, [NT, DBR], BF16) for i in range(2)]
    Z1T = scratch("Z1T", [DBR, NT])
    HTOK = {n_: scratch("HTOK%d" % n_, [n_, 2 * DBR], BF16) for n_ in (TL, TC)}
    HF = {n_: scratch("HF%d" % n_, [2, n_ // 128, 128, 4, 256]) for n_ in (TL, TC)}
    RKVTOK = [scratch("RKVTOK%d" % i, [NT, DBR]) for i in range(3)]
    XD = scratch("XD", [2, 16, TT, 320])
    XV = scratch("XV", [2, 8, 16, TT, 8])
    YSC = scratch("YSC", [2, 8, 16, TT, 8])
    BONUST = scratch("BONUST", [DBR, NT])
    HTv = HT.t.rearrange("(c p) g -> p c g", p=128)
    YSv = YST.t.rearrange("(m c p) g -> p m c g", p=128, c=4)
    UTv = UTd.t.rearrange("(c p) g -> p c g", p=128)

    with Phase(kb) as G:
        ident = G.sb("ident", [128, 128])
        ones = G.sb("ones", [128, 128])
        fnw = G.sb("fnw", [128, 8])
        kb.dma("sp", ident[:], ident_in[:, :], writes=[ident])
        kb.dma("sp", fnw[:], fnw_in[:, :], writes=[fnw])
        kb.op("dve", lambda e: e.memset(ones[:], 1.0), writes=[ones])
        maskF = G.sb("maskF", [128, 4, 512])
        maskB = G.sb("maskB", [128, 4, 512])
        blk64 = G.sb("blk64", [128, 128])
        kb.dma("sp", maskF[:], maskF_in[:, :, :], writes=[maskF])
        kb.dma("sp", maskB[:], maskB_in[:, :, :], writes=[maskB])
        kb.dma("sp", blk64[:], blk64_in[:, :], writes=[blk64])

        with Phase(kb) as P:
            xin = [P.sb("xin%d" % i, [128, D]) for i in range(2)]
            stg = [P.sb("stg%d" % i, [128, 8, 512]) for i in range(2)]
            it = 0
            for b in range(NB):
                for (t0, tl) in TILES:
                    sg = stg[it % 2]
                    it += 1
                    for s in range(tl // 128):
                        xi = xin[s % 2]
                        tt = t0 + s * 128
                        src = ctx_in[b, tt:tt + 128, :] if tt < TC else x_in[b, tt - TC:tt - TC + 128, :]
                        kb.dma("sp", xi[:], src, writes=[xi])
                        for c in range(8):
                            ps = kb.ps()
                            kb.op("pe", lambda e: e.transpose(out=ps[:, 0:128], in_=xi[:, c * 128:(c + 1) * 128], identity=ident[:]),
                                  reads=[xi, ident], writes=[ps])
                            eng = "act" if c % 2 else "dve"
                            if eng == "act":
                                kb.op("act", lambda e: e.copy(out=sg[:, c, s * 128:(s + 1) * 128], in_=ps[:, 0:128]),
                                      reads=[ps], writes=[sg], waw=False)
                            else:
                                kb.op("dve", lambda e: e.tensor_copy(out=sg[:, c, s * 128:(s + 1) * 128], in_=ps[:, 0:128]),
                                      reads=[ps], writes=[sg], waw=False)
                    g0 = gtok(b, t0)
                    kb.dma("pool", HT.t.rearrange("(c p) g -> p c g", p=128)[:, :, g0:g0 + tl], sg[:, :, 0:tl],
                           reads=[sg], writes=[HT])

        for l in range(depth):
            with Phase(kb) as LP:
                modA = LP.sb("modA", [128, 8, 3])
                modS = LP.sb("modS", [128, 8, 3])
                modG = LP.sb("modG", [128, 8, 3])
                if "front" not in skip:
                    layer_front(kb, l, locals())
                LL = locals()

                def others():
                    if "hy" not in skip:
                        layer_hyena(kb, l, LL)
                    if "ssd" not in skip:
                        layer_ssd(kb, l, LL)
                    if "ret" not in skip:
                        layer_ret(kb, l, LL)
                if "rwkv" not in skip:
                    rwkv_prep(kb, l, LL)
                    if "nointer" in skip:
                        rwkv_scan(kb, l, LL, None, 0)
                        others()
                    else:
                        rwkv_scan(kb, l, LL, others, budget)
                    rwkv_readout(kb, l, LL)
                else:
                    others()
                if "merge" not in skip:
                    layer_merge(kb, l, locals())

        with Phase(kb) as P:
            hin = [P.sb("hin%d" % i, [128, 8, 512]) for i in range(2)]
            sq = P.sb("sq", [128, 8, 512])
            rstd = P.sb("rstd", [128, 512])
            yn = P.sb("yn", [128, 8, 512])
            ot = [P.sb("ot%d" % i, [128, D]) for i in range(2)]
            it = 0
            oi = 0
            for b in range(NB):
                for (t0, tl) in TILES[1:]:
                    hi = hin[it % 2]
                    it += 1
                    g0 = gtok(b, t0)
                    kb.dma("sp", hi[:], HT.t.rearrange("(c p) g -> p c g", p=128)[:, :, g0:g0 + tl], reads=[HT], writes=[hi])
                    kb.op("act", lambda e: e.activation(out=sq[:], in_=hi[:], func=AF.Square), reads=[hi], writes=[sq])
                    ps = kb.ps()
                    for c in range(8):
                        kb.op("pe", lambda e: e.matmul(ps[:], lhsT=ones[:], rhs=sq[:, c, :], start=(c == 0), stop=(c == 7)),
                              reads=[ones, sq], writes=[ps])
                    rsqrt(kb, rstd, rstd[:], ps, ps[:], 1.0 / D, 1e-6)
                    for c in range(8):
                        kb.op("dve", lambda e: e.scalar_tensor_tensor(out=yn[:, c, :], in0=hi[:, c, :], scalar=fnw[:, c:c + 1], in1=rstd[:],
                                                                      op0=ALU.mult, op1=ALU.mult),
                              reads=[hi, fnw, rstd], writes=[yn], waw=False)
                    for s in range(tl // 128):
                        o = ot[oi % 2]
                        oi += 1
                        for c in range(8):
                            ps2 = kb.ps()
                            kb.op("pe", lambda e: e.transpose(out=ps2[:, 0:128], in_=yn[:, c, s * 128:(s + 1) * 128], identity=ident[:]),
                                  reads=[yn, ident], writes=[ps2])
                            if c % 2:
                                kb.op("act", lambda e: e.copy(out=o[:, c * 128:(c + 1) * 128], in_=ps2[:, 0:128]), reads=[ps2], writes=[o], waw=False)
                            else:
                                kb.op("dve", lambda e: e.tensor_copy(out=o[:, c * 128:(c + 1) * 128], in_=ps2[:, 0:128]), reads=[ps2], writes=[o], waw=False)
                        tt = t0 - TC + s * 128
                        kb.dma("pool", out_d[b, tt:tt + 128, :], o[:], reads=[o], writes=[out_d])
    kb.barrier()
    return nc


_CACHE = {}


def _colmap():
    m = np.full(NCOLS, -1, dtype=np.int64)
    def put(dst, src, n):
        m[dst:dst + n] = np.arange(src, src + n)
    put(C_RKV, 0, 1536); put(C_ZW, 1536, 128); put(C_ZA, 1664, 128); put(C_GA, 1792, 512)
    put(C_XBC, 2304, 1024); put(C_GZ, 3344, 512); put(C_HY, 3856, 1536); put(C_GH, 5392, 512)
    put(C_QKV, 5904, 1536); put(C_GR, 7440, 512); put(C_DT, 3328, 16)
    d = np.arange(512)
    r = d % 32
    partner = np.where(r < 16, d + 16, d - 16)
    m[C_QSW:C_QSW + 512] = 5904 + partner
    m[C_KSW:C_KSW + 512] = 5904 + 512 + partner
    return m


_HYT = {}


def _hy_tables(n):
    if n in _HYT:
        return _HYT[n]
    f = np.float32
    nch = n // 128
    Lp = 2 * n
    j = np.arange(n, dtype=np.float32)
    t = (j / np.float32(max(n - 1, 1))).astype(f)
    ang = (np.float32(2.0 * math.pi) * j / np.float32(n)).astype(f)
    bands = np.linspace(1e-4, 15, 16).astype(f)
    feats = np.concatenate([t[:, None], np.cos(ang[:, None] * bands), -np.sin(ang[:, None] * bands)], axis=-1).astype(f)
    lag = (np.abs(j - n // 2) / np.float32(n / 2.0)).astype(f)
    deltas = np.abs(np.linspace(math.log(1e-2) / 1.5, math.log(1e-2) / 0.3, 512)).astype(f)
    window = (np.exp(-lag[:, None] * deltas[None, :]) + np.float32(0.05)).astype(f)
    sidx = np.arange(n, dtype=np.float64)[:, None]
    fidx = np.arange(n, dtype=np.float64)[None, :]
    Fc = np.cos(2 * np.pi * sidx * fidx / Lp)
    Fs = -np.sin(2 * np.pi * sidx * fidx / Lp)
    Fs[:, 0] = np.cos(np.pi * sidx[:, 0])
    tau = (np.arange(n, dtype=np.float64) + n // 2)[None, :]
    fcol = np.arange(n, dtype=np.float64)[:, None]
    wf = np.where(fcol == 0, 1.0, 2.0)
    Gre = wf / Lp * np.cos(2 * np.pi * fcol * tau / Lp)
    Gim = -2.0 / Lp * np.sin(2 * np.pi * fcol * tau / Lp)
    Gim[0, :] = np.cos(np.pi * tau[0]) / Lp

    def lay_f(M):
        return np.ascontiguousarray(M.reshape(nch, 128, nch, 128).transpose(2, 1, 0, 3)).astype(ml_dtypes.bfloat16)

    def lay_g(M):
        return np.ascontiguousarray(M.reshape(nch, 128, n // 256, 256).transpose(2, 1, 0, 3)).astype(ml_dtypes.bfloat16)

    out = dict(feats=np.ascontiguousarray(feats.T), window=np.ascontiguousarray(window.T.reshape(4, 128, n).transpose(1, 0, 2)),
               Fc=lay_f(Fc), Fs=lay_f(Fs), Gre=lay_g(Gre), Gim=lay_g(Gim))
    _HYT[n] = out
    return out


def _pc(a):
    n = a.shape[-1] // 128
    return np.ascontiguousarray(np.swapaxes(a.reshape(a.shape[:-1] + (n, 128)), -1, -2))


def shared_inputs(inp):
    f = np.float32
    sh = {}
    sh["final_norm_w"] = _pc(inp["final_norm_w"].astype(f))
    sh["ident"] = np.eye(128, dtype=f)
    L = DEPTH
    sh["ada_w"] = np.ascontiguousarray(inp["ada_w"].reshape(L, 8, 128, 3 * D).transpose(0, 2, 1, 3))
    sh["ada_b"] = _pc(inp["ada_b"])
    sh["norm_w"] = _pc(inp["norm_w"])
    sh["merge_w"] = np.ascontiguousarray(inp["merge_w"].reshape(L, 8, 128, 32, 128).transpose(0, 3, 2, 1, 4))
    sh["merge_b"] = _pc(inp["merge_b"])
    sh["branch_w"] = np.ascontiguousarray(inp["branch_w"].reshape(L, 4, 4, 128, 8, 128).transpose(0, 1, 4, 3, 2, 5).reshape(L, 32, 128, 4, 128))
    sh["out_w"] = np.ascontiguousarray(inp["out_w"].reshape(L, 8, 128, 8, 128).transpose(0, 3, 2, 1, 4))
    m_ = np.arange(128)[:, None, None]
    o_ = np.arange(4)[None, :, None]
    n_ = np.arange(512)[None, None, :]
    sh["maskF"] = np.where(n_ >= o_ * 128 + m_, 0.0, -30000.0).astype(f)
    sh["maskB"] = np.where(o_ * 128 + m_ >= n_, 0.0, -30000.0).astype(f)
    blk = np.zeros((128, 128), f)
    blk[:64, :64] = 1.0 / 64
    blk[64:, 64:] = 1.0 / 64
    sh["blk64"] = blk
    t = np.arange(TL)
    rows = (t // 64).astype(np.float64)
    cols = (t % 64).astype(np.float64)
    inv = (10000.0 ** (-np.arange(0, 32, 2, dtype=np.float32) / np.float32(32))).astype(np.float32)
    dch = np.arange(64)
    half = dch // 32
    r = dch % 32
    ang = np.where(half[:, None] == 0, rows[None, :], cols[None, :]).astype(np.float32) * inv[r % 16][:, None]
    cosT = np.cos(ang).astype(f)
    sinT = (np.sin(ang) * np.where(r < 16, -1.0, 1.0)[:, None]).astype(f)
    sh["rope_cos"] = np.ascontiguousarray(np.concatenate([cosT, cosT], 0))
    sh["rope_sin"] = np.ascontiguousarray(np.concatenate([sinT, sinT], 0))
    sh["ssd_conv_w"] = np.ascontiguousarray(inp["ssd_conv_w"].reshape(L, 8, 128, 3).transpose(0, 2, 1, 3))
    sh["ssd_conv_b"] = _pc(inp["ssd_conv_b"])
    sh["ssd_dt_bias"] = np.ascontiguousarray(inp["ssd_dt_bias"].reshape(L, 16, 1))
    sh["ssd_a_log"] = np.ascontiguousarray(inp["ssd_a_log"].reshape(L, 16, 1))
    sh["ssd_d"] = _pc(np.repeat(inp["ssd_d"], 64, axis=-1))
    sh["ssd_norm_w"] = _pc(inp["ssd_norm_w"])
    sh["ret_log_decay"] = np.ascontiguousarray(inp["ret_log_decay"].reshape(L, 16, 1))
    sh["ret_gn_w"] = _pc(inp["ret_gn_w"])
    sh["ret_gn_b"] = _pc(inp["ret_gn_b"])
    sh["hy_conv_w"] = np.ascontiguousarray(inp["hy_conv_w"].reshape(L, 12, 128, 3).transpose(0, 2, 1, 3))
    sh["hy_conv_b"] = _pc(inp["hy_conv_b"])
    sh["hy_w1"] = np.ascontiguousarray(inp["hy_w1"])
    sh["hy_b1"] = np.ascontiguousarray(inp["hy_b1"].reshape(L, 64, 1))
    sh["hy_w2"] = np.ascontiguousarray(inp["hy_w2"])
    sh["hy_b2"] = np.ascontiguousarray(inp["hy_b2"].reshape(L, 64, 1))
    sh["hy_w3"] = np.ascontiguousarray(inp["hy_w3"])
    sh["hy_freq"] = np.ascontiguousarray(inp["hy_freq"].reshape(L, 64, 1))
    sh["hy_bias"] = np.ascontiguousarray(inp["hy_bias"].reshape(L, 2, 4, 128).transpose(0, 3, 1, 2))
    for n_ in (TL, TC):
        for k_, v_ in _hy_tables(n_).items():
            sh["hy_%s%d" % (k_, n_)] = v_
    sh["rwkv_mu"] = _pc(inp["rwkv_mu"].reshape(L, 1536))
    sh["rwkv_w0"] = np.ascontiguousarray(inp["rwkv_w0"])
    sh["rwkv_w2"] = np.ascontiguousarray(inp["rwkv_w2"].reshape(L, 128, 512))
    sh["rwkv_a0"] = np.ascontiguousarray(inp["rwkv_a0"])
    sh["rwkv_a2"] = np.ascontiguousarray(inp["rwkv_a2"].reshape(L, 128, 512))
    sh["rwkv_k_k"] = np.ascontiguousarray(inp["rwkv_k_k"].reshape(L, 1, 512))
    sh["rwkv_k_a"] = np.ascontiguousarray(inp["rwkv_k_a"].reshape(L, 1, 512))
    sh["rwkv_r_k"] = np.ascontiguousarray(inp["rwkv_r_k"].reshape(L, 1, 512))
    sh["rwkv_gn_w"] = _pc(inp["rwkv_gn_w"])
    sh["rwkv_gn_b"] = _pc(inp["rwkv_gn_b"])
    cm = _colmap()
    w = inp["in_w"]
    wa = np.zeros((L, D, NCOLS), dtype=f)
    ok = cm >= 0
    wa[:, :, ok] = w[:, :, cm[ok]]
    sh["in_w"] = np.ascontiguousarray(wa.reshape(L, 8, 128, NBLK, 128).transpose(0, 3, 2, 1, 4))
    return sh


def core_inputs(inp, sh, core):
    bs = slice(core * NB, (core + 1) * NB)
    m = dict(sh)
    m["x"] = np.ascontiguousarray(inp["x"][bs])
    m["ctx"] = np.ascontiguousarray(inp["ctx"][bs])
    cond = np.concatenate([inp["c"][bs], inp["c_ctx"][None, :]], axis=0)
    m["condT"] = np.ascontiguousarray(cond.reshape(3, 8, 128).transpose(2, 1, 0))
    return m


def kernel(**inputs):
    inp = {k: np.asarray(v) for k, v in inputs.items()}
    if "nc" not in _CACHE:
        _CACHE["nc"] = build_program()
    nc = _CACHE["nc"]
    sh = shared_inputs(inp)
    in_maps = [core_inputs(inp, sh, core) for core in range(NCORES)]
    res = run_bass_kernel_spmd(nc, in_maps, core_ids=list(range(NCORES)))
    out = np.concatenate([r["out"] for r in res.results], axis=0)
    return out.astype(np.float32)
```

```python
import math
import threading
from contextlib import ExitStack
import numpy as np
import ml_dtypes
import concourse.bass as bass
import concourse.mybir as mybir
from concourse.bass_utils import run_bass_kernel_spmd

F32 = mybir.dt.float32
BF16 = mybir.dt.bfloat16
ALU = mybir.AluOpType
AF = mybir.ActivationFunctionType
AX = mybir.AxisListType

NCORES = 8
D = 1024
NB = 2
TC = 256
TL = 2048
TT = TC + TL
NT = NB * TT
DEPTH = 4
DBR = 512
NCOLS = 9088
NBLK = NCOLS // 128
C_RKV, C_ZW, C_ZA, C_GA = 0, 1536, 1664, 1792
C_XBC, C_GZ = 2304, 3328
C_HY, C_GH = 3840, 5376
C_QKV, C_GR = 5888, 7424
C_QSW, C_KSW, C_DT = 7936, 8448, 8960
TILES = [(0, 256), (256, 512), (768, 512), (1280, 512), (1792, 512)]


class Buf:
    def __init__(self, t, name=""):
        self.t = t
        self.name = name
        self.w = {}
        self.r = {}

    def __getitem__(self, k):
        return self.t[k]


class KB:
    NDS = 8

    def __init__(self, nc):
        self.nc = nc
        self.eng = {"pe": nc.tensor, "act": nc.scalar, "dve": nc.vector, "pool": nc.gpsimd, "sp": nc.sync}
        self.sem = {}
        self.cnt = {}
        self.seen = {e: {} for e in self.eng}
        for e in self.eng:
            self.sem[e] = nc.alloc_semaphore("s_" + e)
            self.cnt[e] = 0
        self.dq = {}
        for q in ("sp", "act", "pool"):
            for j in range(self.NDS):
                k = "d_%s%d" % (q, j)
                self.sem[k] = nc.alloc_semaphore(k)
                self.cnt[k] = 0
            self.dq[q] = 0
        self.psum = [Buf(nc.alloc_psum_tensor("ps%d" % i, [128, 512], F32), "ps%d" % i) for i in range(8)]
        self.psi = 0
        self.ndram = 0
        self.co = None

    def _wait(self, e, key, val):
        if val <= 0:
            return
        if self.seen[e].get(key, 0) < val:
            self.eng[e].wait_ge(self.sem[key], val)
            self.seen[e][key] = val

    def _deps(self, e, reads, writes, waw):
        for b in reads:
            for k, v in b.w.items():
                self._wait(e, k, v)
        for b in writes:
            for k, v in b.r.items():
                self._wait(e, k, v)
            if waw:
                for k, v in b.w.items():
                    if not (e == "pe" and k == "pe"):
                        self._wait(e, k, v)

    def _record(self, key, val, reads, writes, waw):
        for b in reads:
            if b.r.get(key, 0) < val:
                b.r[key] = val
        for b in writes:
            if waw:
                b.w = {key: val}
                b.r = {}
            else:
                b.w[key] = val

    def op(self, e, emit, reads=(), writes=(), waw=True):
        if self.co is not None:
            self.co.tick()
        self._deps(e, reads, writes, waw)
        ins = emit(self.eng[e])
        self.cnt[e] += 1
        ins.then_inc(self.sem[e], 1)
        self._record(e, self.cnt[e], reads, writes, waw)

    def dma(self, q, out, in_, reads=(), writes=(), waw=False, **kw):
        if self.co is not None:
            self.co.tick()
            if q == "pool" and threading.current_thread() is self.co.wt:
                q = "act"
        j = self.dq[q] % self.NDS
        self.dq[q] += 1
        key = "d_%s%d" % (q, j)
        self._wait(q, key, self.cnt[key])
        self._deps(q, reads, writes, waw)
        ins = self.eng[q].dma_start(out=out, in_=in_, **kw)
        self.cnt[key] += 16
        ins.then_inc(self.sem[key], 16)
        self._record(key, self.cnt[key], reads, writes, waw)

    def barrier(self):
        keys = list(self.sem.keys())
        if self.co is not None and threading.current_thread() is self.co.wt:
            keys = [k for k in keys if k != "pool" and not k.startswith("d_pool")]
        for e in self.eng:
            for k in keys:
                self._wait(e, k, self.cnt[k])

    def ps(self):
        b = self.psum[self.psi % 8]
        self.psi += 1
        return b

    def ps6(self):
        b = self.psum[self.psi % 6]
        self.psi += 1
        return b

    def dram(self, name, shape, dt=F32):
        t = self.nc.dram_tensor(name, list(shape), dt, kind="Internal").ap()
        return Buf(t, name)


class Interleave:
    def __init__(self, fn):
        self.left = 0
        self.go_w = threading.Semaphore(0)
        self.go_m = threading.Semaphore(0)
        self.done = False
        self.exc = None
        self.nops = 0
        self.wt = threading.Thread(target=self._run, args=(fn,))
        self.wt.start()

    def _run(self, fn):
        self.go_w.acquire()
        try:
            fn()
        except BaseException as e:
            self.exc = e
        self.done = True
        self.go_m.release()

    def tick(self):
        if threading.current_thread() is self.wt:
            self.nops += 1
            self.left -= 1
            if self.left <= 0:
                self.go_m.release()
                self.go_w.acquire()

    def give(self, n):
        if self.done:
            return
        self.left = n
        self.go_w.release()
        self.go_m.acquire()
        if self.exc is not None:
            raise self.exc

    def finish(self):
        while not self.done:
            self.give(10 ** 12)
        self.wt.join()
        if self.exc is not None:
            raise self.exc


class Phase:
    def __init__(self, kb):
        self.kb = kb
        self.es = ExitStack()

    def __enter__(self):
        self.kb.barrier()
        self.es.__enter__()
        return self

    def __exit__(self, *a):
        self.kb.barrier()
        return self.es.__exit__(*a)

    def sb(self, name, shape, dt=F32):
        self.kb.ndram += 1
        name = "%s_s%d" % (name, self.kb.ndram)
        t = self.es.enter_context(self.kb.nc.sbuf_tensor(name, list(shape), dt))
        return Buf(t, name)


def rsqrt(kb, ob, out, ib, in_, scale, eps):
    kb.op("dve", lambda e: e.tensor_scalar(out=out, in0=in_, scalar1=scale, scalar2=eps, op0=ALU.mult, op1=ALU.add),
          reads=[ib], writes=[ob])
    kb.op("act", lambda e: e.activation(out=out, in_=out, func=AF.Sqrt), reads=[ob], writes=[ob])
    kb.op("dve", lambda e: e.reciprocal(out=out, in_=out), reads=[ob], writes=[ob])


def layer_front(kb, l, L):
    ones, cond_in, adaw_in, adab_in, normw_in, inw_in = (L[k] for k in ("ones", "cond_in", "adaw_in", "adab_in", "normw_in", "inw_in"))
    modA, modS, modG, HTv, UTv, HT, UTd, PT = (L[k] for k in ("modA", "modS", "modG", "HTv", "UTv", "HT", "UTd", "PT"))
    with Phase(kb) as P:
        cT = P.sb("cT", [128, 8, 3])
        adab = P.sb("adab", [128, 24])
        nw = P.sb("nw", [128, 8])
        mod = P.sb("mod", [128, 24, 3])
        aw = [P.sb("aw%d" % i, [128, 8, 512]) for i in range(2)]
        kb.dma("sp", cT[:], cond_in[:, :, :], writes=[cT])
        kb.dma("sp", adab[:], adab_in[l, :, :], writes=[adab])
        kb.dma("sp", nw[:], normw_in[l, :, :], writes=[nw])
        kb.op("act", lambda e: e.activation(out=cT[:], in_=cT[:], func=AF.Silu), reads=[cT], writes=[cT])
        ps = kb.ps()
        for g in range(6):
            a = aw[g % 2]
            kb.dma("sp", a[:], adaw_in[l, :, :, g * 512:(g + 1) * 512], writes=[a])
            for jj in range(4):
                j = g * 4 + jj
                for c in range(8):
                    kb.op("pe", lambda e: e.matmul(ps[:, j * 3:(j + 1) * 3], lhsT=a[:, c, jj * 128:(jj + 1) * 128], rhs=cT[:, c, :],
                                                   start=(c == 0), stop=(c == 7)),
                          reads=[a, cT], writes=[ps])
        kb.op("dve", lambda e: e.tensor_tensor(out=mod[:], in0=ps[:, 0:72].rearrange("p (j k) -> p j k", k=3),
                                               in1=adab[:].unsqueeze(2).to_broadcast([128, 24, 3]), op=ALU.add),
              reads=[ps, adab], writes=[mod])
        kb.op("dve", lambda e: e.tensor_copy(out=modS[:], in_=mod[:, 0:8, :]), reads=[mod], writes=[modS])
        kb.op("dve", lambda e: e.tensor_copy(out=modG[:], in_=mod[:, 16:24, :]), reads=[mod], writes=[modG])
        kb.op("dve", lambda e: e.scalar_tensor_tensor(out=modA[:], in0=mod[:, 8:16, :], scalar=1.0,
                                                      in1=nw[:].unsqueeze(2).to_broadcast([128, 8, 3]), op0=ALU.add, op1=ALU.mult),
              reads=[mod, nw], writes=[modA])
    with Phase(kb) as P:
        hin = [P.sb("hin%d" % i, [128, 8, 512]) for i in range(2)]
        sq = P.sb("sq", [128, 8, 512])
        rstd = P.sb("rstd", [128, 512])
        tmp = [P.sb("tmp%d" % i, [128, 512]) for i in range(2)]
        uo = [P.sb("uo%d" % i, [128, 8, 512], BF16) for i in range(2)]
        it = 0
        for b in range(NB):
            for (t0, tl) in TILES:
                hi = hin[it % 2]
                u = uo[it % 2]
                it += 1
                j = 2 if t0 < TC else b
                g0 = gtok(b, t0)
                kb.dma("sp", hi[:, :, 0:tl], HTv[:, :, g0:g0 + tl], reads=[HT], writes=[hi])
                kb.op("act", lambda e: e.activation(out=sq[:, :, 0:tl], in_=hi[:, :, 0:tl], func=AF.Square), reads=[hi], writes=[sq])
                ps = kb.ps()
                for c in range(8):
                    kb.op("pe", lambda e: e.matmul(ps[:, 0:tl], lhsT=ones[:], rhs=sq[:, c, 0:tl], start=(c == 0), stop=(c == 7)),
                          reads=[ones, sq], writes=[ps])
                rsqrt(kb, rstd, rstd[:, 0:tl], ps, ps[:, 0:tl], 1.0 / D, 1e-6)
                for c in range(8):
                    tm = tmp[c % 2]
                    kb.op("dve", lambda e: e.scalar_tensor_tensor(out=tm[:, 0:tl], in0=hi[:, c, 0:tl], scalar=modA[:, c, j:j + 1], in1=rstd[:, 0:tl],
                                                                  op0=ALU.mult, op1=ALU.mult),
                          reads=[hi, modA, rstd], writes=[tm])
                    kb.op("act", lambda e: e.activation(out=u[:, c, 0:tl], in_=tm[:, 0:tl], func=AF.Identity, bias=modS[:, c, j:j + 1]),
                          reads=[tm, modS], writes=[u], waw=False)
                kb.dma("pool", UTv[:, :, g0:g0 + tl], u[:, :, 0:tl], reads=[u], writes=[UTd])
    with Phase(kb) as P:
        UT = P.sb("UT", [128, 8, NT], BF16)
        wst = [P.sb("wst%d" % i, [128, 8, 128]) for i in range(2)]
        wb = [P.sb("wb%d" % i, [128, 8, 128], BF16) for i in range(2)]
        stage = [P.sb("stage%d" % i, [128, NT]) for i in range(2)]
        for c in range(8):
            kb.dma("sp", UT[:, c, :], UTv[:, c, :], reads=[UTd], writes=[UT])
        ev = 0
        for blk in range(NBLK):
            ws, w, st = wst[blk % 2], wb[blk % 2], stage[blk % 2]
            kb.dma("sp", ws[:], inw_in[l, blk, :, :, :], writes=[ws])
            kb.op("pool", lambda e: e.tensor_copy(out=w[:], in_=ws[:]), reads=[ws], writes=[w])
            for b in range(NB):
                for (t0, tl) in TILES:
                    g0 = gtok(b, t0)
                    ps = kb.ps()
                    for c in range(8):
                        kb.op("pe", lambda e: e.matmul(ps[:, 0:tl], lhsT=w[:, c, :], rhs=UT[:, c, g0:g0 + tl], start=(c == 0), stop=(c == 7)),
                              reads=[w, UT], writes=[ps])
                    if ev % 2:
                        kb.op("act", lambda e: e.copy(out=st[:, g0:g0 + tl], in_=ps[:, 0:tl]), reads=[ps], writes=[st], waw=False)
                    else:
                        kb.op("dve", lambda e: e.tensor_copy(out=st[:, g0:g0 + tl], in_=ps[:, 0:tl]), reads=[ps], writes=[st], waw=False)
                    ev += 1
            kb.dma("pool", PT[blk * 128:(blk + 1) * 128, :], st[:], reads=[st], writes=[PT])


def layer_merge(kb, l, L):
    mergew_in, mergeb_in, branchw_in, outw_in = (L[k] for k in ("mergew_in", "mergeb_in", "branchw_in", "outw_in"))
    modG, HTv, UTv, YSv, HT, UTd, YST = (L[k] for k in ("modG", "HTv", "UTv", "YSv", "HT", "UTd", "YST"))
    MWB, BWB, OWB = L["MWB"], L["BWB"], L["OWB"]
    with Phase(kb) as P:
        ws8 = [P.sb("ws8_%d" % i, [128, 8, 128]) for i in range(3)]
        wb8 = [P.sb("wb8_%d" % i, [128, 8, 128], BF16) for i in range(3)]
        k = 0
        for (src, dst, n_, cc) in ((mergew_in, MWB, 32, 8), (branchw_in, BWB, 32, 4), (outw_in, OWB, 8, 8)):
            for j in range(n_):
                a_, b_ = ws8[k % 3], wb8[k % 3]
                kb.dma("sp", a_[:, 0:cc, :], src[l, j, :, :, :], writes=[a_])
                if k % 2:
                    kb.op("act", lambda e: e.copy(out=b_[:, 0:cc, :], in_=a_[:, 0:cc, :]), reads=[a_], writes=[b_])
                else:
                    kb.op("dve", lambda e: e.tensor_copy(out=b_[:, 0:cc, :], in_=a_[:, 0:cc, :]), reads=[a_], writes=[b_])
                kb.dma("pool", dst[j, :, :, :], b_[:, 0:cc, :], reads=[b_], writes=[dst])
                k += 1
    with Phase(kb) as P:
        mb = P.sb("mb", [128, 32])
        kb.dma("sp", mb[:], mergeb_in[l, :, :], writes=[mb])
        ut = [P.sb("ut%d" % i, [128, 8, 512], BF16) for i in range(2)]
        ys = [P.sb("ys%d" % i, [128, 4, 4, 512], BF16) for i in range(2)]
        hin = [P.sb("hin%d" % i, [128, 8, 512]) for i in range(2)]
        hout = [P.sb("hout%d" % i, [128, 8, 512]) for i in range(2)]
        acc = P.sb("acc", [128, 8, 512])
        accb = P.sb("accb", [128, 8, 512], BF16)
        gt = [P.sb("gt%d" % i, [128, 512]) for i in range(2)]
        tm = [P.sb("tm%d" % i, [128, 512]) for i in range(2)]
        mwb = [P.sb("mwb%d" % i, [128, 8, 128], BF16) for i in range(3)]
        bwb = [P.sb("bwb%d" % i, [128, 4, 128], BF16) for i in range(3)]
        it = 0
        wi = 0
        for b in range(NB):
            for (t0, tl) in TILES:
                u, y, hi, ho = ut[it % 2], ys[it % 2], hin[it % 2], hout[it % 2]
                it += 1
                jc = 2 if t0 < TC else b
                g0 = gtok(b, t0)
                kb.dma("sp", u[:, :, 0:tl], UTv[:, :, g0:g0 + tl], reads=[UTd], writes=[u])
                for m in range(4):
                    kb.dma("sp", y[:, m, :, 0:tl], YSv[:, m, :, g0:g0 + tl], reads=[YST], writes=[y])
                kb.dma("sp", hi[:, :, 0:tl], HTv[:, :, g0:g0 + tl], reads=[HT], writes=[hi])
                for m in range(4):
                    for j in range(8):
                        mj = m * 8 + j
                        wb_, bb = mwb[wi % 3], bwb[wi % 3]
                        g, t_ = gt[wi % 2], tm[wi % 2]
                        wi += 1
                        kb.dma("sp", wb_[:], MWB[mj, :, :, :], reads=[MWB], writes=[wb_])
                        kb.dma("sp", bb[:], BWB[mj, :, :, :], reads=[BWB], writes=[bb])
                        psg = kb.ps()
                        for c in range(8):
                            kb.op("pe", lambda e: e.matmul(psg[:, 0:tl], lhsT=wb_[:, c, :], rhs=u[:, c, 0:tl], start=(c == 0), stop=(c == 7)),
                                  reads=[wb_, u], writes=[psg])
                        psb = kb.ps()
                        for c in range(4):
                            kb.op("pe", lambda e: e.matmul(psb[:, 0:tl], lhsT=bb[:, c, :], rhs=y[:, m, c, 0:tl], start=(c == 0), stop=(c == 3)),
                                  reads=[bb, y], writes=[psb])
                        kb.op("act", lambda e: e.activation(out=g[:, 0:tl], in_=psg[:, 0:tl], func=AF.Sigmoid, bias=mb[:, mj:mj + 1]),
                              reads=[psg, mb], writes=[g])
                        if m == 0:
                            kb.op("dve", lambda e: e.tensor_tensor(out=acc[:, j, 0:tl], in0=psb[:, 0:tl], in1=g[:, 0:tl], op=ALU.mult),
                                  reads=[psb, g], writes=[acc], waw=False)
                        else:
                            kb.op("dve", lambda e: e.tensor_tensor(out=t_[:, 0:tl], in0=psb[:, 0:tl], in1=g[:, 0:tl], op=ALU.mult),
                                  reads=[psb, g], writes=[t_])
                            kb.op("dve", lambda e: e.tensor_tensor(out=acc[:, j, 0:tl], in0=acc[:, j, 0:tl], in1=t_[:, 0:tl], op=ALU.add),
                                  reads=[t_, acc], writes=[acc])
                kb.op("act", lambda e: e.copy(out=accb[:, :, 0:tl], in_=acc[:, :, 0:tl]), reads=[acc], writes=[accb])
                for j2 in range(8):
                    wb_ = mwb[wi % 3]
                    wi += 1
                    kb.dma("sp", wb_[:], OWB[j2, :, :, :], reads=[OWB], writes=[wb_])
                    pso = kb.ps()
                    for c in range(8):
                        kb.op("pe", lambda e: e.matmul(pso[:, 0:tl], lhsT=wb_[:, c, :], rhs=accb[:, c, 0:tl], start=(c == 0), stop=(c == 7)),
                              reads=[wb_, accb], writes=[pso])
                    kb.op("dve", lambda e: e.scalar_tensor_tensor(out=ho[:, j2, 0:tl], in0=pso[:, 0:tl], scalar=modG[:, j2, jc:jc + 1], in1=hi[:, j2, 0:tl],
                                                                  op0=ALU.mult, op1=ALU.add),
                          reads=[pso, modG, hi], writes=[ho], waw=False)
                kb.dma("pool", HTv[:, :, g0:g0 + tl], ho[:, :, 0:tl], reads=[ho], writes=[HT])


def make_A(kb, P, L, b, sT, negA):
    AT, ident = L["AT"], L["ident"]
    onesr = P.sb("onesr", [16, TT])
    Pf = P.sb("Pf", [16, TT])
    Ab = P.sb("Ab", [16, TT])
    cc = P.sb("cc", [16, 2])
    kb.op("dve", lambda e: e.memset(onesr[:], 1.0), writes=[onesr])
    kb.op("dve", lambda e: e.tensor_tensor_scan(out=Pf[:], data0=onesr[:], data1=sT[:], initial=0.0, op0=ALU.mult, op1=ALU.add),
          reads=[onesr, sT], writes=[Pf])
    kb.op("dve", lambda e: e.tensor_copy(out=cc[:, 0:1], in_=Pf[:, TC - 1:TC]), reads=[Pf], writes=[cc])
    kb.op("dve", lambda e: e.tensor_tensor(out=cc[:, 1:2], in0=Pf[:, TC - 1:TC], in1=Pf[:, TT - 1:TT], op=ALU.add), reads=[Pf, cc], writes=[cc])
    kb.op("dve", lambda e: e.scalar_tensor_tensor(out=Ab[:, 0:TC], in0=sT[:, 0:TC], scalar=cc[:, 0:1], in1=Pf[:, 0:TC], op0=ALU.add, op1=ALU.subtract),
          reads=[sT, cc, Pf], writes=[Ab])
    kb.op("dve", lambda e: e.scalar_tensor_tensor(out=Ab[:, TC:TT], in0=sT[:, TC:TT], scalar=cc[:, 1:2], in1=Pf[:, TC:TT], op0=ALU.add, op1=ALU.subtract),
          reads=[sT, cc, Pf], writes=[Ab])
    kb.op("dve", lambda e: e.tensor_copy(out=Ab[0:8, :], in_=Pf[0:8, :]), reads=[Pf, Ab], writes=[Ab])
    kb.dma("sp", AT[b, :, :], Ab[:], reads=[Ab], writes=[AT])
    for i in range(TT // 128):
        ps = kb.ps6()
        kb.op("pe", lambda e: e.transpose(out=ps[:, 0:16], in_=Ab[:, i * 128:(i + 1) * 128], identity=ident[0:16, 0:16]),
              reads=[Ab, ident], writes=[ps])
        kb.op("dve", lambda e: e.tensor_scalar(out=negA[:, i, :], in0=ps[:, 0:16], scalar1=-1.0, scalar2=None, op0=ALU.mult),
              reads=[ps], writes=[negA], waw=False)


def tok_transpose(kb, L, srcT, dst, nchunk, scale=None):
    ident = L["ident"]
    k = 0
    for i in range(TT // 128):
        for c in range(nchunk):
            ps = kb.ps6()
            kb.op("pe", lambda e: e.transpose(out=ps[:, 0:128], in_=srcT[:, c, i * 128:(i + 1) * 128], identity=ident[:]),
                  reads=[srcT, ident], writes=[ps])
            if k % 2:
                kb.op("act", lambda e: e.copy(out=dst[:, i, c * 128:(c + 1) * 128], in_=ps[:, 0:128]), reads=[ps], writes=[dst], waw=False)
            else:
                kb.op("dve", lambda e: e.tensor_copy(out=dst[:, i, c * 128:(c + 1) * 128], in_=ps[:, 0:128]), reads=[ps], writes=[dst], waw=False)
            k += 1


def decay_attn(kb, P, L, b, kq, Vtok, cscal, negA, wk):
    AT, YRAW, maskF, maskB = L["AT"], L["YRAW"], L["maskF"], L["maskB"]
    abc, Et, Tt, Wt, yst = wk
    accb = [kb.psum[6], kb.psum[7]]
    U = []
    gi = 0
    for h in range(8):
        for (t0, tl) in TILES:
            c0, c1 = t0 // 128, (t0 + tl) // 128
            isctx_q = t0 < TC
            units = []
            for i in range(TT // 128):
                isctx_k = i < 2
                for d in range(2):
                    if c0 <= i < c1:
                        units.append((i, d, i - c0))
                    elif d == 0 and i < c0:
                        units.append((i, d, None))
                    elif d == 1 and ((i >= c1 and not isctx_q) or (isctx_k and not isctx_q)):
                        units.append((i, d, None))
            for k, (i, d, off) in enumerate(units):
                U.append(dict(h=h, t0=t0, tl=tl, i=i, d=d, off=off, first=(k == 0), last=(k == len(units) - 1), g=gi,
                              newi=(k == 0 or units[k - 1][0] != i), newh=(k == 0 and t0 == 0)))
            gi += 1
    state = {"st": None, "tn": 0}

    def stA(n, u):
        h, t0, tl, i, d, off = u["h"], u["t0"], u["tl"], u["i"], u["d"], u["off"]
        ab = abc[h % 2]
        if u["newh"]:
            for dd in range(2):
                kb.dma("sp", ab[:, dd, :], AT[b, dd * 8 + h:dd * 8 + h + 1, :].to_broadcast([128, TT]), reads=[AT], writes=[ab])
        lf, rf, bufs = kq(h)
        if u["newi"]:
            st = kb.ps6()
            kb.op("pe", lambda e: e.matmul(st[:, 0:tl], lhsT=lf(i), rhs=rf(t0, t0 + tl), start=True, stop=True), reads=bufs, writes=[st])
            state["st"] = st
        u["st"] = state["st"]
        E = Et[n % 3]
        col = d * 8 + h
        if off is None:
            kb.op("act", lambda e: e.activation(out=E[:, 0:tl], in_=ab[:, d, t0:t0 + tl], func=AF.Exp, bias=negA[:, i, col:col + 1]),
                  reads=[ab, negA], writes=[E])
        else:
            T_ = Tt[state["tn"] % 2]
            state["tn"] += 1
            mk = maskF if d == 0 else maskB
            kb.op("dve", lambda e: e.tensor_scalar(out=T_[:, 0:tl], in0=ab[:, d, t0:t0 + tl], scalar1=negA[:, i, col:col + 1], scalar2=0.0,
                                                   op0=ALU.add, op1=ALU.min), reads=[ab, negA], writes=[T_])
            kb.op("dve", lambda e: e.tensor_tensor(out=T_[:, 0:tl], in0=T_[:, 0:tl], in1=mk[:, off, 0:tl], op=ALU.add), reads=[T_, mk], writes=[T_])
            kb.op("act", lambda e: e.activation(out=E[:, 0:tl], in_=T_[:, 0:tl], func=AF.Exp), reads=[T_], writes=[E])

    def stB(n, u):
        tl = u["tl"]
        E, W, st = Et[n % 3], Wt[n % 3], u["st"]
        cs, csb = cscal(u["i"], u["h"], u["d"])
        kb.op("dve", lambda e: e.scalar_tensor_tensor(out=W[:, 0:tl], in0=st[:, 0:tl], scalar=cs, in1=E[:, 0:tl], op0=ALU.mult, op1=ALU.mult),
              reads=[st, E] + csb, writes=[W])

    def stC(n, u):
        h, t0, tl, i = u["h"], u["t0"], u["tl"], u["i"]
        W = Wt[n % 3]
        acc = accb[u["g"] % 2]
        kb.op("pe", lambda e: e.matmul(acc[0:64, 0:tl], lhsT=Vtok[:, i, h * 64:(h + 1) * 64], rhs=W[:, 0:tl], start=u["first"], stop=u["last"]),
              reads=[Vtok, W], writes=[acc])
        if u["last"]:
            ys_ = yst[u["g"] % 2]
            kb.op("act", lambda e: e.copy(out=ys_[0:64, 0:tl], in_=acc[0:64, 0:tl]), reads=[acc], writes=[ys_])
            g0 = gtok(b, t0)
            kb.dma("sp", YRAW[h * 64:(h + 1) * 64, g0:g0 + tl], ys_[0:64, 0:tl], reads=[ys_], writes=[YRAW])

    N = len(U)
    for t in range(N + 2):
        if t < N:
            stA(t, U[t])
        if 0 <= t - 1 < N:
            stB(t - 1, U[t - 1])
        if 0 <= t - 2 < N:
            stC(t - 2, U[t - 2])


def attn_work(P):
    abc = [P.sb("abc%d" % i, [128, 2, TT]) for i in range(2)]
    Et = [P.sb("E%d" % i, [128, 512]) for i in range(3)]
    Tt = [P.sb("T%d" % i, [128, 512]) for i in range(2)]
    Wt = [P.sb("W%d" % i, [128, 512], BF16) for i in range(3)]
    yst = [P.sb("yst%d" % i, [64, 512]) for i in range(2)]
    return abc, Et, Tt, Wt, yst


def dwconv3(kb, x, acc, w, segs):
    xb, xa = x
    ab, aa = acc
    wb_, wa = w
    kb.op("dve", lambda e: e.tensor_scalar(out=aa, in0=xa, scalar1=wa[:, 1:2], scalar2=None, op0=ALU.mult), reads=[xb, wb_], writes=[ab])
    for (s0, s1) in segs:
        kb.op("dve", lambda e: e.scalar_tensor_tensor(out=aa[:, s0 + 1:s1], in0=xa[:, s0:s1 - 1], scalar=wa[:, 0:1], in1=aa[:, s0 + 1:s1],
                                                      op0=ALU.mult, op1=ALU.add), reads=[xb, wb_, ab], writes=[ab])
        kb.op("dve", lambda e: e.scalar_tensor_tensor(out=aa[:, s0:s1 - 1], in0=xa[:, s0 + 1:s1], scalar=wa[:, 2:3], in1=aa[:, s0:s1 - 1],
                                                      op0=ALU.mult, op1=ALU.add), reads=[xb, wb_, ab], writes=[ab])


SEGS = [(0, TC), (TC, TT)]


def layer_ssd(kb, l, L):
    PT, XS, YRAW, YST, ones = (L[k] for k in ("PT", "XS", "YRAW", "YST", "ones"))
    for b in range(NB):
        with Phase(kb) as P:
            g0 = gtok(b, 0)
            cw = P.sb("cw", [128, 8, 3])
            cb = P.sb("cb", [128, 8])
            kb.dma("sp", cw[:], L["ssdcw_in"][l, :, :, :], writes=[cw])
            kb.dma("sp", cb[:], L["ssdcb_in"][l, :, :], writes=[cb])
            BT = P.sb("BT", [128, 2, TT], BF16)
            CT = P.sb("CT", [128, 2, TT], BF16)
            Vtok = P.sb("Vtok", [128, 18, 512], BF16)
            negA = P.sb("negA", [128, 18, 16])
            dttok = P.sb("dttok", [128, 18, 16])
            with Phase(kb) as Q:
                xin = [Q.sb("xin%d" % i, [128, TT]) for i in range(2)]
                cacc = [Q.sb("cacc%d" % i, [128, TT]) for i in range(2)]
                xs32 = Q.sb("xs32", [128, 4, TT])
                for c in range(8):
                    xi, ca = xin[c % 2], cacc[c % 2]
                    r0 = C_XBC + c * 128
                    kb.dma("sp", xi[:], PT[r0:r0 + 128, g0:g0 + TT], reads=[PT], writes=[xi])
                    dwconv3(kb, (xi, xi[:]), (ca, ca[:]), (cw, cw[:, c, :]), SEGS)
                    if c < 4:
                        kb.op("act", lambda e: e.activation(out=xs32[:, c, :], in_=ca[:], func=AF.Silu, bias=cb[:, c:c + 1]),
                              reads=[ca, cb], writes=[xs32], waw=False)
                        kb.dma("sp", XS[c * 128:(c + 1) * 128, g0:g0 + TT], xs32[:, c, :], reads=[xs32], writes=[XS])
                    else:
                        dst = BT if c < 6 else CT
                        kb.op("act", lambda e: e.activation(out=dst[:, c % 2, :], in_=ca[:], func=AF.Silu, bias=cb[:, c:c + 1]),
                              reads=[ca, cb], writes=[dst], waw=False)
                tok_transpose(kb, L, xs32, Vtok, 4)
            with Phase(kb) as Q:
                dtb = Q.sb("dtb", [16, 1])
                alog = Q.sb("alog", [16, 1])
                dtT = Q.sb("dtT", [16, TT])
                sT = Q.sb("sT", [16, TT])
                kb.dma("sp", dtb[:], L["ssddtb_in"][l, :, :], writes=[dtb])
                kb.dma("sp", alog[:], L["ssdalog_in"][l, :, :], writes=[alog])
                kb.dma("sp", dtT[:], PT[C_DT:C_DT + 16, g0:g0 + TT], reads=[PT], writes=[dtT])
                kb.op("act", lambda e: e.activation(out=dtT[:], in_=dtT[:], func=AF.Exp, bias=dtb[:, 0:1]), reads=[dtT, dtb], writes=[dtT])
                kb.op("act", lambda e: e.activation(out=dtT[:], in_=dtT[:], func=AF.Ln, bias=1.0), reads=[dtT], writes=[dtT])
                kb.op("act", lambda e: e.activation(out=alog[:], in_=alog[:], func=AF.Exp), reads=[alog], writes=[alog])
                kb.op("dve", lambda e: e.tensor_scalar(out=alog[:], in0=alog[:], scalar1=-1.0, scalar2=None, op0=ALU.mult), reads=[alog], writes=[alog])
                kb.op("dve", lambda e: e.tensor_scalar(out=sT[:], in0=dtT[:], scalar1=alog[:, 0:1], scalar2=None, op0=ALU.mult),
                      reads=[dtT, alog], writes=[sT])
                make_A(kb, Q, L, b, sT, negA)
                for i in range(TT // 128):
                    ps = kb.ps6()
                    kb.op("pe", lambda e: e.transpose(out=ps[:, 0:16], in_=dtT[:, i * 128:(i + 1) * 128], identity=L["ident"][0:16, 0:16]),
                          reads=[dtT, L["ident"]], writes=[ps])
                    kb.op("dve", lambda e: e.tensor_copy(out=dttok[:, i, :], in_=ps[:, 0:16]), reads=[ps], writes=[dttok], waw=False)
            with Phase(kb) as Q:
                wk = attn_work(Q)

                def kq(h):
                    g = h // 4
                    return (lambda i: BT[:, g, i * 128:(i + 1) * 128]), (lambda n0, n1: CT[:, g, n0:n1]), [BT, CT]

                def cscal(i, h, d):
                    col = d * 8 + h
                    return dttok[:, i, col:col + 1], [dttok]
                decay_attn(kb, Q, L, b, kq, Vtok, cscal, negA, wk)
    with Phase(kb) as P:
        Dv = P.sb("Dv", [128, 4])
        nw = P.sb("nw", [128, 4])
        kb.dma("sp", Dv[:], L["ssdD_in"][l, :, :], writes=[Dv])
        kb.dma("sp", nw[:], L["ssdnw_in"][l, :, :], writes=[nw])
        yb = [P.sb("yb%d" % i, [128, 4, 512]) for i in range(2)]
        xb = [P.sb("xb%d" % i, [128, 4, 512]) for i in range(2)]
        zb = [P.sb("zb%d" % i, [128, 4, 512]) for i in range(2)]
        sq = P.sb("sq", [128, 4, 512])
        rstd = P.sb("rstd", [128, 512])
        ob = [P.sb("ob%d" % i, [128, 4, 512], BF16) for i in range(2)]
        it = 0
        for b in range(NB):
            for (t0, tl) in TILES:
                y, x, z, o = yb[it % 2], xb[it % 2], zb[it % 2], ob[it % 2]
                it += 1
                g0 = gtok(b, t0)
                kb.dma("sp", y[:, :, 0:tl], YRAW.t.rearrange("(c p) g -> p c g", p=128)[:, :, g0:g0 + tl], reads=[YRAW], writes=[y])
                kb.dma("sp", x[:, :, 0:tl], XS.t.rearrange("(c p) g -> p c g", p=128)[:, :, g0:g0 + tl], reads=[XS], writes=[x])
                kb.dma("sp", z[:, :, 0:tl], PT[C_GZ:C_GZ + 512, :].rearrange("(c p) g -> p c g", p=128)[:, :, g0:g0 + tl], reads=[PT], writes=[z])
                for c in range(4):
                    kb.op("dve", lambda e: e.scalar_tensor_tensor(out=y[:, c, 0:tl], in0=x[:, c, 0:tl], scalar=Dv[:, c:c + 1], in1=y[:, c, 0:tl],
                                                                  op0=ALU.mult, op1=ALU.add), reads=[x, Dv, y], writes=[y])
                kb.op("act", lambda e: e.activation(out=z[:, :, 0:tl], in_=z[:, :, 0:tl], func=AF.Silu), reads=[z], writes=[z])
                kb.op("dve", lambda e: e.tensor_tensor(out=y[:, :, 0:tl], in0=y[:, :, 0:tl], in1=z[:, :, 0:tl], op=ALU.mult), reads=[y, z], writes=[y])
                kb.op("act", lambda e: e.activation(out=sq[:, :, 0:tl], in_=y[:, :, 0:tl], func=AF.Square), reads=[y], writes=[sq])
                ps = kb.ps()
                for c in range(4):
                    kb.op("pe", lambda e: e.matmul(ps[:, 0:tl], lhsT=ones[:], rhs=sq[:, c, 0:tl], start=(c == 0), stop=(c == 3)),
                          reads=[ones, sq], writes=[ps])
                rsqrt(kb, rstd, rstd[:, 0:tl], ps, ps[:, 0:tl], 1.0 / DBR, 1e-6)
                for c in range(4):
                    kb.op("dve", lambda e: e.scalar_tensor_tensor(out=o[:, c, 0:tl], in0=y[:, c, 0:tl], scalar=nw[:, c:c + 1], in1=rstd[:, 0:tl],
                                                                  op0=ALU.mult, op1=ALU.mult), reads=[y, nw, rstd], writes=[o], waw=False)
                kb.dma("pool", L["YSv"][:, 1, :, g0:g0 + tl], o[:, :, 0:tl], reads=[o], writes=[YST])


def layer_ret(kb, l, L):
    PT, YRAW, YST, blk64 = (L[k] for k in ("PT", "YRAW", "YST", "blk64"))
    for b in range(NB):
        with Phase(kb) as P:
            g0 = gtok(b, 0)
            QT = P.sb("QT", [128, 4, TT], BF16)
            KT = P.sb("KT", [128, 4, TT], BF16)
            Vtok = P.sb("Vtok", [128, 18, 512], BF16)
            negA = P.sb("negA", [128, 18, 16])
            with Phase(kb) as Q:
                v32 = Q.sb("v32", [128, 4, TT])
                for c in range(4):
                    r0 = C_QKV + 1024 + c * 128
                    kb.dma("sp", v32[:, c, :], PT[r0:r0 + 128, g0:g0 + TT], reads=[PT], writes=[v32])
                tok_transpose(kb, L, v32, Vtok, 4)
            with Phase(kb) as Q:
                lg = Q.sb("lg", [16, 1])
                sT = Q.sb("sT", [16, TT])
                kb.dma("sp", lg[:], L["retlg_in"][l, :, :], writes=[lg])
                kb.op("dve", lambda e: e.memset(sT[:], 1.0), writes=[sT])
                kb.op("dve", lambda e: e.tensor_scalar(out=sT[:], in0=sT[:], scalar1=lg[:, 0:1], scalar2=None, op0=ALU.mult), reads=[sT, lg], writes=[sT])
                make_A(kb, Q, L, b, sT, negA)
            with Phase(kb) as Q:
                cosT = Q.sb("cosT", [128, TL])
                sinT = Q.sb("sinT", [128, TL])
                kb.dma("sp", cosT[:], L["ropec_in"][:, :], writes=[cosT])
                kb.dma("sp", sinT[:], L["ropes_in"][:, :], writes=[sinT])
                xin = [Q.sb("xin%d" % i, [128, TT]) for i in range(2)]
                xsw = [Q.sb("xsw%d" % i, [128, TT]) for i in range(2)]
                k = 0
                for (dst, r0, rs) in ((QT, C_QKV, C_QSW), (KT, C_QKV + 512, C_KSW)):
                    for c in range(4):
                        xi, xw = xin[k % 2], xsw[k % 2]
                        k += 1
                        kb.dma("sp", xi[:], PT[r0 + c * 128:r0 + (c + 1) * 128, g0:g0 + TT], reads=[PT], writes=[xi])
                        kb.dma("sp", xw[:, TC:TT], PT[rs + c * 128:rs + (c + 1) * 128, g0 + TC:g0 + TT], reads=[PT], writes=[xw])
                        kb.op("act", lambda e: e.copy(out=dst[:, c, 0:TC], in_=xi[:, 0:TC]), reads=[xi], writes=[dst], waw=False)
                        kb.op("dve", lambda e: e.tensor_tensor(out=xi[:, TC:TT], in0=xi[:, TC:TT], in1=cosT[:], op=ALU.mult), reads=[xi, cosT], writes=[xi])
                        kb.op("dve", lambda e: e.tensor_tensor(out=xw[:, TC:TT], in0=xw[:, TC:TT], in1=sinT[:], op=ALU.mult), reads=[xw, sinT], writes=[xw])
                        kb.op("dve", lambda e: e.tensor_tensor(out=dst[:, c, TC:TT], in0=xi[:, TC:TT], in1=xw[:, TC:TT], op=ALU.add),
                              reads=[xi, xw], writes=[dst], waw=False)
            with Phase(kb) as Q:
                wk = attn_work(Q)

                def kq(h):
                    c, p0 = h // 2, (h % 2) * 64
                    return (lambda i: KT[p0:p0 + 64, c, i * 128:(i + 1) * 128]), (lambda n0, n1: QT[p0:p0 + 64, c, n0:n1]), [KT, QT]

                def cscal(i, h, d):
                    return 0.125, []
                decay_attn(kb, Q, L, b, kq, Vtok, cscal, negA, wk)
    with Phase(kb) as P:
        gw = P.sb("gw", [128, 4])
        gb = P.sb("gb", [128, 4])
        kb.dma("sp", gw[:], L["retgw_in"][l, :, :], writes=[gw])
        kb.dma("sp", gb[:], L["retgb_in"][l, :, :], writes=[gb])
        groupnorm_gate_readout(kb, P, L, YRAW, C_GR, gw, gb, 1e-5, 3, None)


def groupnorm_gate_readout(kb, P, L, YR, c_gate, gw, gb, eps, mslot, extra):
    PT, YST, blk64 = L["PT"], L["YST"], L["blk64"]
    yb = [P.sb("yb%d" % i, [128, 4, 512]) for i in range(2)]
    zb = [P.sb("zb%d" % i, [128, 4, 512]) for i in range(2)]
    eb = [P.sb("eb%d" % i, [128, 4, 512]) for i in range(2)] if extra is not None else None
    dd = [P.sb("dd%d" % i, [128, 512]) for i in range(2)]
    sq = [P.sb("sq%d" % i, [128, 512]) for i in range(2)]
    rs = [P.sb("rs%d" % i, [128, 512]) for i in range(2)]
    ob = [P.sb("ob%d" % i, [128, 4, 512], BF16) for i in range(2)]
    it = 0
    k = 0
    for b in range(NB):
        for (t0, tl) in TILES:
            y, z, o = yb[it % 2], zb[it % 2], ob[it % 2]
            ex = eb[it % 2] if extra is not None else None
            it += 1
            g0 = gtok(b, t0)
            kb.dma("sp", y[:, :, 0:tl], YR.t.rearrange("(c p) g -> p c g", p=128)[:, :, g0:g0 + tl], reads=[YR], writes=[y])
            kb.dma("sp", z[:, :, 0:tl], PT[c_gate:c_gate + 512, :].rearrange("(c p) g -> p c g", p=128)[:, :, g0:g0 + tl], reads=[PT], writes=[z])
            if extra is not None:
                kb.dma("sp", ex[:, :, 0:tl], extra.t.rearrange("(c p) g -> p c g", p=128)[:, :, g0:g0 + tl], reads=[extra], writes=[ex])
            kb.op("act", lambda e: e.activation(out=z[:, :, 0:tl], in_=z[:, :, 0:tl], func=AF.Silu), reads=[z], writes=[z])
            for c in range(4):
                d_, s_, r_ = dd[k % 2], sq[k % 2], rs[k % 2]
                k += 1
                pm = kb.ps()
                kb.op("pe", lambda e: e.matmul(pm[:, 0:tl], lhsT=blk64[:], rhs=y[:, c, 0:tl], start=True, stop=True), reads=[blk64, y], writes=[pm])
                kb.op("dve", lambda e: e.tensor_tensor(out=d_[:, 0:tl], in0=y[:, c, 0:tl], in1=pm[:, 0:tl], op=ALU.subtract), reads=[y, pm], writes=[d_])
                kb.op("act", lambda e: e.activation(out=s_[:, 0:tl], in_=d_[:, 0:tl], func=AF.Square), reads=[d_], writes=[s_])
                pv = kb.ps()
                kb.op("pe", lambda e: e.matmul(pv[:, 0:tl], lhsT=blk64[:], rhs=s_[:, 0:tl], start=True, stop=True), reads=[blk64, s_], writes=[pv])
                rsqrt(kb, r_, r_[:, 0:tl], pv, pv[:, 0:tl], 1.0, eps)
                kb.op("dve", lambda e: e.tensor_tensor(out=d_[:, 0:tl], in0=d_[:, 0:tl], in1=r_[:, 0:tl], op=ALU.mult), reads=[d_, r_], writes=[d_])
                kb.op("act", lambda e: e.activation(out=d_[:, 0:tl], in_=d_[:, 0:tl], func=AF.Identity, scale=gw[:, c:c + 1], bias=gb[:, c:c + 1]),
                      reads=[d_, gw, gb], writes=[d_])
                if extra is not None:
                    kb.op("dve", lambda e: e.tensor_tensor(out=d_[:, 0:tl], in0=d_[:, 0:tl], in1=ex[:, c, 0:tl], op=ALU.add), reads=[d_, ex], writes=[d_])
                kb.op("dve", lambda e: e.tensor_tensor(out=o[:, c, 0:tl], in0=d_[:, 0:tl], in1=z[:, c, 0:tl], op=ALU.mult), reads=[d_, z], writes=[o], waw=False)
            kb.dma("pool", L["YSv"][:, mslot, :, g0:g0 + tl], o[:, :, 0:tl], reads=[o], writes=[YST])


TWO_PI = 2.0 * math.pi


def sin_inplace(kb, xb, xa, kib, kia, kfb, kfa, mpi):
    kb.op("dve", lambda e: e.tensor_scalar(out=xa, in0=xa, scalar1=17.0 * math.pi, scalar2=None, op0=ALU.add), reads=[xb], writes=[xb])
    kb.op("dve", lambda e: e.tensor_scalar(out=kfa, in0=xa, scalar1=1.0 / TWO_PI, scalar2=None, op0=ALU.mult), reads=[xb], writes=[kfb])
    kb.op("dve", lambda e: e.tensor_copy(out=kia, in_=kfa), reads=[kfb], writes=[kib])
    kb.op("dve", lambda e: e.tensor_copy(out=kfa, in_=kia), reads=[kib], writes=[kfb])
    kb.op("dve", lambda e: e.scalar_tensor_tensor(out=xa, in0=kfa, scalar=-TWO_PI, in1=xa, op0=ALU.mult, op1=ALU.add), reads=[kfb, xb], writes=[xb])
    kb.op("dve", lambda e: e.tensor_scalar(out=kfa, in0=xa, scalar1=0.0, scalar2=TWO_PI, op0=ALU.is_lt, op1=ALU.mult), reads=[xb], writes=[kfb])
    kb.op("dve", lambda e: e.tensor_tensor(out=xa, in0=xa, in1=kfa, op=ALU.add), reads=[xb, kfb], writes=[xb])
    kb.op("act", lambda e: e.activation(out=xa, in_=xa, func=AF.Sin, bias=mpi[0:xa.shape[0], 0:1]), reads=[xb, mpi], writes=[xb])


def fwd_dft(kb, P, n, tab, load_rhs, consume):
    nch = n // 128
    fcs = [P.sb("fc%d" % i, [128, nch, 128], BF16) for i in range(2)]
    fss = [P.sb("fs%d" % i, [128, nch, 128], BF16) for i in range(2)]
    k = 0
    pend = None
    for cc in range(4):
        rt = load_rhs(cc)
        for fch in range(nch):
            fc, fs = fcs[k % 2], fss[k % 2]
            k += 1
            kb.dma("sp", fc[:], tab["Fc"][fch, :, :, :], writes=[fc])
            kb.dma("sp", fs[:], tab["Fs"][fch, :, :, :], writes=[fs])
            pre, pim = kb.ps(), kb.ps()
            for j in range(nch):
                kb.op("pe", lambda e: e.matmul(pre[:, 0:256], lhsT=fc[:, j, :], rhs=rt[:, j, :], start=(j == 0), stop=(j == nch - 1)),
                      reads=[fc, rt], writes=[pre])
            for j in range(nch):
                kb.op("pe", lambda e: e.matmul(pim[:, 0:256], lhsT=fs[:, j, :], rhs=rt[:, j, :], start=(j == 0), stop=(j == nch - 1)),
                      reads=[fs, rt], writes=[pim])
            if pend is not None:
                consume(*pend)
            pend = (cc, fch, pre, pim)
            if fch == nch - 1:
                consume(*pend)
                pend = None


def layer_hyena(kb, l, L):
    PT, HYC, UTOK, Z1T, YST, ident = (L[k] for k in ("PT", "HYC", "UTOK", "Z1T", "YST", "ident"))
    for b in range(NB):
        with Phase(kb) as P:
            g0 = gtok(b, 0)
            cw = P.sb("cw", [128, 12, 3])
            cb = P.sb("cb", [128, 12])
            kb.dma("sp", cw[:], L["hycw_in"][l, :, :, :], writes=[cw])
            kb.dma("sp", cb[:], L["hycb_in"][l, :, :], writes=[cb])
            xin = [P.sb("xin%d" % i, [128, TT]) for i in range(2)]
            cacc = [P.sb("cacc%d" % i, [128, TT]) for i in range(2)]
            v32 = P.sb("v32", [128, 4, TT])
            vt = P.sb("vt", [128, 18, 512], BF16)
            for c in range(12):
                xi, ca = xin[c % 2], cacc[c % 2]
                r0 = C_HY + c * 128
                kb.dma("sp", xi[:], PT[r0:r0 + 128, g0:g0 + TT], reads=[PT], writes=[xi])
                dwconv3(kb, (xi, xi[:]), (ca, ca[:]), (cw, cw[:, c, :]), SEGS)
                if c < 4:
                    kb.op("act", lambda e: e.activation(out=v32[:, c, :], in_=ca[:], func=AF.Identity, bias=cb[:, c:c + 1]),
                          reads=[ca, cb], writes=[v32], waw=False)
                    kb.dma("pool", HYC[c * 128:(c + 1) * 128, g0:g0 + TT], v32[:, c, :], reads=[v32], writes=[HYC])
                else:
                    kb.op("act", lambda e: e.activation(out=ca[:], in_=ca[:], func=AF.Identity, bias=cb[:, c:c + 1]), reads=[ca, cb], writes=[ca])
                    kb.dma("pool", HYC[c * 128:(c + 1) * 128, g0:g0 + TT], ca[:], reads=[ca], writes=[HYC])
            tok_transpose(kb, L, v32, vt, 4)
            kb.dma("pool", UTOK[0][g0:g0 + TT, :].rearrange("(i p) c -> p i c", p=128), vt[:], reads=[vt], writes=[UTOK[0]])
    for (n, t0s) in ((TL, TC), (TC, 0)):
        nch = n // 128
        tab = L["hytab"][n]
        HTOKn, HFn = L["HTOK"][n], L["HF"][n]
        with Phase(kb) as P:
            feats = P.sb("feats", [33, n])
            w1 = P.sb("w1", [33, 64])
            w2 = P.sb("w2", [64, 64])
            w3 = P.sb("w3", [64, 1024])
            b1 = P.sb("b1", [64, 1])
            b2 = P.sb("b2", [64, 1])
            fr = P.sb("fr", [64, 1])
            mpi = P.sb("mpi", [128, 1])
            hid1 = P.sb("hid1", [64, n])
            hid2 = P.sb("hid2", [64, n])
            ki = P.sb("ki", [64, n], mybir.dt.int32)
            kf = P.sb("kf", [64, n])
            win = P.sb("win", [128, 4, n])
            hw = [P.sb("hw%d" % i, [128, n]) for i in range(2)]
            asum = P.sb("asum", [128, 2])
            ht = [P.sb("ht%d" % i, [128, nch, 128], BF16) for i in range(2)]
            for (dst, src) in ((feats, tab["feats"][:, :]), (w1, L["hyw1_in"][l, :, :]), (w2, L["hyw2_in"][l, :, :]), (w3, L["hyw3_in"][l, :, :]),
                               (b1, L["hyb1_in"][l, :, :]), (b2, L["hyb2_in"][l, :, :]), (fr, L["hyfreq_in"][l, :, :]), (win, tab["window"][:, :, :])):
                kb.dma("sp", dst[:], src, writes=[dst])
            kb.op("dve", lambda e: e.memset(mpi[:], -math.pi), writes=[mpi])
            kb.op("dve", lambda e: e.tensor_tensor(out=b1[:], in0=b1[:], in1=fr[:], op=ALU.mult), reads=[b1, fr], writes=[b1])
            kb.op("dve", lambda e: e.tensor_tensor(out=b2[:], in0=b2[:], in1=fr[:], op=ALU.mult), reads=[b2, fr], writes=[b2])
            tiles = [(t, min(512, n - t)) for t in range(0, n, 512)]
            for (hid, w, K_, src, bb) in ((hid1, w1, 33, feats, b1), (hid2, w2, 64, hid1, b2)):
                for (t, wd) in tiles:
                    ps = kb.ps()
                    kb.op("pe", lambda e: e.matmul(ps[0:64, 0:wd], lhsT=w[0:K_, :], rhs=src[0:K_, t:t + wd], start=True, stop=True),
                          reads=[w, src], writes=[ps])
                    kb.op("dve", lambda e: e.tensor_scalar(out=hid[:, t:t + wd], in0=ps[0:64, 0:wd], scalar1=fr[:, 0:1], scalar2=bb[:, 0:1],
                                                           op0=ALU.mult, op1=ALU.add), reads=[ps, fr, bb], writes=[hid], waw=False)
                sin_inplace(kb, hid, hid[:], ki, ki[:], kf, kf[:], mpi)
            for j8 in range(8):
                h_ = hw[j8 % 2]
                hs = ht[j8 % 2]
                for (t, wd) in tiles:
                    ps = kb.ps()
                    kb.op("pe", lambda e: e.matmul(ps[:, 0:wd], lhsT=w3[:, j8 * 128:(j8 + 1) * 128], rhs=hid2[:, t:t + wd], start=True, stop=True),
                          reads=[w3, hid2], writes=[ps])
                    kb.op("dve", lambda e: e.tensor_tensor(out=h_[:, t:t + wd], in0=ps[:, 0:wd], in1=win[:, j8 % 4, t:t + wd], op=ALU.mult),
                          reads=[ps, win], writes=[h_], waw=False)
                kb.op("dve", lambda e: e.tensor_reduce(out=asum[:, 0:1], in_=h_[:], axis=AX.X, op=ALU.add, apply_absolute_value=True),
                      reads=[h_], writes=[asum])
                kb.op("dve", lambda e: e.reciprocal(out=asum[:, 1:2], in_=asum[:, 0:1]), reads=[asum], writes=[asum])
                kb.op("dve", lambda e: e.tensor_scalar(out=h_[:], in0=h_[:], scalar1=asum[:, 1:2], scalar2=None, op0=ALU.mult), reads=[h_, asum], writes=[h_])
                for i in range(nch):
                    ps = kb.ps()
                    kb.op("pe", lambda e: e.transpose(out=ps[:, 0:128], in_=h_[:, i * 128:(i + 1) * 128], identity=ident[:]), reads=[h_, ident], writes=[ps])
                    if i % 2:
                        kb.op("act", lambda e: e.copy(out=hs[:, i, :], in_=ps[:, 0:128]), reads=[ps], writes=[hs], waw=False)
                    else:
                        kb.op("dve", lambda e: e.tensor_copy(out=hs[:, i, :], in_=ps[:, 0:128]), reads=[ps], writes=[hs], waw=False)
                kb.dma("pool", HTOKn[:, j8 * 128:(j8 + 1) * 128].rearrange("(i p) c -> p i c", p=128), hs[:], reads=[hs], writes=[HTOKn])
        with Phase(kb) as P:
            rts = [P.sb("rt%d" % i, [128, nch, 256], BF16) for i in range(2)]
            ho = [P.sb("ho%d" % i, [128, 2, 256]) for i in range(2)]
            cnt = [0, 0]

            def load_rhs(cc):
                rt = rts[cnt[0] % 2]
                cnt[0] += 1
                for o in range(2):
                    c0 = o * 512 + cc * 128
                    kb.dma("sp", rt[:, :, o * 128:(o + 1) * 128], HTOKn[:, c0:c0 + 128].rearrange("(i p) c -> p i c", p=128), reads=[HTOKn], writes=[rt])
                return rt

            def consume(cc, fch, pre, pim):
                h_ = ho[cnt[1] % 2]
                cnt[1] += 1
                kb.op("act", lambda e: e.copy(out=h_[:, 0, :], in_=pre[:, 0:256]), reads=[pre], writes=[h_], waw=False)
                kb.op("dve", lambda e: e.tensor_copy(out=h_[:, 1, :], in_=pim[:, 0:256]), reads=[pim], writes=[h_], waw=False)
                for ri in range(2):
                    kb.dma("pool", HFn[ri, fch, :, cc, :], h_[:, ri, :], reads=[h_], writes=[HFn])
            fwd_dft(kb, P, n, tab, load_rhs, consume)
        for o in range(2):
            with Phase(kb) as P:
                hb = P.sb("hb", [128, 2, 4])
                kb.dma("sp", hb[:], L["hybias_in"][l, :, :, :], writes=[hb])
                rts = [P.sb("rt%d" % i, [128, nch, 256], BF16) for i in range(2)]
                Y = P.sb("Y", [128, nch, 2, 256], BF16)
                hf = [P.sb("hf%d" % i, [128, 2, 128]) for i in range(2)]
                ta = [P.sb("ta%d" % i, [128, 256]) for i in range(2)]
                tb = [P.sb("tb%d" % i, [128, 256]) for i in range(2)]
                gre = P.sb("gre", [128, nch, 256], BF16)
                gim = P.sb("gim", [128, nch, 256], BF16)
                ut = [P.sb("ut%d" % i, [128, 256]) for i in range(2)]
                xt = [P.sb("xt%d" % i, [128, 256]) for i in range(2)]
                gt = [P.sb("gt%d" % i, [128, 256]) for i in range(2)]
                zt = [P.sb("zt%d" % i, [128, 256]) for i in range(2)]
                zo = [P.sb("zo%d" % i, [128, 256], BF16) for i in range(2)]
                ztok = [P.sb("ztok%d" % i, [128, 2, 128], BF16) for i in range(2)]
                cnt = [0, 0, 0]

                def load_rhs(cc):
                    rt = rts[cnt[0] % 2]
                    cnt[0] += 1
                    for b in range(NB):
                        g0 = gtok(b, t0s)
                        kb.dma("sp", rt[:, :, b * 128:(b + 1) * 128], UTOK[o][g0:g0 + n, cc * 128:(cc + 1) * 128].rearrange("(i p) c -> p i c", p=128),
                               reads=[UTOK[o]], writes=[rt])
                    return rt

                def consume(cc, fch, pre, pim):
                    h_ = hf[cnt[1] % 2]
                    a_, b_ = ta[cnt[1] % 2], tb[cnt[1] % 2]
                    cnt[1] += 1
                    for ri in range(2):
                        kb.dma("sp", h_[:, ri, :], HFn[ri, fch, :, cc, o * 128:(o + 1) * 128], reads=[HFn], writes=[h_])
                    hre = h_[:, 0, :].unsqueeze(1).to_broadcast([128, 2, 128])
                    him = h_[:, 1, :].unsqueeze(1).to_broadcast([128, 2, 128])
                    v3 = lambda ap: ap.rearrange("p (b c) -> p b c", b=2)
                    kb.op("dve", lambda e: e.tensor_tensor(out=v3(a_[:]), in0=v3(pre[:, 0:256]), in1=hre, op=ALU.mult), reads=[pre, h_], writes=[a_])
                    kb.op("dve", lambda e: e.tensor_tensor(out=v3(b_[:]), in0=v3(pim[:, 0:256]), in1=him, op=ALU.mult), reads=[pim, h_], writes=[b_])
                    kb.op("dve", lambda e: e.tensor_tensor(out=Y[:, fch, 0, :], in0=a_[:], in1=b_[:], op=ALU.subtract), reads=[a_, b_], writes=[Y], waw=False)
                    if fch == 0:
                        kb.op("dve", lambda e: e.tensor_copy(out=Y[0:1, 0, 0, :], in_=a_[0:1, :]), reads=[a_, Y], writes=[Y])
                    a2, b2_ = ta[cnt[1] % 2], tb[cnt[1] % 2]
                    cnt[1] += 1
                    kb.op("dve", lambda e: e.tensor_tensor(out=v3(a2[:]), in0=v3(pre[:, 0:256]), in1=him, op=ALU.mult), reads=[pre, h_], writes=[a2])
                    kb.op("dve", lambda e: e.tensor_tensor(out=v3(b2_[:]), in0=v3(pim[:, 0:256]), in1=hre, op=ALU.mult), reads=[pim, h_], writes=[b2_])
                    kb.op("dve", lambda e: e.tensor_tensor(out=Y[:, fch, 1, :], in0=a2[:], in1=b2_[:], op=ALU.add), reads=[a2, b2_], writes=[Y], waw=False)
                    if fch == 0:
                        kb.op("dve", lambda e: e.tensor_copy(out=Y[0:1, 0, 1, :], in_=b_[0:1, :]), reads=[b_, Y], writes=[Y])
                    if fch == nch - 1:
                        inverse(cc)

                def inverse(cc):
                    for tt in range(n // 256):
                        kb.dma("sp", gre[:], tab["Gre"][tt, :, :, :], writes=[gre])
                        kb.dma("sp", gim[:], tab["Gim"][tt, :, :, :], writes=[gim])
                        for b in range(NB):
                            k = cnt[2]
                            cnt[2] += 1
                            g = gtok(b, t0s + tt * 256)
                            u_, x_, g_, z_, zo_, zk = ut[k % 2], xt[k % 2], gt[k % 2], zt[k % 2], zo[k % 2], ztok[k % 2]
                            usrc = HYC if o == 0 else Z1T
                            kb.dma("sp", u_[:], usrc[cc * 128:(cc + 1) * 128, g:g + 256], reads=[usrc], writes=[u_])
                            xr = 512 * (o + 1) + cc * 128
                            kb.dma("sp", x_[:], HYC[xr:xr + 128, g:g + 256], reads=[HYC], writes=[x_])
                            if o == 1:
                                kb.dma("sp", g_[:], PT[C_GH + cc * 128:C_GH + (cc + 1) * 128, g:g + 256], reads=[PT], writes=[g_])
                                kb.op("act", lambda e: e.activation(out=g_[:], in_=g_[:], func=AF.Silu), reads=[g_], writes=[g_])
                            ps = kb.ps()
                            for fch in range(nch):
                                kb.op("pe", lambda e: e.matmul(ps[:, 0:256], lhsT=Y[:, fch, 0, b * 128:(b + 1) * 128], rhs=gre[:, fch, :],
                                                               start=(fch == 0), stop=False), reads=[Y, gre], writes=[ps])
                                kb.op("pe", lambda e: e.matmul(ps[:, 0:256], lhsT=Y[:, fch, 1, b * 128:(b + 1) * 128], rhs=gim[:, fch, :],
                                                               start=False, stop=(fch == nch - 1)), reads=[Y, gim], writes=[ps])
                            kb.op("dve", lambda e: e.scalar_tensor_tensor(out=z_[:], in0=u_[:], scalar=hb[:, o, cc:cc + 1], in1=ps[:, 0:256],
                                                                          op0=ALU.mult, op1=ALU.add), reads=[u_, hb, ps], writes=[z_])
                            if o == 0:
                                kb.op("dve", lambda e: e.tensor_tensor(out=z_[:], in0=z_[:], in1=x_[:], op=ALU.mult), reads=[z_, x_], writes=[z_])
                                kb.dma("pool", Z1T[cc * 128:(cc + 1) * 128, g:g + 256], z_[:], reads=[z_], writes=[Z1T])
                                for hh in range(2):
                                    p2 = kb.ps()
                                    kb.op("pe", lambda e: e.transpose(out=p2[:, 0:128], in_=z_[:, hh * 128:(hh + 1) * 128], identity=ident[:]),
                                          reads=[z_, ident], writes=[p2])
                                    kb.op("act", lambda e: e.copy(out=zk[:, hh, :], in_=p2[:, 0:128]), reads=[p2], writes=[zk], waw=False)
                                kb.dma("pool", UTOK[1][g:g + 256, cc * 128:(cc + 1) * 128].rearrange("(i p) c -> p i c", p=128), zk[:],
                                       reads=[zk], writes=[UTOK[1]])
                            else:
                                kb.op("dve", lambda e: e.tensor_tensor(out=z_[:], in0=z_[:], in1=x_[:], op=ALU.mult), reads=[z_, x_], writes=[z_])
                                kb.op("dve", lambda e: e.tensor_tensor(out=zo_[:], in0=z_[:], in1=g_[:], op=ALU.mult), reads=[z_, g_], writes=[zo_])
                                kb.dma("pool", L["YSv"][:, 2, cc, g:g + 256], zo_[:], reads=[zo_], writes=[YST])
                fwd_dft(kb, P, n, tab, load_rhs, consume)


def tview(ap, d, tok_lo, n):
    return ap[:, tok_lo:tok_lo + n, :]


def fm_transpose(kb, L, src, dst):
    for c in range(4):
        ps = kb.ps()
        kb.op("pe", lambda e: e.transpose(out=ps[:, 0:128], in_=src[:, c * 128:(c + 1) * 128], identity=L["ident"][:]), reads=[src, L["ident"]], writes=[ps])
        if c % 2:
            kb.op("act", lambda e: e.copy(out=dst[:, c, :], in_=ps[:, 0:128]), reads=[ps], writes=[dst], waw=False)
        else:
            kb.op("dve", lambda e: e.tensor_copy(out=dst[:, c, :], in_=ps[:, 0:128]), reads=[ps], writes=[dst], waw=False)


TBLK = 8


def rwkv_prep(kb, l, L):
    PT, RKVTOK, XD, YSC, BONUST, YRAW = (L[k] for k in ("PT", "RKVTOK", "XD", "YSC", "BONUST", "YRAW"))
    for b in range(NB):
        with Phase(kb) as P:
            g0 = gtok(b, 0)
            mu = P.sb("mu", [128, 12])
            cw = P.sb("cw", [128, 12, 3])
            kb.dma("sp", mu[:], L["rwmu_in"][l, :, :], writes=[mu])
            kb.op("dve", lambda e: e.tensor_scalar(out=cw[:, :, 0], in0=mu[:], scalar1=0.5, scalar2=None, op0=ALU.mult), reads=[mu], writes=[cw], waw=False)
            kb.op("dve", lambda e: e.tensor_scalar(out=cw[:, :, 2], in0=mu[:], scalar1=0.5, scalar2=None, op0=ALU.mult), reads=[mu], writes=[cw], waw=False)
            kb.op("dve", lambda e: e.tensor_scalar(out=cw[:, :, 1], in0=mu[:], scalar1=-1.0, scalar2=1.0, op0=ALU.mult, op1=ALU.add), reads=[mu], writes=[cw], waw=False)
            xin = [P.sb("xin%d" % i, [128, TT]) for i in range(2)]
            v32 = P.sb("v32", [128, 4, TT])
            vt = P.sb("vt", [128, 18, 512])
            for grp in range(3):
                for c4 in range(4):
                    c = grp * 4 + c4
                    xi = xin[c % 2]
                    kb.dma("sp", xi[:], PT[C_RKV + c * 128:C_RKV + (c + 1) * 128, g0:g0 + TT], reads=[PT], writes=[xi])
                    dwconv3(kb, (xi, xi[:]), (v32, v32[:, c4, :]), (cw, cw[:, c, :]), SEGS)
                tok_transpose(kb, L, v32, vt, 4)
                kb.dma("pool", RKVTOK[grp][g0:g0 + TT, :].rearrange("(i p) c -> p i c", p=128), vt[:], reads=[vt], writes=[RKVTOK[grp]])
    with Phase(kb) as P:
        def bc(name, src):
            t = P.sb(name, [128, 512])
            kb.dma("sp", t[:], src.to_broadcast([128, 512]), writes=[t])
            return t
        w0 = [bc("w0%d" % d, L["rww0_in"][l, d:d + 1, :]) for d in range(2)]
        a0 = [bc("a0%d" % d, L["rwa0_in"][l, d:d + 1, :]) for d in range(2)]
        k_k = bc("k_k", L["rwkk_in"][l, 0:1, :])
        k_a = bc("k_a", L["rwka_in"][l, 0:1, :])
        r_k = bc("r_k", L["rwrk_in"][l, 0:1, :])
        w2 = P.sb("w2", [128, 512])
        a2 = P.sb("a2", [128, 512])
        kb.dma("sp", w2[:], L["rww2_in"][l, :, :], writes=[w2])
        kb.dma("sp", a2[:], L["rwa2_in"][l, :, :], writes=[a2])
        zw = P.sb("zw", [128, TT])
        za = P.sb("za", [128, TT])
        NTL = 22
        tl_ = [[P.sb("t%d_%d" % (j, i), [128, 512]) for i in range(NTL)] for j in range(2)]
        sm = [P.sb("sm%d" % i, [128, 8]) for i in range(2)]
        bst = [P.sb("bst%d" % i, [128, 4, 128]) for i in range(2)]
        v3 = lambda ap: ap.rearrange("p (h k) -> p h k", k=64)
        it = 0
        for b in range(NB):
            g0 = gtok(b, 0)
            kb.dma("sp", zw[:], PT[C_ZW:C_ZW + 128, g0:g0 + TT], reads=[PT], writes=[zw])
            kb.dma("sp", za[:], PT[C_ZA:C_ZA + 128, g0:g0 + TT], reads=[PT], writes=[za])
            kb.op("act", lambda e: e.activation(out=zw[:], in_=zw[:], func=AF.Tanh), reads=[zw], writes=[zw])
            for i in range(TT // 128):
                T = tl_[it % 2]
                s_ = sm[it % 2]
                bs_ = bst[it % 2]
                it += 1
                g = g0 + i * 128
                r, k, v = T[0], T[1], T[2]
                for (dst, src) in ((r, RKVTOK[0]), (k, RKVTOK[1]), (v, RKVTOK[2])):
                    kb.dma("sp", dst[:], src[g:g + 128, :], reads=[src], writes=[dst])
                kx, sq, kk = T[3], T[4], T[5]
                kb.op("dve", lambda e: e.tensor_tensor(out=kx[:], in0=k[:], in1=k_k[:], op=ALU.mult), reads=[k, k_k], writes=[kx])
                kb.op("act", lambda e: e.activation(out=sq[:], in_=kx[:], func=AF.Square), reads=[kx], writes=[sq])
                kb.op("dve", lambda e: e.tensor_reduce(out=s_[:], in_=v3(sq[:]), axis=AX.X, op=ALU.add), reads=[sq], writes=[s_])
                kb.op("dve", lambda e: e.tensor_scalar(out=s_[:], in0=s_[:], scalar1=1e-12, scalar2=None, op0=ALU.max), reads=[s_], writes=[s_])
                kb.op("act", lambda e: e.activation(out=s_[:], in_=s_[:], func=AF.Sqrt), reads=[s_], writes=[s_])
                kb.op("dve", lambda e: e.reciprocal(out=s_[:], in_=s_[:]), reads=[s_], writes=[s_])
                kb.op("dve", lambda e: e.tensor_tensor(out=v3(kk[:]), in0=v3(kx[:]), in1=s_[:].unsqueeze(2).to_broadcast([128, 8, 64]), op=ALU.mult),
                      reads=[kx, s_], writes=[kk])
                kds = []
                for d in range(2):
                    wr, dec, aa, kd, bd = T[6 + d * 5], T[7 + d * 5], T[8 + d * 5], T[9 + d * 5], T[10 + d * 5]
                    pw = kb.ps()
                    kb.op("pe", lambda e: e.matmul(pw[:, :], lhsT=zw[d * 64:(d + 1) * 64, i * 128:(i + 1) * 128], rhs=w2[d * 64:(d + 1) * 64, :], start=True, stop=True),
                          reads=[zw, w2], writes=[pw])
                    pa = kb.ps()
                    kb.op("pe", lambda e: e.matmul(pa[:, :], lhsT=za[d * 64:(d + 1) * 64, i * 128:(i + 1) * 128], rhs=a2[d * 64:(d + 1) * 64, :], start=True, stop=True),
                          reads=[za, a2], writes=[pa])
                    kb.op("dve", lambda e: e.tensor_tensor(out=wr[:], in0=pw[:, :], in1=w0[d][:], op=ALU.add), reads=[pw, w0[d]], writes=[wr])
                    kb.op("act", lambda e: e.activation(out=wr[:], in_=wr[:], func=AF.Sigmoid), reads=[wr], writes=[wr])
                    kb.op("act", lambda e: e.activation(out=dec[:], in_=wr[:], func=AF.Exp, scale=-math.exp(-0.5)), reads=[wr], writes=[dec])
                    kb.op("dve", lambda e: e.tensor_tensor(out=aa[:], in0=pa[:, :], in1=a0[d][:], op=ALU.add), reads=[pa, a0[d]], writes=[aa])
                    kb.op("act", lambda e: e.activation(out=aa[:], in_=aa[:], func=AF.Sigmoid), reads=[aa], writes=[aa])
                    kb.op("dve", lambda e: e.tensor_tensor(out=bd[:], in0=kk[:], in1=aa[:], op=ALU.mult), reads=[kk, aa], writes=[bd])
                    kb.op("dve", lambda e: e.scalar_tensor_tensor(out=kd[:], in0=aa[:], scalar=-1.0, in1=k_a[:], op0=ALU.add, op1=ALU.mult), reads=[aa, k_a], writes=[kd])
                    kb.op("dve", lambda e: e.scalar_tensor_tensor(out=kd[:], in0=kd[:], scalar=1.0, in1=k[:], op0=ALU.add, op1=ALU.mult), reads=[kd, k], writes=[kd])
                    kds.append(kd)
                    for fi, src in enumerate((dec, kk, bd, kd, r)):
                        kb.dma("pool", tview(XD[d, b * 8:(b + 1) * 8, :, fi * 64:(fi + 1) * 64], d, i * 128, 128).rearrange("h t k -> t h k"),
                               v3(src[:]), reads=[src], writes=[XD])
                    v4 = v[:].rearrange("p (h g i) -> p h g i", h=8, g=8)
                    for vg in range(8):
                        kb.dma("pool", tview(L["XV"][d, vg, b * 8:(b + 1) * 8, :, :], d, i * 128, 128).rearrange("h t i -> t h i"),
                               v4[:, :, vg, :], reads=[v], writes=[L["XV"]])
                rk, ks, bo = T[16], T[17], T[18]
                kb.op("dve", lambda e: e.tensor_tensor(out=rk[:], in0=r[:], in1=r_k[:], op=ALU.mult), reads=[r, r_k], writes=[rk])
                kb.op("dve", lambda e: e.tensor_tensor(out=ks[:], in0=kds[0][:], in1=kds[1][:], op=ALU.add), reads=[kds[0], kds[1]], writes=[ks])
                kb.op("dve", lambda e: e.tensor_tensor(out=ks[:], in0=ks[:], in1=rk[:], op=ALU.mult), reads=[ks, rk], writes=[ks])
                kb.op("dve", lambda e: e.tensor_reduce(out=s_[:], in_=v3(ks[:]), axis=AX.X, op=ALU.add), reads=[ks, s_], writes=[s_])
                kb.op("dve", lambda e: e.tensor_tensor(out=v3(bo[:]), in0=v3(v[:]), in1=s_[:].unsqueeze(2).to_broadcast([128, 8, 64]), op=ALU.mult),
                      reads=[v, s_], writes=[bo])
                fm_transpose(kb, L, bo, bs_)
                kb.dma("pool", BONUST.t.rearrange("(c p) g -> p c g", p=128)[:, :, g:g + 128], bs_[:], reads=[bs_], writes=[BONUST])


def rwkv_scan(kb, l, L, others, budget):
    XD, YSC = L["XD"], L["YSC"]
    with Phase(kb) as P:
        S = P.sb("S", [128, 2, 8, 64])
        T1 = P.sb("T1", [128, 2, 8, 64])
        sa = P.sb("sa", [128, 2, 8])
        X = [P.sb("X%d" % i, [128, 2, TBLK, 320]) for i in range(2)]
        V = [P.sb("V%d" % i, [128, 2, TBLK, 8]) for i in range(2)]
        Yb = [P.sb("Y%d" % i, [128, 2, TBLK, 8]) for i in range(2)]
        T3 = [P.sb("T3_%d" % i, [128, 2, 8, 64]) for i in range(4)]
        kb.op("dve", lambda e: e.memset(S[:], 0.0), writes=[S])
        nblk = TT // TBLK
        SH = [128, 2, 8, 64]
        bk = lambda ap: ap.unsqueeze(2).to_broadcast(SH)
        bv = lambda ap: ap.unsqueeze(3).to_broadcast(SH)

        XV = L["XV"]

        def tok0_b(j):
            s0 = j * TBLK
            if s0 < TC:
                return TC - s0 - TBLK
            return TT - (s0 - TC) - TBLK

        def load(j):
            x, v = X[j % 2], V[j % 2]
            s0 = j * TBLK
            tb = tok0_b(j)
            kb.dma("pool", x[:, 0, :, :].rearrange("p t k -> p (t k)"),
                   XD[0, :, s0:s0 + TBLK, :].rearrange("q t k -> q (t k)").unsqueeze(0).to_broadcast([8, 16, TBLK * 320]), reads=[XD], writes=[x])
            for vg in range(8):
                kb.dma("pool", x[vg * 16:(vg + 1) * 16, 1, :, :], XD[1, :, tb:tb + TBLK, :][:, ::-1, :], reads=[XD], writes=[x])
            kb.dma("pool", v[:, 0, :, :].rearrange("p t i -> p (t i)"),
                   XV[0, :, :, s0:s0 + TBLK, :].rearrange("g q t i -> (g q) (t i)"), reads=[XV], writes=[v])
            kb.dma("pool", v[:, 1, :, :], XV[1, :, :, tb:tb + TBLK, :].rearrange("g q t i -> (g q) t i")[:, ::-1, :], reads=[XV], writes=[v])
        def store(j):
            y = Yb[j % 2]
            kb.dma("pool", YSC[0, :, :, j * TBLK:(j + 1) * TBLK, :].rearrange("g q t i -> (g q) (t i)"),
                   y[:, 0, :, :].rearrange("p t i -> p (t i)"), reads=[y], writes=[YSC])
            tb = tok0_b(j)
            kb.dma("pool", YSC[1, :, :, tb:tb + TBLK, :].rearrange("g q t i -> (g q) t i")[:, ::-1, :], y[:, 1, :, :], reads=[y], writes=[YSC])

        load(0)
        load(1)
        co = Interleave(others) if others is not None else None
        kb.co = co
        deferred = []
        k3 = 0
        for j in range(nblk):
            x, v, y = X[j % 2], V[j % 2], Yb[j % 2]
            for c in range(TBLK):
                t3 = T3[k3 % 4]
                k3 += 1
                kb.op("pool", lambda e: e.tensor_tensor(out=t3[:], in0=bv(v[:, :, c, :]), in1=bk(x[:, :, c, 192:256]), op=ALU.mult), reads=[v, x], writes=[t3])
                if c == 2:
                    for f in deferred:
                        f()
                    deferred = []
                kb.op("dve", lambda e: e.tensor_tensor(out=T1[:], in0=S[:], in1=bk(x[:, :, c, 64:128]), op=ALU.mult), reads=[S, x], writes=[T1])
                kb.op("dve", lambda e: e.tensor_reduce(out=sa[:], in_=T1[:], axis=AX.X, op=ALU.add), reads=[T1], writes=[sa])
                kb.op("dve", lambda e: e.tensor_tensor(out=S[:], in0=S[:], in1=bk(x[:, :, c, 0:64]), op=ALU.mult), reads=[S, x], writes=[S])
                kb.op("dve", lambda e: e.tensor_tensor(out=T1[:], in0=bv(sa[:]), in1=bk(x[:, :, c, 128:192]), op=ALU.mult), reads=[sa, x], writes=[T1])
                kb.op("dve", lambda e: e.tensor_tensor(out=S[:], in0=S[:], in1=T1[:], op=ALU.subtract), reads=[S, T1], writes=[S])
                kb.op("dve", lambda e: e.tensor_tensor(out=S[:], in0=S[:], in1=t3[:], op=ALU.add), reads=[S, t3], writes=[S])
                kb.op("dve", lambda e: e.tensor_tensor(out=T1[:], in0=S[:], in1=bk(x[:, :, c, 256:320]), op=ALU.mult), reads=[S, x], writes=[T1])
                kb.op("dve", lambda e: e.tensor_reduce(out=y[:, :, c, :], in_=T1[:], axis=AX.X, op=ALU.add), reads=[T1], writes=[y], waw=False)
            deferred.append(lambda j=j: store(j))
            if j + 2 < nblk:
                deferred.append(lambda j=j: load(j + 2))
            if co is not None:
                co.give(budget)
        for f in deferred:
            f()
        if co is not None:
            if not co.done:
                print("[build] interleave: worker still had work after the scan (emitted %d ops so far)" % co.nops)
            co.finish()
            print("[build] interleave: worker ops=%d, scan blocks=%d, budget=%d" % (co.nops, nblk, budget))
        kb.co = None


def rwkv_readout(kb, l, L):
    PT, RKVTOK, XD, YSC, BONUST, YRAW = (L[k] for k in ("PT", "RKVTOK", "XD", "YSC", "BONUST", "YRAW"))
    with Phase(kb) as P:
        ya = [P.sb("ya%d" % i, [128, 512]) for i in range(2)]
        yb_ = [P.sb("yb%d" % i, [128, 512]) for i in range(2)]
        st = [P.sb("st%d" % i, [128, 4, 128]) for i in range(2)]
        v3 = lambda ap: ap.rearrange("p (h k) -> p h k", k=64)
        it = 0
        for b in range(NB):
            for i in range(TT // 128):
                a_, b_, s_ = ya[it % 2], yb_[it % 2], st[it % 2]
                it += 1
                g = gtok(b, i * 128)
                for d, dst in ((0, a_), (1, b_)):
                    d4 = dst[:].rearrange("p (h g i) -> p h g i", h=8, g=8)
                    for vg in range(8):
                        kb.dma("sp", d4[:, :, vg, :], tview(YSC[d, vg, b * 8:(b + 1) * 8, :, :], d, i * 128, 128).rearrange("h t i -> t h i"),
                               reads=[YSC], writes=[dst])
                kb.op("dve", lambda e: e.tensor_tensor(out=a_[:], in0=a_[:], in1=b_[:], op=ALU.add), reads=[a_, b_], writes=[a_])
                fm_transpose(kb, L, a_, s_)
                kb.dma("pool", YRAW.t.rearrange("(c p) g -> p c g", p=128)[:, :, g:g + 128], s_[:], reads=[s_], writes=[YRAW])
    with Phase(kb) as P:
        gw = P.sb("gw", [128, 4])
        gb = P.sb("gb", [128, 4])
        kb.dma("sp", gw[:], L["rwgw_in"][l, :, :], writes=[gw])
        kb.dma("sp", gb[:], L["rwgb_in"][l, :, :], writes=[gb])
        groupnorm_gate_readout(kb, P, L, YRAW, C_GA, gw, gb, 64e-5, 0, BONUST)


def gtok(b, t0):
    return b * TT + t0


def build_program(depth=DEPTH, debug=(), ext_in=(), skip=(), budget=112):
    nc = bass.Bass("TRN2", target_bir_lowering=False)
    kb = KB(nc)

    def din(name, shape, dt=F32):
        return Buf(nc.dram_tensor(name, list(shape), dt, kind="ExternalInput").ap(), name)

    x_in = din("x", [NB, TL, D])
    ctx_in = din("ctx", [NB, TC, D])
    fnw_in = din("final_norm_w", [128, 8])
    ident_in = din("ident", [128, 128])
    out_d = Buf(nc.dram_tensor("out", [NB, TL, D], F32, kind="ExternalOutput").ap(), "out")

    cond_in = din("condT", [128, 8, 3])
    adaw_in = din("ada_w", [DEPTH, 128, 8, 3 * D])
    adab_in = din("ada_b", [DEPTH, 128, 24])
    normw_in = din("norm_w", [DEPTH, 128, 8])
    inw_in = din("in_w", [DEPTH, NBLK, 128, 8, 128])

    mergew_in = din("merge_w", [DEPTH, 32, 128, 8, 128])
    mergeb_in = din("merge_b", [DEPTH, 128, 32])
    branchw_in = din("branch_w", [DEPTH, 32, 128, 4, 128])
    outw_in = din("out_w", [DEPTH, 8, 128, 8, 128])

    maskF_in = din("maskF", [128, 4, 512])
    maskB_in = din("maskB", [128, 4, 512])
    blk64_in = din("blk64", [128, 128])
    ropec_in = din("rope_cos", [128, TL])
    ropes_in = din("rope_sin", [128, TL])
    ssdcw_in = din("ssd_conv_w", [DEPTH, 128, 8, 3])
    ssdcb_in = din("ssd_conv_b", [DEPTH, 128, 8])
    ssddtb_in = din("ssd_dt_bias", [DEPTH, 16, 1])
    ssdalog_in = din("ssd_a_log", [DEPTH, 16, 1])
    ssdD_in = din("ssd_d", [DEPTH, 128, 4])
    ssdnw_in = din("ssd_norm_w", [DEPTH, 128, 4])
    retlg_in = din("ret_log_decay", [DEPTH, 16, 1])
    retgw_in = din("ret_gn_w", [DEPTH, 128, 4])
    retgb_in = din("ret_gn_b", [DEPTH, 128, 4])

    hycw_in = din("hy_conv_w", [DEPTH, 128, 12, 3])
    hycb_in = din("hy_conv_b", [DEPTH, 128, 12])
    hyw1_in = din("hy_w1", [DEPTH, 33, 64])
    hyb1_in = din("hy_b1", [DEPTH, 64, 1])
    hyw2_in = din("hy_w2", [DEPTH, 64, 64])
    hyb2_in = din("hy_b2", [DEPTH, 64, 1])
    hyw3_in = din("hy_w3", [DEPTH, 64, 1024])
    hyfreq_in = din("hy_freq", [DEPTH, 64, 1])
    hybias_in = din("hy_bias", [DEPTH, 128, 2, 4])
    hytab = {}
    for n_ in (TL, TC):
        k_ = n_ // 128
        hytab[n_] = dict(feats=din("hy_feats%d" % n_, [33, n_]), window=din("hy_window%d" % n_, [128, 4, n_]),
                         Fc=din("hy_Fc%d" % n_, [k_, 128, k_, 128], BF16), Fs=din("hy_Fs%d" % n_, [k_, 128, k_, 128], BF16),
                         Gre=din("hy_Gre%d" % n_, [n_ // 256, 128, k_, 256], BF16), Gim=din("hy_Gim%d" % n_, [n_ // 256, 128, k_, 256], BF16))

    rwmu_in = din("rwkv_mu", [DEPTH, 128, 12])
    rww0_in = din("rwkv_w0", [DEPTH, 2, 512])
    rww2_in = din("rwkv_w2", [DEPTH, 128, 512])
    rwa0_in = din("rwkv_a0", [DEPTH, 2, 512])
    rwa2_in = din("rwkv_a2", [DEPTH, 128, 512])
    rwkk_in = din("rwkv_k_k", [DEPTH, 1, 512])
    rwka_in = din("rwkv_k_a", [DEPTH, 1, 512])
    rwrk_in = din("rwkv_r_k", [DEPTH, 1, 512])
    rwgw_in = din("rwkv_gn_w", [DEPTH, 128, 4])
    rwgb_in = din("rwkv_gn_b", [DEPTH, 128, 4])

    def scratch(name, shape, dt=F32):
        if name in ext_in:
            return Buf(nc.dram_tensor(name, list(shape), dt, kind="ExternalInput").ap(), name)
        if name in debug:
            return Buf(nc.dram_tensor(name, list(shape), dt, kind="ExternalOutput").ap(), name)
        return kb.dram(name, shape, dt)

    HT = scratch("HT", [D, NT])
    UTd = scratch("UTd", [D, NT], BF16)
    PT = scratch("PT", [NCOLS, NT])
    MWB = scratch("MWB", [32, 128, 8, 128], BF16)
    BWB = scratch("BWB", [32, 128, 4, 128], BF16)
    OWB = scratch("OWB", [8, 128, 8, 128], BF16)
    YST = scratch("YST", [4 * DBR, NT], BF16)
    AT = scratch("AT", [NB, 16, TT])
    YRAW = scratch("YRAW", [DBR, NT])
    XS = scratch("XS", [DBR, NT])
    HYC = scratch("HYC", [3 * DBR, NT])
    UTOK = [scratch("UTOK%d" % i, [NT, DBR], BF16) for i in range(2)]
    Z1T = scratch("Z1T", [DBR, NT])
    HTOK = {n_: scratch("HTOK%d" % n_, [n_, 2 * DBR], BF16) for n_ in (TL, TC)}
    HF = {n_: scratch("HF%d" % n_, [2, n_ // 128, 128, 4, 256]) for n_ in (TL, TC)}
    RKVTOK = [scratch("RKVTOK%d" % i, [NT, DBR]) for i in range(3)]
    XD = scratch("XD", [2, 16, TT, 320])
    XV = scratch("XV", [2, 8, 16, TT, 8])
    YSC = scratch("YSC", [2, 8, 16, TT, 8])
    BONUST = scratch("BONUST", [DBR, NT])
    HTv = HT.t.rearrange("(c p) g -> p c g", p=128)
    YSv = YST.t.rearrange("(m c p) g -> p m c g", p=128, c=4)
    UTv = UTd.t.rearrange("(c p) g -> p c g", p=128)

    with Phase(kb) as G:
        ident = G.sb("ident", [128, 128])
        ones = G.sb("ones", [128, 128])
        fnw = G.sb("fnw", [128, 8])
        kb.dma("sp", ident[:], ident_in[:, :], writes=[ident])
        kb.dma("sp", fnw[:], fnw_in[:, :], writes=[fnw])
        kb.op("dve", lambda e: e.memset(ones[:], 1.0), writes=[ones])
        maskF = G.sb("maskF", [128, 4, 512])
        maskB = G.sb("maskB", [128, 4, 512])
        blk64 = G.sb("blk64", [128, 128])
        kb.dma("sp", maskF[:], maskF_in[:, :, :], writes=[maskF])
        kb.dma("sp", maskB[:], maskB_in[:, :, :], writes=[maskB])
        kb.dma("sp", blk64[:], blk64_in[:, :], writes=[blk64])

        with Phase(kb) as P:
            xin = [P.sb("xin%d" % i, [128, D]) for i in range(2)]
            stg = [P.sb("stg%d" % i, [128, 8, 512]) for i in range(2)]
            it = 0
            for b in range(NB):
                for (t0, tl) in TILES:
                    sg = stg[it % 2]
                    it += 1
                    for s in range(tl // 128):
                        xi = xin[s % 2]
                        tt = t0 + s * 128
                        src = ctx_in[b, tt:tt + 128, :] if tt < TC else x_in[b, tt - TC:tt - TC + 128, :]
                        kb.dma("sp", xi[:], src, writes=[xi])
                        for c in range(8):
                            ps = kb.ps()
                            kb.op("pe", lambda e: e.transpose(out=ps[:, 0:128], in_=xi[:, c * 128:(c + 1) * 128], identity=ident[:]),
                                  reads=[xi, ident], writes=[ps])
                            eng = "act" if c % 2 else "dve"
                            if eng == "act":
                                kb.op("act", lambda e: e.copy(out=sg[:, c, s * 128:(s + 1) * 128], in_=ps[:, 0:128]),
                                      reads=[ps], writes=[sg], waw=False)
                            else:
                                kb.op("dve", lambda e: e.tensor_copy(out=sg[:, c, s * 128:(s + 1) * 128], in_=ps[:, 0:128]),
                                      reads=[ps], writes=[sg], waw=False)
                    g0 = gtok(b, t0)
                    kb.dma("pool", HT.t.rearrange("(c p) g -> p c g", p=128)[:, :, g0:g0 + tl], sg[:, :, 0:tl],
                           reads=[sg], writes=[HT])

        for l in range(depth):
            with Phase(kb) as LP:
                modA = LP.sb("modA", [128, 8, 3])
                modS = LP.sb("modS", [128, 8, 3])
                modG = LP.sb("modG", [128, 8, 3])
                if "front" not in skip:
                    layer_front(kb, l, locals())
                LL = locals()

                def others():
                    if "hy" not in skip:
                        layer_hyena(kb, l, LL)
                    if "ssd" not in skip:
                        layer_ssd(kb, l, LL)
                    if "ret" not in skip:
                        layer_ret(kb, l, LL)
                if "rwkv" not in skip:
                    rwkv_prep(kb, l, LL)
                    if "nointer" in skip:
                        rwkv_scan(kb, l, LL, None, 0)
                        others()
                    else:
                        rwkv_scan(kb, l, LL, others, budget)
                    rwkv_readout(kb, l, LL)
                else:
                    others()
                if "merge" not in skip:
                    layer_merge(kb, l, locals())

        with Phase(kb) as P:
            hin = [P.sb("hin%d" % i, [128, 8, 512]) for i in range(2)]
            sq = P.sb("sq", [128, 8, 512])
            rstd = P.sb("rstd", [128, 512])
            yn = P.sb("yn", [128, 8, 512])
            ot = [P.sb("ot%d" % i, [128, D]) for i in range(2)]
            it = 0
            oi = 0
            for b in range(NB):
                for (t0, tl) in TILES[1:]:
                    hi = hin[it % 2]
                    it += 1
                    g0 = gtok(b, t0)
                    kb.dma("sp", hi[:], HT.t.rearrange("(c p) g -> p c g", p=128)[:, :, g0:g0 + tl], reads=[HT], writes=[hi])
                    kb.op("act", lambda e: e.activation(out=sq[:], in_=hi[:], func=AF.Square), reads=[hi], writes=[sq])
                    ps = kb.ps()
                    for c in range(8):
                        kb.op("pe", lambda e: e.matmul(ps[:], lhsT=ones[:], rhs=sq[:, c, :], start=(c == 0), stop=(c == 7)),
                              reads=[ones, sq], writes=[ps])
                    rsqrt(kb, rstd, rstd[:], ps, ps[:], 1.0 / D, 1e-6)
                    for c in range(8):
                        kb.op("dve", lambda e: e.scalar_tensor_tensor(out=yn[:, c, :], in0=hi[:, c, :], scalar=fnw[:, c:c + 1], in1=rstd[:],
                                                                      op0=ALU.mult, op1=ALU.mult),
                              reads=[hi, fnw, rstd], writes=[yn], waw=False)
                    for s in range(tl // 128):
                        o = ot[oi % 2]
                        oi += 1
                        for c in range(8):
                            ps2 = kb.ps()
                            kb.op("pe", lambda e: e.transpose(out=ps2[:, 0:128], in_=yn[:, c, s * 128:(s + 1) * 128], identity=ident[:]),
                                  reads=[yn, ident], writes=[ps2])
                            if c % 2:
                                kb.op("act", lambda e: e.copy(out=o[:, c * 128:(c + 1) * 128], in_=ps2[:, 0:128]), reads=[ps2], writes=[o], waw=False)
                            else:
                                kb.op("dve", lambda e: e.tensor_copy(out=o[:, c * 128:(c + 1) * 128], in_=ps2[:, 0:128]), reads=[ps2], writes=[o], waw=False)
                        tt = t0 - TC + s * 128
                        kb.dma("pool", out_d[b, tt:tt + 128, :], o[:], reads=[o], writes=[out_d])
    kb.barrier()
    return nc


_CACHE = {}


def _colmap():
    m = np.full(NCOLS, -1, dtype=np.int64)
    def put(dst, src, n):
        m[dst:dst + n] = np.arange(src, src + n)
    put(C_RKV, 0, 1536); put(C_ZW, 1536, 128); put(C_ZA, 1664, 128); put(C_GA, 1792, 512)
    put(C_XBC, 2304, 1024); put(C_GZ, 3344, 512); put(C_HY, 3856, 1536); put(C_GH, 5392, 512)
    put(C_QKV, 5904, 1536); put(C_GR, 7440, 512); put(C_DT, 3328, 16)
    d = np.arange(512)
    r = d % 32
    partner = np.where(r < 16, d + 16, d - 16)
    m[C_QSW:C_QSW + 512] = 5904 + partner
    m[C_KSW:C_KSW + 512] = 5904 + 512 + partner
    return m


_HYT = {}


def _hy_tables(n):
    if n in _HYT:
        return _HYT[n]
    f = np.float32
    nch = n // 128
    Lp = 2 * n
    j = np.arange(n, dtype=np.float32)
    t = (j / np.float32(max(n - 1, 1))).astype(f)
    ang = (np.float32(2.0 * math.pi) * j / np.float32(n)).astype(f)
    bands = np.linspace(1e-4, 15, 16).astype(f)
    feats = np.concatenate([t[:, None], np.cos(ang[:, None] * bands), -np.sin(ang[:, None] * bands)], axis=-1).astype(f)
    lag = (np.abs(j - n // 2) / np.float32(n / 2.0)).astype(f)
    deltas = np.abs(np.linspace(math.log(1e-2) / 1.5, math.log(1e-2) / 0.3, 512)).astype(f)
    window = (np.exp(-lag[:, None] * deltas[None, :]) + np.float32(0.05)).astype(f)
    sidx = np.arange(n, dtype=np.float64)[:, None]
    fidx = np.arange(n, dtype=np.float64)[None, :]
    Fc = np.cos(2 * np.pi * sidx * fidx / Lp)
    Fs = -np.sin(2 * np.pi * sidx * fidx / Lp)
    Fs[:, 0] = np.cos(np.pi * sidx[:, 0])
    tau = (np.arange(n, dtype=np.float64) + n // 2)[None, :]
    fcol = np.arange(n, dtype=np.float64)[:, None]
    wf = np.where(fcol == 0, 1.0, 2.0)
    Gre = wf / Lp * np.cos(2 * np.pi * fcol * tau / Lp)
    Gim = -2.0 / Lp * np.sin(2 * np.pi * fcol * tau / Lp)
    Gim[0, :] = np.cos(np.pi * tau[0]) / Lp

    def lay_f(M):
        return np.ascontiguousarray(M.reshape(nch, 128, nch, 128).transpose(2, 1, 0, 3)).astype(ml_dtypes.bfloat16)

    def lay_g(M):
        return np.ascontiguousarray(M.reshape(nch, 128, n // 256, 256).transpose(2, 1, 0, 3)).astype(ml_dtypes.bfloat16)

    out = dict(feats=np.ascontiguousarray(feats.T), window=np.ascontiguousarray(window.T.reshape(4, 128, n).transpose(1, 0, 2)),
               Fc=lay_f(Fc), Fs=lay_f(Fs), Gre=lay_g(Gre), Gim=lay_g(Gim))
    _HYT[n] = out
    return out


def _pc(a):
    n = a.shape[-1] // 128
    return np.ascontiguousarray(np.swapaxes(a.reshape(a.shape[:-1] + (n, 128)), -1, -2))


def shared_inputs(inp):
    f = np.float32
    sh = {}
    sh["final_norm_w"] = _pc(inp["final_norm_w"].astype(f))
    sh["ident"] = np.eye(128, dtype=f)
    L = DEPTH
    sh["ada_w"] = np.ascontiguousarray(inp["ada_w"].reshape(L, 8, 128, 3 * D).transpose(0, 2, 1, 3))
    sh["ada_b"] = _pc(inp["ada_b"])
    sh["norm_w"] = _pc(inp["norm_w"])
    sh["merge_w"] = np.ascontiguousarray(inp["merge_w"].reshape(L, 8, 128, 32, 128).transpose(0, 3, 2, 1, 4))
    sh["merge_b"] = _pc(inp["merge_b"])
    sh["branch_w"] = np.ascontiguousarray(inp["branch_w"].reshape(L, 4, 4, 128, 8, 128).transpose(0, 1, 4, 3, 2, 5).reshape(L, 32, 128, 4, 128))
    sh["out_w"] = np.ascontiguousarray(inp["out_w"].reshape(L, 8, 128, 8, 128).transpose(0, 3, 2, 1, 4))
    m_ = np.arange(128)[:, None, None]
    o_ = np.arange(4)[None, :, None]
    n_ = np.arange(512)[None, None, :]
    sh["maskF"] = np.where(n_ >= o_ * 128 + m_, 0.0, -30000.0).astype(f)
    sh["maskB"] = np.where(o_ * 128 + m_ >= n_, 0.0, -30000.0).astype(f)
    blk = np.zeros((128, 128), f)
    blk[:64, :64] = 1.0 / 64
    blk[64:, 64:] = 1.0 / 64
    sh["blk64"] = blk
    t = np.arange(TL)
    rows = (t // 64).astype(np.float64)
    cols = (t % 64).astype(np.float64)
    inv = (10000.0 ** (-np.arange(0, 32, 2, dtype=np.float32) / np.float32(32))).astype(np.float32)
    dch = np.arange(64)
    half = dch // 32
    r = dch % 32
    ang = np.where(half[:, None] == 0, rows[None, :], cols[None, :]).astype(np.float32) * inv[r % 16][:, None]
    cosT = np.cos(ang).astype(f)
    sinT = (np.sin(ang) * np.where(r < 16, -1.0, 1.0)[:, None]).astype(f)
    sh["rope_cos"] = np.ascontiguousarray(np.concatenate([cosT, cosT], 0))
    sh["rope_sin"] = np.ascontiguousarray(np.concatenate([sinT, sinT], 0))
    sh["ssd_conv_w"] = np.ascontiguousarray(inp["ssd_conv_w"].reshape(L, 8, 128, 3).transpose(0, 2, 1, 3))
    sh["ssd_conv_b"] = _pc(inp["ssd_conv_b"])
    sh["ssd_dt_bias"] = np.ascontiguousarray(inp["ssd_dt_bias"].reshape(L, 16, 1))
    sh["ssd_a_log"] = np.ascontiguousarray(inp["ssd_a_log"].reshape(L, 16, 1))
    sh["ssd_d"] = _pc(np.repeat(inp["ssd_d"], 64, axis=-1))
    sh["ssd_norm_w"] = _pc(inp["ssd_norm_w"])
    sh["ret_log_decay"] = np.ascontiguousarray(inp["ret_log_decay"].reshape(L, 16, 1))
    sh["ret_gn_w"] = _pc(inp["ret_gn_w"])
    sh["ret_gn_b"] = _pc(inp["ret_gn_b"])
    sh["hy_conv_w"] = np.ascontiguousarray(inp["hy_conv_w"].reshape(L, 12, 128, 3).transpose(0, 2, 1, 3))
    sh["hy_conv_b"] = _pc(inp["hy_conv_b"])
    sh["hy_w1"] = np.ascontiguousarray(inp["hy_w1"])
    sh["hy_b1"] = np.ascontiguousarray(inp["hy_b1"].reshape(L, 64, 1))
    sh["hy_w2"] = np.ascontiguousarray(inp["hy_w2"])
    sh["hy_b2"] = np.ascontiguousarray(inp["hy_b2"].reshape(L, 64, 1))
    sh["hy_w3"] = np.ascontiguousarray(inp["hy_w3"])
    sh["hy_freq"] = np.ascontiguousarray(inp["hy_freq"].reshape(L, 64, 1))
    sh["hy_bias"] = np.ascontiguousarray(inp["hy_bias"].reshape(L, 2, 4, 128).transpose(0, 3, 1, 2))
    for n_ in (TL, TC):
        for k_, v_ in _hy_tables(n_).items():
            sh["hy_%s%d" % (k_, n_)] = v_
    sh["rwkv_mu"] = _pc(inp["rwkv_mu"].reshape(L, 1536))
    sh["rwkv_w0"] = np.ascontiguousarray(inp["rwkv_w0"])
    sh["rwkv_w2"] = np.ascontiguousarray(inp["rwkv_w2"].reshape(L, 128, 512))
    sh["rwkv_a0"] = np.ascontiguousarray(inp["rwkv_a0"])
    sh["rwkv_a2"] = np.ascontiguousarray(inp["rwkv_a2"].reshape(L, 128, 512))
    sh["rwkv_k_k"] = np.ascontiguousarray(inp["rwkv_k_k"].reshape(L, 1, 512))
    sh["rwkv_k_a"] = np.ascontiguousarray(inp["rwkv_k_a"].reshape(L, 1, 512))
    sh["rwkv_r_k"] = np.ascontiguousarray(inp["rwkv_r_k"].reshape(L, 1, 512))
    sh["rwkv_gn_w"] = _pc(inp["rwkv_gn_w"])
    sh["rwkv_gn_b"] = _pc(inp["rwkv_gn_b"])
    cm = _colmap()
    w = inp["in_w"]
    wa = np.zeros((L, D, NCOLS), dtype=f)
    ok = cm >= 0
    wa[:, :, ok] = w[:, :, cm[ok]]
    sh["in_w"] = np.ascontiguousarray(wa.reshape(L, 8, 128, NBLK, 128).transpose(0, 3, 2, 1, 4))
    return sh


def core_inputs(inp, sh, core):
    bs = slice(core * NB, (core + 1) * NB)
    m = dict(sh)
    m["x"] = np.ascontiguousarray(inp["x"][bs])
    m["ctx"] = np.ascontiguousarray(inp["ctx"][bs])
    cond = np.concatenate([inp["c"][bs], inp["c_ctx"][None, :]], axis=0)
    m["condT"] = np.ascontiguousarray(cond.reshape(3, 8, 128).transpose(2, 1, 0))
    return m


def kernel(**inputs):
    inp = {k: np.asarray(v) for k, v in inputs.items()}
    if "nc" not in _CACHE:
        _CACHE["nc"] = build_program()
    nc = _CACHE["nc"]
    sh = shared_inputs(inp)
    in_maps = [core_inputs(inp, sh, core) for core in range(NCORES)]
    res = run_bass_kernel_spmd(nc, in_maps, core_ids=list(range(NCORES)))
    out = np.concatenate([r["out"] for r in res.results], axis=0)
    return out.astype(np.float32)
```

```python
import math
import threading
from contextlib import ExitStack
import numpy as np
import ml_dtypes
import concourse.bass as bass
import concourse.mybir as mybir
from concourse.bass_utils import run_bass_kernel_spmd

F32 = mybir.dt.float32
BF16 = mybir.dt.bfloat16
ALU = mybir.AluOpType
AF = mybir.ActivationFunctionType
AX = mybir.AxisListType

NCORES = 8
D = 1024
NB = 2
TC = 256
TL = 2048
TT = TC + TL
NT = NB * TT
DEPTH = 4
DBR = 512
NCOLS = 9088
NBLK = NCOLS // 128
C_RKV, C_ZW, C_ZA, C_GA = 0, 1536, 1664, 1792
C_XBC, C_GZ = 2304, 3328
C_HY, C_GH = 3840, 5376
C_QKV, C_GR = 5888, 7424
C_QSW, C_KSW, C_DT = 7936, 8448, 8960
TILES = [(0, 256), (256, 512), (768, 512), (1280, 512), (1792, 512)]


class Buf:
    def __init__(self, t, name=""):
        self.t = t
        self.name = name
        self.w = {}
        self.r = {}

    def __getitem__(self, k):
        return self.t[k]


class KB:
    NDS = 8

    def __init__(self, nc):
        self.nc = nc
        self.eng = {"pe": nc.tensor, "act": nc.scalar, "dve": nc.vector, "pool": nc.gpsimd, "sp": nc.sync}
        self.sem = {}
        self.cnt = {}
        self.seen = {e: {} for e in self.eng}
        for e in self.eng:
            self.sem[e] = nc.alloc_semaphore("s_" + e)
            self.cnt[e] = 0
        self.dq = {}
        for q in ("sp", "act", "pool"):
            for j in range(self.NDS):
                k = "d_%s%d" % (q, j)
                self.sem[k] = nc.alloc_semaphore(k)
                self.cnt[k] = 0
            self.dq[q] = 0
        self.psum = [Buf(nc.alloc_psum_tensor("ps%d" % i, [128, 512], F32), "ps%d" % i) for i in range(8)]
        self.psi = 0
        self.ndram = 0
        self.co = None

    def _wait(self, e, key, val):
        if val <= 0:
            return
        if self.seen[e].get(key, 0) < val:
            self.eng[e].wait_ge(self.sem[key], val)
            self.seen[e][key] = val

    def _deps(self, e, reads, writes, waw):
        for b in reads:
            for k, v in b.w.items():
                self._wait(e, k, v)
        for b in writes:
            for k, v in b.r.items():
                self._wait(e, k, v)
            if waw:
                for k, v in b.w.items():
                    if not (e == "pe" and k == "pe"):
                        self._wait(e, k, v)

    def _record(self, key, val, reads, writes, waw):
        for b in reads:
            if b.r.get(key, 0) < val:
                b.r[key] = val
        for b in writes:
            if waw:
                b.w = {key: val}
                b.r = {}
            else:
                b.w[key] = val

    def op(self, e, emit, reads=(), writes=(), waw=True):
        if self.co is not None:
            self.co.tick()
        self._deps(e, reads, writes, waw)
        ins = emit(self.eng[e])
        self.cnt[e] += 1
        ins.then_inc(self.sem[e], 1)
        self._record(e, self.cnt[e], reads, writes, waw)

    def dma(self, q, out, in_, reads=(), writes=(), waw=False, **kw):
        if self.co is not None:
            self.co.tick()
            if q == "pool" and threading.current_thread() is self.co.wt:
                q = "act"
        j = self.dq[q] % self.NDS
        self.dq[q] += 1
        key = "d_%s%d" % (q, j)
        self._wait(q, key, self.cnt[key])
        self._deps(q, reads, writes, waw)
        ins = self.eng[q].dma_start(out=out, in_=in_, **kw)
        self.cnt[key] += 16
        ins.then_inc(self.sem[key], 16)
        self._record(key, self.cnt[key], reads, writes, waw)

    def barrier(self):
        keys = list(self.sem.keys())
        if self.co is not None and threading.current_thread() is self.co.wt:
            keys = [k for k in keys if k != "pool" and not k.startswith("d_pool")]
        for e in self.eng:
            for k in keys:
                self._wait(e, k, self.cnt[k])

    def ps(self):
        b = self.psum[self.psi % 8]
        self.psi += 1
        return b

    def ps6(self):
        b = self.psum[self.psi % 6]
        self.psi += 1
        return b

    def dram(self, name, shape, dt=F32):
        t = self.nc.dram_tensor(name, list(shape), dt, kind="Internal").ap()
        return Buf(t, name)


class Interleave:
    def __init__(self, fn):
        self.left = 0
        self.go_w = threading.Semaphore(0)
        self.go_m = threading.Semaphore(0)
        self.done = False
        self.exc = None
        self.nops = 0
        self.wt = threading.Thread(target=self._run, args=(fn,))
        self.wt.start()

    def _run(self, fn):
        self.go_w.acquire()
        try:
            fn()
        except BaseException as e:
            self.exc = e
        self.done = True
        self.go_m.release()

    def tick(self):
        if threading.current_thread() is self.wt:
            self.nops += 1
            self.left -= 1
            if self.left <= 0:
                self.go_m.release()
                self.go_w.acquire()

    def give(self, n):
        if self.done:
            return
        self.left = n
        self.go_w.release()
        self.go_m.acquire()
        if self.exc is not None:
            raise self.exc

    def finish(self):
        while not self.done:
            self.give(10 ** 12)
        self.wt.join()
        if self.exc is not None:
            raise self.exc


class Phase:
    def __init__(self, kb):
        self.kb = kb
        self.es = ExitStack()

    def __enter__(self):
        self.kb.barrier()
        self.es.__enter__()
        return self

    def __exit__(self, *a):
        self.kb.barrier()
        return self.es.__exit__(*a)

    def sb(self, name, shape, dt=F32):
        self.kb.ndram += 1
        name = "%s_s%d" % (name, self.kb.ndram)
        t = self.es.enter_context(self.kb.nc.sbuf_tensor(name, list(shape), dt))
        return Buf(t, name)


def rsqrt(kb, ob, out, ib, in_, scale, eps):
    kb.op("dve", lambda e: e.tensor_scalar(out=out, in0=in_, scalar1=scale, scalar2=eps, op0=ALU.mult, op1=ALU.add),
          reads=[ib], writes=[ob])
    kb.op("act", lambda e: e.activation(out=out, in_=out, func=AF.Sqrt), reads=[ob], writes=[ob])
    kb.op("dve", lambda e: e.reciprocal(out=out, in_=out), reads=[ob], writes=[ob])


def layer_front(kb, l, L):
    ones, cond_in, adaw_in, adab_in, normw_in, inw_in = (L[k] for k in ("ones", "cond_in", "adaw_in", "adab_in", "normw_in", "inw_in"))
    modA, modS, modG, HTv, UTv, HT, UTd, PT = (L[k] for k in ("modA", "modS", "modG", "HTv", "UTv", "HT", "UTd", "PT"))
    with Phase(kb) as P:
        cT = P.sb("cT", [128, 8, 3])
        adab = P.sb("adab", [128, 24])
        nw = P.sb("nw", [128, 8])
        mod = P.sb("mod", [128, 24, 3])
        aw = [P.sb("aw%d" % i, [128, 8, 512]) for i in range(2)]
        kb.dma("sp", cT[:], cond_in[:, :, :], writes=[cT])
        kb.dma("sp", adab[:], adab_in[l, :, :], writes=[adab])
        kb.dma("sp", nw[:], normw_in[l, :, :], writes=[nw])
        kb.op("act", lambda e: e.activation(out=cT[:], in_=cT[:], func=AF.Silu), reads=[cT], writes=[cT])
        ps = kb.ps()
        for g in range(6):
            a = aw[g % 2]
            kb.dma("sp", a[:], adaw_in[l, :, :, g * 512:(g + 1) * 512], writes=[a])
            for jj in range(4):
                j = g * 4 + jj
                for c in range(8):
                    kb.op("pe", lambda e: e.matmul(ps[:, j * 3:(j + 1) * 3], lhsT=a[:, c, jj * 128:(jj + 1) * 128], rhs=cT[:, c, :],
                                                   start=(c == 0), stop=(c == 7)),
                          reads=[a, cT], writes=[ps])
        kb.op("dve", lambda e: e.tensor_tensor(out=mod[:], in0=ps[:, 0:72].rearrange("p (j k) -> p j k", k=3),
                                               in1=adab[:].unsqueeze(2).to_broadcast([128, 24, 3]), op=ALU.add),
              reads=[ps, adab], writes=[mod])
        kb.op("dve", lambda e: e.tensor_copy(out=modS[:], in_=mod[:, 0:8, :]), reads=[mod], writes=[modS])
        kb.op("dve", lambda e: e.tensor_copy(out=modG[:], in_=mod[:, 16:24, :]), reads=[mod], writes=[modG])
        kb.op("dve", lambda e: e.scalar_tensor_tensor(out=modA[:], in0=mod[:, 8:16, :], scalar=1.0,
                                                      in1=nw[:].unsqueeze(2).to_broadcast([128, 8, 3]), op0=ALU.add, op1=ALU.mult),
              reads=[mod, nw], writes=[modA])
    with Phase(kb) as P:
        hin = [P.sb("hin%d" % i, [128, 8, 512]) for i in range(2)]
        sq = P.sb("sq", [128, 8, 512])
        rstd = P.sb("rstd", [128, 512])
        tmp = [P.sb("tmp%d" % i, [128, 512]) for i in range(2)]
        uo = [P.sb("uo%d" % i, [128, 8, 512], BF16) for i in range(2)]
        it = 0
        for b in range(NB):
            for (t0, tl) in TILES:
                hi = hin[it % 2]
                u = uo[it % 2]
                it += 1
                j = 2 if t0 < TC else b
                g0 = gtok(b, t0)
                kb.dma("sp", hi[:, :, 0:tl], HTv[:, :, g0:g0 + tl], reads=[HT], writes=[hi])
                kb.op("act", lambda e: e.activation(out=sq[:, :, 0:tl], in_=hi[:, :, 0:tl], func=AF.Square), reads=[hi], writes=[sq])
                ps = kb.ps()
                for c in range(8):
                    kb.op("pe", lambda e: e.matmul(ps[:, 0:tl], lhsT=ones[:], rhs=sq[:, c, 0:tl], start=(c == 0), stop=(c == 7)),
                          reads=[ones, sq], writes=[ps])
                rsqrt(kb, rstd, rstd[:, 0:tl], ps, ps[:, 0:tl], 1.0 / D, 1e-6)
                for c in range(8):
                    tm = tmp[c % 2]
                    kb.op("dve", lambda e: e.scalar_tensor_tensor(out=tm[:, 0:tl], in0=hi[:, c, 0:tl], scalar=modA[:, c, j:j + 1], in1=rstd[:, 0:tl],
                                                                  op0=ALU.mult, op1=ALU.mult),
                          reads=[hi, modA, rstd], writes=[tm])
                    kb.op("act", lambda e: e.activation(out=u[:, c, 0:tl], in_=tm[:, 0:tl], func=AF.Identity, bias=modS[:, c, j:j + 1]),
                          reads=[tm, modS], writes=[u], waw=False)
                kb.dma("pool", UTv[:, :, g0:g0 + tl], u[:, :, 0:tl], reads=[u], writes=[UTd])
    with Phase(kb) as P:
        UT = P.sb("UT", [128, 8, NT], BF16)
        wst = [P.sb("wst%d" % i, [128, 8, 128]) for i in range(2)]
        wb = [P.sb("wb%d" % i, [128, 8, 128], BF16) for i in range(2)]
        stage = [P.sb("stage%d" % i, [128, NT]) for i in range(2)]
        for c in range(8):
            kb.dma("sp", UT[:, c, :], UTv[:, c, :], reads=[UTd], writes=[UT])
        ev = 0
        for blk in range(NBLK):
            ws, w, st = wst[blk % 2], wb[blk % 2], stage[blk % 2]
            kb.dma("sp", ws[:], inw_in[l, blk, :, :, :], writes=[ws])
            kb.op("pool", lambda e: e.tensor_copy(out=w[:], in_=ws[:]), reads=[ws], writes=[w])
            for b in range(NB):
                for (t0, tl) in TILES:
                    g0 = gtok(b, t0)
                    ps = kb.ps()
                    for c in range(8):
                        kb.op("pe", lambda e: e.matmul(ps[:, 0:tl], lhsT=w[:, c, :], rhs=UT[:, c, g0:g0 + tl], start=(c == 0), stop=(c == 7)),
                              reads=[w, UT], writes=[ps])
                    if ev % 2:
                        kb.op("act", lambda e: e.copy(out=st[:, g0:g0 + tl], in_=ps[:, 0:tl]), reads=[ps], writes=[st], waw=False)
                    else:
                        kb.op("dve", lambda e: e.tensor_copy(out=st[:, g0:g0 + tl], in_=ps[:, 0:tl]), reads=[ps], writes=[st], waw=False)
                    ev += 1
            kb.dma("pool", PT[blk * 128:(blk + 1) * 128, :], st[:], reads=[st], writes=[PT])


def layer_merge(kb, l, L):
    mergew_in, mergeb_in, branchw_in, outw_in = (L[k] for k in ("mergew_in", "mergeb_in", "branchw_in", "outw_in"))
    modG, HTv, UTv, YSv, HT, UTd, YST = (L[k] for k in ("modG", "HTv", "UTv", "YSv", "HT", "UTd", "YST"))
    MWB, BWB, OWB = L["MWB"], L["BWB"], L["OWB"]
    with Phase(kb) as P:
        ws8 = [P.sb("ws8_%d" % i, [128, 8, 128]) for i in range(3)]
        wb8 = [P.sb("wb8_%d" % i, [128, 8, 128], BF16) for i in range(3)]
        k = 0
        for (src, dst, n_, cc) in ((mergew_in, MWB, 32, 8), (branchw_in, BWB, 32, 4), (outw_in, OWB, 8, 8)):
            for j in range(n_):
                a_, b_ = ws8[k % 3], wb8[k % 3]
                kb.dma("sp", a_[:, 0:cc, :], src[l, j, :, :, :], writes=[a_])
                if k % 2:
                    kb.op("act", lambda e: e.copy(out=b_[:, 0:cc, :], in_=a_[:, 0:cc, :]), reads=[a_], writes=[b_])
                else:
                    kb.op("dve", lambda e: e.tensor_copy(out=b_[:, 0:cc, :], in_=a_[:, 0:cc, :]), reads=[a_], writes=[b_])
                kb.dma("pool", dst[j, :, :, :], b_[:, 0:cc, :], reads=[b_], writes=[dst])
                k += 1
    with Phase(kb) as P:
        mb = P.sb("mb", [128, 32])
        kb.dma("sp", mb[:], mergeb_in[l, :, :], writes=[mb])
        ut = [P.sb("ut%d" % i, [128, 8, 512], BF16) for i in range(2)]
        ys = [P.sb("ys%d" % i, [128, 4, 4, 512], BF16) for i in range(2)]
        hin = [P.sb("hin%d" % i, [128, 8, 512]) for i in range(2)]
        hout = [P.sb("hout%d" % i, [128, 8, 512]) for i in range(2)]
        acc = P.sb("acc", [128, 8, 512])
        accb = P.sb("accb", [128, 8, 512], BF16)
        gt = [P.sb("gt%d" % i, [128, 512]) for i in range(2)]
        tm = [P.sb("tm%d" % i, [128, 512]) for i in range(2)]
        mwb = [P.sb("mwb%d" % i, [128, 8, 128], BF16) for i in range(3)]
        bwb = [P.sb("bwb%d" % i, [128, 4, 128], BF16) for i in range(3)]
        it = 0
        wi = 0
        for b in range(NB):
            for (t0, tl) in TILES:
                u, y, hi, ho = ut[it % 2], ys[it % 2], hin[it % 2], hout[it % 2]
                it += 1
                jc = 2 if t0 < TC else b
                g0 = gtok(b, t0)
                kb.dma("sp", u[:, :, 0:tl], UTv[:, :, g0:g0 + tl], reads=[UTd], writes=[u])
                for m in range(4):
                    kb.dma("sp", y[:, m, :, 0:tl], YSv[:, m, :, g0:g0 + tl], reads=[YST], writes=[y])
                kb.dma("sp", hi[:, :, 0:tl], HTv[:, :, g0:g0 + tl], reads=[HT], writes=[hi])
                for m in range(4):
                    for j in range(8):
                        mj = m * 8 + j
                        wb_, bb = mwb[wi % 3], bwb[wi % 3]
                        g, t_ = gt[wi % 2], tm[wi % 2]
                        wi += 1
                        kb.dma("sp", wb_[:], MWB[mj, :, :, :], reads=[MWB], writes=[wb_])
                        kb.dma("sp", bb[:], BWB[mj, :, :, :], reads=[BWB], writes=[bb])
                        psg = kb.ps()
                        for c in range(8):
                            kb.op("pe", lambda e: e.matmul(psg[:, 0:tl], lhsT=wb_[:, c, :], rhs=u[:, c, 0:tl], start=(c == 0), stop=(c == 7)),
                                  reads=[wb_, u], writes=[psg])
                        psb = kb.ps()
                        for c in range(4):
                            kb.op("pe", lambda e: e.matmul(psb[:, 0:tl], lhsT=bb[:, c, :], rhs=y[:, m, c, 0:tl], start=(c == 0), stop=(c == 3)),
                                  reads=[bb, y], writes=[psb])
                        kb.op("act", lambda e: e.activation(out=g[:, 0:tl], in_=psg[:, 0:tl], func=AF.Sigmoid, bias=mb[:, mj:mj + 1]),
                              reads=[psg, mb], writes=[g])
                        if m == 0:
                            kb.op("dve", lambda e: e.tensor_tensor(out=acc[:, j, 0:tl], in0=psb[:, 0:tl], in1=g[:, 0:tl], op=ALU.mult),
                                  reads=[psb, g], writes=[acc], waw=False)
                        else:
                            kb.op("dve", lambda e: e.tensor_tensor(out=t_[:, 0:tl], in0=psb[:, 0:tl], in1=g[:, 0:tl], op=ALU.mult),
                                  reads=[psb, g], writes=[t_])
                            kb.op("dve", lambda e: e.tensor_tensor(out=acc[:, j, 0:tl], in0=acc[:, j, 0:tl], in1=t_[:, 0:tl], op=ALU.add),
                                  reads=[t_, acc], writes=[acc])
                kb.op("act", lambda e: e.copy(out=accb[:, :, 0:tl], in_=acc[:, :, 0:tl]), reads=[acc], writes=[accb])
                for j2 in range(8):
                    wb_ = mwb[wi % 3]
                    wi += 1
                    kb.dma("sp", wb_[:], OWB[j2, :, :, :], reads=[OWB], writes=[wb_])
                    pso = kb.ps()
                    for c in range(8):
                        kb.op("pe", lambda e: e.matmul(pso[:, 0:tl], lhsT=wb_[:, c, :], rhs=accb[:, c, 0:tl], start=(c == 0), stop=(c == 7)),
                              reads=[wb_, accb], writes=[pso])
                    kb.op("dve", lambda e: e.scalar_tensor_tensor(out=ho[:, j2, 0:tl], in0=pso[:, 0:tl], scalar=modG[:, j2, jc:jc + 1], in1=hi[:, j2, 0:tl],
                                                                  op0=ALU.mult, op1=ALU.add),
                          reads=[pso, modG, hi], writes=[ho], waw=False)
                kb.dma("pool", HTv[:, :, g0:g0 + tl], ho[:, :, 0:tl], reads=[ho], writes=[HT])


def make_A(kb, P, L, b, sT, negA):
    AT, ident = L["AT"], L["ident"]
    onesr = P.sb("onesr", [16, TT])
    Pf = P.sb("Pf", [16, TT])
    Ab = P.sb("Ab", [16, TT])
    cc = P.sb("cc", [16, 2])
    kb.op("dve", lambda e: e.memset(onesr[:], 1.0), writes=[onesr])
    kb.op("dve", lambda e: e.tensor_tensor_scan(out=Pf[:], data0=onesr[:], data1=sT[:], initial=0.0, op0=ALU.mult, op1=ALU.add),
          reads=[onesr, sT], writes=[Pf])
    kb.op("dve", lambda e: e.tensor_copy(out=cc[:, 0:1], in_=Pf[:, TC - 1:TC]), reads=[Pf], writes=[cc])
    kb.op("dve", lambda e: e.tensor_tensor(out=cc[:, 1:2], in0=Pf[:, TC - 1:TC], in1=Pf[:, TT - 1:TT], op=ALU.add), reads=[Pf, cc], writes=[cc])
    kb.op("dve", lambda e: e.scalar_tensor_tensor(out=Ab[:, 0:TC], in0=sT[:, 0:TC], scalar=cc[:, 0:1], in1=Pf[:, 0:TC], op0=ALU.add, op1=ALU.subtract),
          reads=[sT, cc, Pf], writes=[Ab])
    kb.op("dve", lambda e: e.scalar_tensor_tensor(out=Ab[:, TC:TT], in0=sT[:, TC:TT], scalar=cc[:, 1:2], in1=Pf[:, TC:TT], op0=ALU.add, op1=ALU.subtract),
          reads=[sT, cc, Pf], writes=[Ab])
    kb.op("dve", lambda e: e.tensor_copy(out=Ab[0:8, :], in_=Pf[0:8, :]), reads=[Pf, Ab], writes=[Ab])
    kb.dma("sp", AT[b, :, :], Ab[:], reads=[Ab], writes=[AT])
    for i in range(TT // 128):
        ps = kb.ps6()
        kb.op("pe", lambda e: e.transpose(out=ps[:, 0:16], in_=Ab[:, i * 128:(i + 1) * 128], identity=ident[0:16, 0:16]),
              reads=[Ab, ident], writes=[ps])
        kb.op("dve", lambda e: e.tensor_scalar(out=negA[:, i, :], in0=ps[:, 0:16], scalar1=-1.0, scalar2=None, op0=ALU.mult),
              reads=[ps], writes=[negA], waw=False)


def tok_transpose(kb, L, srcT, dst, nchunk, scale=None):
    ident = L["ident"]
    k = 0
    for i in range(TT // 128):
        for c in range(nchunk):
            ps = kb.ps6()
            kb.op("pe", lambda e: e.transpose(out=ps[:, 0:128], in_=srcT[:, c, i * 128:(i + 1) * 128], identity=ident[:]),
                  reads=[srcT, ident], writes=[ps])
            if k % 2:
                kb.op("act", lambda e: e.copy(out=dst[:, i, c * 128:(c + 1) * 128], in_=ps[:, 0:128]), reads=[ps], writes=[dst], waw=False)
            else:
                kb.op("dve", lambda e: e.tensor_copy(out=dst[:, i, c * 128:(c + 1) * 128], in_=ps[:, 0:128]), reads=[ps], writes=[dst], waw=False)
            k += 1


def decay_attn(kb, P, L, b, kq, Vtok, cscal, negA, wk):
    AT, YRAW, maskF, maskB = L["AT"], L["YRAW"], L["maskF"], L["maskB"]
    abc, Et, Tt, Wt, yst = wk
    accb = [kb.psum[6], kb.psum[7]]
    U = []
    gi = 0
    for h in range(8):
        for (t0, tl) in TILES:
            c0, c1 = t0 // 128, (t0 + tl) // 128
            isctx_q = t0 < TC
            units = []
            for i in range(TT // 128):
                isctx_k = i < 2
                for d in range(2):
                    if c0 <= i < c1:
                        units.append((i, d, i - c0))
                    elif d == 0 and i < c0:
                        units.append((i, d, None))
                    elif d == 1 and ((i >= c1 and not isctx_q) or (isctx_k and not isctx_q)):
                        units.append((i, d, None))
            for k, (i, d, off) in enumerate(units):
                U.append(dict(h=h, t0=t0, tl=tl, i=i, d=d, off=off, first=(k == 0), last=(k == len(units) - 1), g=gi,
                              newi=(k == 0 or units[k - 1][0] != i), newh=(k == 0 and t0 == 0)))
            gi += 1
    state = {"st": None, "tn": 0}

    def stA(n, u):
        h, t0, tl, i, d, off = u["h"], u["t0"], u["tl"], u["i"], u["d"], u["off"]
        ab = abc[h % 2]
        if u["newh"]:
            for dd in range(2):
                kb.dma("sp", ab[:, dd, :], AT[b, dd * 8 + h:dd * 8 + h + 1, :].to_broadcast([128, TT]), reads=[AT], writes=[ab])
        lf, rf, bufs = kq(h)
        if u["newi"]:
            st = kb.ps6()
            kb.op("pe", lambda e: e.matmul(st[:, 0:tl], lhsT=lf(i), rhs=rf(t0, t0 + tl), start=True, stop=True), reads=bufs, writes=[st])
            state["st"] = st
        u["st"] = state["st"]
        E = Et[n % 3]
        col = d * 8 + h
        if off is None:
            kb.op("act", lambda e: e.activation(out=E[:, 0:tl], in_=ab[:, d, t0:t0 + tl], func=AF.Exp, bias=negA[:, i, col:col + 1]),
                  reads=[ab, negA], writes=[E])
        else:
            T_ = Tt[state["tn"] % 2]
            state["tn"] += 1
            mk = maskF if d == 0 else maskB
            kb.op("dve", lambda e: e.tensor_scalar(out=T_[:, 0:tl], in0=ab[:, d, t0:t0 + tl], scalar1=negA[:, i, col:col + 1], scalar2=0.0,
                                                   op0=ALU.add, op1=ALU.min), reads=[ab, negA], writes=[T_])
            kb.op("dve", lambda e: e.tensor_tensor(out=T_[:, 0:tl], in0=T_[:, 0:tl], in1=mk[:, off, 0:tl], op=ALU.add), reads=[T_, mk], writes=[T_])
            kb.op("act", lambda e: e.activation(out=E[:, 0:tl], in_=T_[:, 0:tl], func=AF.Exp), reads=[T_], writes=[E])

    def stB(n, u):
        tl = u["tl"]
        E, W, st = Et[n % 3], Wt[n % 3], u["st"]
        cs, csb = cscal(u["i"], u["h"], u["d"])
        kb.op("dve", lambda e: e.scalar_tensor_tensor(out=W[:, 0:tl], in0=st[:, 0:tl], scalar=cs, in1=E[:, 0:tl], op0=ALU.mult, op1=ALU.mult),
              reads=[st, E] + csb, writes=[W])

    def stC(n, u):
        h, t0, tl, i = u["h"], u["t0"], u["tl"], u["i"]
        W = Wt[n % 3]
        acc = accb[u["g"] % 2]
        kb.op("pe", lambda e: e.matmul(acc[0:64, 0:tl], lhsT=Vtok[:, i, h * 64:(h + 1) * 64], rhs=W[:, 0:tl], start=u["first"], stop=u["last"]),
              reads=[Vtok, W], writes=[acc])
        if u["last"]:
            ys_ = yst[u["g"] % 2]
            kb.op("act", lambda e: e.copy(out=ys_[0:64, 0:tl], in_=acc[0:64, 0:tl]), reads=[acc], writes=[ys_])
            g0 = gtok(b, t0)
            kb.dma("sp", YRAW[h * 64:(h + 1) * 64, g0:g0 + tl], ys_[0:64, 0:tl], reads=[ys_], writes=[YRAW])

    N = len(U)
    for t in range(N + 2):
        if t < N:
            stA(t, U[t])
        if 0 <= t - 1 < N:
            stB(t - 1, U[t - 1])
        if 0 <= t - 2 < N:
            stC(t - 2, U[t - 2])


def attn_work(P):
    abc = [P.sb("abc%d" % i, [128, 2, TT]) for i in range(2)]
    Et = [P.sb("E%d" % i, [128, 512]) for i in range(3)]
    Tt = [P.sb("T%d" % i, [128, 512]) for i in range(2)]
    Wt = [P.sb("W%d" % i, [128, 512], BF16) for i in range(3)]
    yst = [P.sb("yst%d" % i, [64, 512]) for i in range(2)]
    return abc, Et, Tt, Wt, yst


def dwconv3(kb, x, acc, w, segs):
    xb, xa = x
    ab, aa = acc
    wb_, wa = w
    kb.op("dve", lambda e: e.tensor_scalar(out=aa, in0=xa, scalar1=wa[:, 1:2], scalar2=None, op0=ALU.mult), reads=[xb, wb_], writes=[ab])
    for (s0, s1) in segs:
        kb.op("dve", lambda e: e.scalar_tensor_tensor(out=aa[:, s0 + 1:s1], in0=xa[:, s0:s1 - 1], scalar=wa[:, 0:1], in1=aa[:, s0 + 1:s1],
                                                      op0=ALU.mult, op1=ALU.add), reads=[xb, wb_, ab], writes=[ab])
        kb.op("dve", lambda e: e.scalar_tensor_tensor(out=aa[:, s0:s1 - 1], in0=xa[:, s0 + 1:s1], scalar=wa[:, 2:3], in1=aa[:, s0:s1 - 1],
                                                      op0=ALU.mult, op1=ALU.add), reads=[xb, wb_, ab], writes=[ab])


SEGS = [(0, TC), (TC, TT)]


def layer_ssd(kb, l, L):
    PT, XS, YRAW, YST, ones = (L[k] for k in ("PT", "XS", "YRAW", "YST", "ones"))
    for b in range(NB):
        with Phase(kb) as P:
            g0 = gtok(b, 0)
            cw = P.sb("cw", [128, 8, 3])
            cb = P.sb("cb", [128, 8])
            kb.dma("sp", cw[:], L["ssdcw_in"][l, :, :, :], writes=[cw])
            kb.dma("sp", cb[:], L["ssdcb_in"][l, :, :], writes=[cb])
            BT = P.sb("BT", [128, 2, TT], BF16)
            CT = P.sb("CT", [128, 2, TT], BF16)
            Vtok = P.sb("Vtok", [128, 18, 512], BF16)
            negA = P.sb("negA", [128, 18, 16])
            dttok = P.sb("dttok", [128, 18, 16])
            with Phase(kb) as Q:
                xin = [Q.sb("xin%d" % i, [128, TT]) for i in range(2)]
                cacc = [Q.sb("cacc%d" % i, [128, TT]) for i in range(2)]
                xs32 = Q.sb("xs32", [128, 4, TT])
                for c in range(8):
                    xi, ca = xin[c % 2], cacc[c % 2]
                    r0 = C_XBC + c * 128
                    kb.dma("sp", xi[:], PT[r0:r0 + 128, g0:g0 + TT], reads=[PT], writes=[xi])
                    dwconv3(kb, (xi, xi[:]), (ca, ca[:]), (cw, cw[:, c, :]), SEGS)
                    if c < 4:
                        kb.op("act", lambda e: e.activation(out=xs32[:, c, :], in_=ca[:], func=AF.Silu, bias=cb[:, c:c + 1]),
                              reads=[ca, cb], writes=[xs32], waw=False)
                        kb.dma("sp", XS[c * 128:(c + 1) * 128, g0:g0 + TT], xs32[:, c, :], reads=[xs32], writes=[XS])
                    else:
                        dst = BT if c < 6 else CT
                        kb.op("act", lambda e: e.activation(out=dst[:, c % 2, :], in_=ca[:], func=AF.Silu, bias=cb[:, c:c + 1]),
                              reads=[ca, cb], writes=[dst], waw=False)
                tok_transpose(kb, L, xs32, Vtok, 4)
            with Phase(kb) as Q:
                dtb = Q.sb("dtb", [16, 1])
                alog = Q.sb("alog", [16, 1])
                dtT = Q.sb("dtT", [16, TT])
                sT = Q.sb("sT", [16, TT])
                kb.dma("sp", dtb[:], L["ssddtb_in"][l, :, :], writes=[dtb])
                kb.dma("sp", alog[:], L["ssdalog_in"][l, :, :], writes=[alog])
                kb.dma("sp", dtT[:], PT[C_DT:C_DT + 16, g0:g0 + TT], reads=[PT], writes=[dtT])
                kb.op("act", lambda e: e.activation(out=dtT[:], in_=dtT[:], func=AF.Exp, bias=dtb[:, 0:1]), reads=[dtT, dtb], writes=[dtT])
                kb.op("act", lambda e: e.activation(out=dtT[:], in_=dtT[:], func=AF.Ln, bias=1.0), reads=[dtT], writes=[dtT])
                kb.op("act", lambda e: e.activation(out=alog[:], in_=alog[:], func=AF.Exp), reads=[alog], writes=[alog])
                kb.op("dve", lambda e: e.tensor_scalar(out=alog[:], in0=alog[:], scalar1=-1.0, scalar2=None, op0=ALU.mult), reads=[alog], writes=[alog])
                kb.op("dve", lambda e: e.tensor_scalar(out=sT[:], in0=dtT[:], scalar1=alog[:, 0:1], scalar2=None, op0=ALU.mult),
                      reads=[dtT, alog], writes=[sT])
                make_A(kb, Q, L, b, sT, negA)
                for i in range(TT // 128):
                    ps = kb.ps6()
                    kb.op("pe", lambda e: e.transpose(out=ps[:, 0:16], in_=dtT[:, i * 128:(i + 1) * 128], identity=L["ident"][0:16, 0:16]),
                          reads=[dtT, L["ident"]], writes=[ps])
                    kb.op("dve", lambda e: e.tensor_copy(out=dttok[:, i, :], in_=ps[:, 0:16]), reads=[ps], writes=[dttok], waw=False)
            with Phase(kb) as Q:
                wk = attn_work(Q)

                def kq(h):
                    g = h // 4
                    return (lambda i: BT[:, g, i * 128:(i + 1) * 128]), (lambda n0, n1: CT[:, g, n0:n1]), [BT, CT]

                def cscal(i, h, d):
                    col = d * 8 + h
                    return dttok[:, i, col:col + 1], [dttok]
                decay_attn(kb, Q, L, b, kq, Vtok, cscal, negA, wk)
    with Phase(kb) as P:
        Dv = P.sb("Dv", [128, 4])
        nw = P.sb("nw", [128, 4])
        kb.dma("sp", Dv[:], L["ssdD_in"][l, :, :], writes=[Dv])
        kb.dma("sp", nw[:], L["ssdnw_in"][l, :, :], writes=[nw])
        yb = [P.sb("yb%d" % i, [128, 4, 512]) for i in range(2)]
        xb = [P.sb("xb%d" % i, [128, 4, 512]) for i in range(2)]
        zb = [P.sb("zb%d" % i, [128, 4, 512]) for i in range(2)]
        sq = P.sb("sq", [128, 4, 512])
        rstd = P.sb("rstd", [128, 512])
        ob = [P.sb("ob%d" % i, [128, 4, 512], BF16) for i in range(2)]
        it = 0
        for b in range(NB):
            for (t0, tl) in TILES:
                y, x, z, o = yb[it % 2], xb[it % 2], zb[it % 2], ob[it % 2]
                it += 1
                g0 = gtok(b, t0)
                kb.dma("sp", y[:, :, 0:tl], YRAW.t.rearrange("(c p) g -> p c g", p=128)[:, :, g0:g0 + tl], reads=[YRAW], writes=[y])
                kb.dma("sp", x[:, :, 0:tl], XS.t.rearrange("(c p) g -> p c g", p=128)[:, :, g0:g0 + tl], reads=[XS], writes=[x])
                kb.dma("sp", z[:, :, 0:tl], PT[C_GZ:C_GZ + 512, :].rearrange("(c p) g -> p c g", p=128)[:, :, g0:g0 + tl], reads=[PT], writes=[z])
                for c in range(4):
                    kb.op("dve", lambda e: e.scalar_tensor_tensor(out=y[:, c, 0:tl], in0=x[:, c, 0:tl], scalar=Dv[:, c:c + 1], in1=y[:, c, 0:tl],
                                                                  op0=ALU.mult, op1=ALU.add), reads=[x, Dv, y], writes=[y])
                kb.op("act", lambda e: e.activation(out=z[:, :, 0:tl], in_=z[:, :, 0:tl], func=AF.Silu), reads=[z], writes=[z])
                kb.op("dve", lambda e: e.tensor_tensor(out=y[:, :, 0:tl], in0=y[:, :, 0:tl], in1=z[:, :, 0:tl], op=ALU.mult), reads=[y, z], writes=[y])
                kb.op("act", lambda e: e.activation(out=sq[:, :, 0:tl], in_=y[:, :, 0:tl], func=AF.Square), reads=[y], writes=[sq])
                ps = kb.ps()
                for c in range(4):
                    kb.op("pe", lambda e: e.matmul(ps[:, 0:tl], lhsT=ones[:], rhs=sq[:, c, 0:tl], start=(c == 0), stop=(c == 3)),
                          reads=[ones, sq], writes=[ps])
                rsqrt(kb, rstd, rstd[:, 0:tl], ps, ps[:, 0:tl], 1.0 / DBR, 1e-6)
                for c in range(4):
                    kb.op("dve", lambda e: e.scalar_tensor_tensor(out=o[:, c, 0:tl], in0=y[:, c, 0:tl], scalar=nw[:, c:c + 1], in1=rstd[:, 0:tl],
                                                                  op0=ALU.mult, op1=ALU.mult), reads=[y, nw, rstd], writes=[o], waw=False)
                kb.dma("pool", L["YSv"][:, 1, :, g0:g0 + tl], o[:, :, 0:tl], reads=[o], writes=[YST])


def layer_ret(kb, l, L):
    PT, YRAW, YST, blk64 = (L[k] for k in ("PT", "YRAW", "YST", "blk64"))
    for b in range(NB):
        with Phase(kb) as P:
            g0 = gtok(b, 0)
            QT = P.sb("QT", [128, 4, TT], BF16)
            KT = P.sb("KT", [128, 4, TT], BF16)
            Vtok = P.sb("Vtok", [128, 18, 512], BF16)
            negA = P.sb("negA", [128, 18, 16])
            with Phase(kb) as Q:
                v32 = Q.sb("v32", [128, 4, TT])
                for c in range(4):
                    r0 = C_QKV + 1024 + c * 128
                    kb.dma("sp", v32[:, c, :], PT[r0:r0 + 128, g0:g0 + TT], reads=[PT], writes=[v32])
                tok_transpose(kb, L, v32, Vtok, 4)
            with Phase(kb) as Q:
                lg = Q.sb("lg", [16, 1])
                sT = Q.sb("sT", [16, TT])
                kb.dma("sp", lg[:], L["retlg_in"][l, :, :], writes=[lg])
                kb.op("dve", lambda e: e.memset(sT[:], 1.0), writes=[sT])
                kb.op("dve", lambda e: e.tensor_scalar(out=sT[:], in0=sT[:], scalar1=lg[:, 0:1], scalar2=None, op0=ALU.mult), reads=[sT, lg], writes=[sT])
                make_A(kb, Q, L, b, sT, negA)
            with Phase(kb) as Q:
                cosT = Q.sb("cosT", [128, TL])
                sinT = Q.sb("sinT", [128, TL])
                kb.dma("sp", cosT[:], L["ropec_in"][:, :], writes=[cosT])
                kb.dma("sp", sinT[:], L["ropes_in"][:, :], writes=[sinT])
                xin = [Q.sb("xin%d" % i, [128, TT]) for i in range(2)]
                xsw = [Q.sb("xsw%d" % i, [128, TT]) for i in range(2)]
                k = 0
                for (dst, r0, rs) in ((QT, C_QKV, C_QSW), (KT, C_QKV + 512, C_KSW)):
                    for c in range(4):
                        xi, xw = xin[k % 2], xsw[k % 2]
                        k += 1
                        kb.dma("sp", xi[:], PT[r0 + c * 128:r0 + (c + 1) * 128, g0:g0 + TT], reads=[PT], writes=[xi])
                        kb.dma("sp", xw[:, TC:TT], PT[rs + c * 128:rs + (c + 1) * 128, g0 + TC:g0 + TT], reads=[PT], writes=[xw])
                        kb.op("act", lambda e: e.copy(out=dst[:, c, 0:TC], in_=xi[:, 0:TC]), reads=[xi], writes=[dst], waw=False)
                        kb.op("dve", lambda e: e.tensor_tensor(out=xi[:, TC:TT], in0=xi[:, TC:TT], in1=cosT[:], op=ALU.mult), reads=[xi, cosT], writes=[xi])
                        kb.op("dve", lambda e: e.tensor_tensor(out=xw[:, TC:TT], in0=xw[:, TC:TT], in1=sinT[:], op=ALU.mult), reads=[xw, sinT], writes=[xw])
                        kb.op("dve", lambda e: e.tensor_tensor(out=dst[:, c, TC:TT], in0=xi[:, TC:TT], in1=xw[:, TC:TT], op=ALU.add),
                              reads=[xi, xw], writes=[dst], waw=False)
            with Phase(kb) as Q:
                wk = attn_work(Q)

                def kq(h):
                    c, p0 = h // 2, (h % 2) * 64
                    return (lambda i: KT[p0:p0 + 64, c, i * 128:(i + 1) * 128]), (lambda n0, n1: QT[p0:p0 + 64, c, n0:n1]), [KT, QT]

                def cscal(i, h, d):
                    return 0.125, []
                decay_attn(kb, Q, L, b, kq, Vtok, cscal, negA, wk)
    with Phase(kb) as P:
        gw = P.sb("gw", [128, 4])
        gb = P.sb("gb", [128, 4])
        kb.dma("sp", gw[:], L["retgw_in"][l, :, :], writes=[gw])
        kb.dma("sp", gb[:], L["retgb_in"][l, :, :], writes=[gb])
        groupnorm_gate_readout(kb, P, L, YRAW, C_GR, gw, gb, 1e-5, 3, None)


def groupnorm_gate_readout(kb, P, L, YR, c_gate, gw, gb, eps, mslot, extra):
    PT, YST, blk64 = L["PT"], L["YST"], L["blk64"]
    yb = [P.sb("yb%d" % i, [128, 4, 512]) for i in range(2)]
    zb = [P.sb("zb%d" % i, [128, 4, 512]) for i in range(2)]
    eb = [P.sb("eb%d" % i, [128, 4, 512]) for i in range(2)] if extra is not None else None
    dd = [P.sb("dd%d" % i, [128, 512]) for i in range(2)]
    sq = [P.sb("sq%d" % i, [128, 512]) for i in range(2)]
    rs = [P.sb("rs%d" % i, [128, 512]) for i in range(2)]
    ob = [P.sb("ob%d" % i, [128, 4, 512], BF16) for i in range(2)]
    it = 0
    k = 0
    for b in range(NB):
        for (t0, tl) in TILES:
            y, z, o = yb[it % 2], zb[it % 2], ob[it % 2]
            ex = eb[it % 2] if extra is not None else None
            it += 1
            g0 = gtok(b, t0)
            kb.dma("sp", y[:, :, 0:tl], YR.t.rearrange("(c p) g -> p c g", p=128)[:, :, g0:g0 + tl], reads=[YR], writes=[y])
            kb.dma("sp", z[:, :, 0:tl], PT[c_gate:c_gate + 512, :].rearrange("(c p) g -> p c g", p=128)[:, :, g0:g0 + tl], reads=[PT], writes=[z])
            if extra is not None:
                kb.dma("sp", ex[:, :, 0:tl], extra.t.rearrange("(c p) g -> p c g", p=128)[:, :, g0:g0 + tl], reads=[extra], writes=[ex])
            kb.op("act", lambda e: e.activation(out=z[:, :, 0:tl], in_=z[:, :, 0:tl], func=AF.Silu), reads=[z], writes=[z])
            for c in range(4):
                d_, s_, r_ = dd[k % 2], sq[k % 2], rs[k % 2]
                k += 1
                pm = kb.ps()
                kb.op("pe", lambda e: e.matmul(pm[:, 0:tl], lhsT=blk64[:], rhs=y[:, c, 0:tl], start=True, stop=True), reads=[blk64, y], writes=[pm])
                kb.op("dve", lambda e: e.tensor_tensor(out=d_[:, 0:tl], in0=y[:, c, 0:tl], in1=pm[:, 0:tl], op=ALU.subtract), reads=[y, pm], writes=[d_])
                kb.op("act", lambda e: e.activation(out=s_[:, 0:tl], in_=d_[:, 0:tl], func=AF.Square), reads=[d_], writes=[s_])
                pv = kb.ps()
                kb.op("pe", lambda e: e.matmul(pv[:, 0:tl], lhsT=blk64[:], rhs=s_[:, 0:tl], start=True, stop=True), reads=[blk64, s_], writes=[pv])
                rsqrt(kb, r_, r_[:, 0:tl], pv, pv[:, 0:tl], 1.0, eps)
                kb.op("dve", lambda e: e.tensor_tensor(out=d_[:, 0:tl], in0=d_[:, 0:tl], in1=r_[:, 0:tl], op=ALU.mult), reads=[d_, r_], writes=[d_])
                kb.op("act", lambda e: e.activation(out=d_[:, 0:tl], in_=d_[:, 0:tl], func=AF.Identity, scale=gw[:, c:c + 1], bias=gb[:, c:c + 1]),
                      reads=[d_, gw, gb], writes=[d_])
                if extra is not None:
                    kb.op("dve", lambda e: e.tensor_tensor(out=d_[:, 0:tl], in0=d_[:, 0:tl], in1=ex[:, c, 0:tl], op=ALU.add), reads=[d_, ex], writes=[d_])
                kb.op("dve", lambda e: e.tensor_tensor(out=o[:, c, 0:tl], in0=d_[:, 0:tl], in1=z[:, c, 0:tl], op=ALU.mult), reads=[d_, z], writes=[o], waw=False)
            kb.dma("pool", L["YSv"][:, mslot, :, g0:g0 + tl], o[:, :, 0:tl], reads=[o], writes=[YST])


TWO_PI = 2.0 * math.pi


def sin_inplace(kb, xb, xa, kib, kia, kfb, kfa, mpi):
    kb.op("dve", lambda e: e.tensor_scalar(out=xa, in0=xa, scalar1=17.0 * math.pi, scalar2=None, op0=ALU.add), reads=[xb], writes=[xb])
    kb.op("dve", lambda e: e.tensor_scalar(out=kfa, in0=xa, scalar1=1.0 / TWO_PI, scalar2=None, op0=ALU.mult), reads=[xb], writes=[kfb])
    kb.op("dve", lambda e: e.tensor_copy(out=kia, in_=kfa), reads=[kfb], writes=[kib])
    kb.op("dve", lambda e: e.tensor_copy(out=kfa, in_=kia), reads=[kib], writes=[kfb])
    kb.op("dve", lambda e: e.scalar_tensor_tensor(out=xa, in0=kfa, scalar=-TWO_PI, in1=xa, op0=ALU.mult, op1=ALU.add), reads=[kfb, xb], writes=[xb])
    kb.op("dve", lambda e: e.tensor_scalar(out=kfa, in0=xa, scalar1=0.0, scalar2=TWO_PI, op0=ALU.is_lt, op1=ALU.mult), reads=[xb], writes=[kfb])
    kb.op("dve", lambda e: e.tensor_tensor(out=xa, in0=xa, in1=kfa, op=ALU.add), reads=[xb, kfb], writes=[xb])
    kb.op("act", lambda e: e.activation(out=xa, in_=xa, func=AF.Sin, bias=mpi[0:xa.shape[0], 0:1]), reads=[xb, mpi], writes=[xb])


def fwd_dft(kb, P, n, tab, load_rhs, consume):
    nch = n // 128
    fcs = [P.sb("fc%d" % i, [128, nch, 128], BF16) for i in range(2)]
    fss = [P.sb("fs%d" % i, [128, nch, 128], BF16) for i in range(2)]
    k = 0
    pend = None
    for cc in range(4):
        rt = load_rhs(cc)
        for fch in range(nch):
            fc, fs = fcs[k % 2], fss[k % 2]
            k += 1
            kb.dma("sp", fc[:], tab["Fc"][fch, :, :, :], writes=[fc])
            kb.dma("sp", fs[:], tab["Fs"][fch, :, :, :], writes=[fs])
            pre, pim = kb.ps(), kb.ps()
            for j in range(nch):
                kb.op("pe", lambda e: e.matmul(pre[:, 0:256], lhsT=fc[:, j, :], rhs=rt[:, j, :], start=(j == 0), stop=(j == nch - 1)),
                      reads=[fc, rt], writes=[pre])
            for j in range(nch):
                kb.op("pe", lambda e: e.matmul(pim[:, 0:256], lhsT=fs[:, j, :], rhs=rt[:, j, :], start=(j == 0), stop=(j == nch - 1)),
                      reads=[fs, rt], writes=[pim])
            if pend is not None:
                consume(*pend)
            pend = (cc, fch, pre, pim)
            if fch == nch - 1:
                consume(*pend)
                pend = None


def layer_hyena(kb, l, L):
    PT, HYC, UTOK, Z1T, YST, ident = (L[k] for k in ("PT", "HYC", "UTOK", "Z1T", "YST", "ident"))
    for b in range(NB):
        with Phase(kb) as P:
            g0 = gtok(b, 0)
            cw = P.sb("cw", [128, 12, 3])
            cb = P.sb("cb", [128, 12])
            kb.dma("sp", cw[:], L["hycw_in"][l, :, :, :], writes=[cw])
            kb.dma("sp", cb[:], L["hycb_in"][l, :, :], writes=[cb])
            xin = [P.sb("xin%d" % i, [128, TT]) for i in range(2)]
            cacc = [P.sb("cacc%d" % i, [128, TT]) for i in range(2)]
            v32 = P.sb("v32", [128, 4, TT])
            vt = P.sb("vt", [128, 18, 512], BF16)
            for c in range(12):
                xi, ca = xin[c % 2], cacc[c % 2]
                r0 = C_HY + c * 128
                kb.dma("sp", xi[:], PT[r0:r0 + 128, g0:g0 + TT], reads=[PT], writes=[xi])
                dwconv3(kb, (xi, xi[:]), (ca, ca[:]), (cw, cw[:, c, :]), SEGS)
                if c < 4:
                    kb.op("act", lambda e: e.activation(out=v32[:, c, :], in_=ca[:], func=AF.Identity, bias=cb[:, c:c + 1]),
                          reads=[ca, cb], writes=[v32], waw=False)
                    kb.dma("pool", HYC[c * 128:(c + 1) * 128, g0:g0 + TT], v32[:, c, :], reads=[v32], writes=[HYC])
                else:
                    kb.op("act", lambda e: e.activation(out=ca[:], in_=ca[:], func=AF.Identity, bias=cb[:, c:c + 1]), reads=[ca, cb], writes=[ca])
                    kb.dma("pool", HYC[c * 128:(c + 1) * 128, g0:g0 + TT], ca[:], reads=[ca], writes=[HYC])
            tok_transpose(kb, L, v32, vt, 4)
            kb.dma("pool", UTOK[0][g0:g0 + TT, :].rearrange("(i p) c -> p i c", p=128), vt[:], reads=[vt], writes=[UTOK[0]])
    for (n, t0s) in ((TL, TC), (TC, 0)):
        nch = n // 128
        tab = L["hytab"][n]
        HTOKn, HFn = L["HTOK"][n], L["HF"][n]
        with Phase(kb) as P:
            feats = P.sb("feats", [33, n])
            w1 = P.sb("w1", [33, 64])
            w2 = P.sb("w2", [64, 64])
            w3 = P.sb("w3", [64, 1024])
            b1 = P.sb("b1", [64, 1])
            b2 = P.sb("b2", [64, 1])
            fr = P.sb("fr", [64, 1])
            mpi = P.sb("mpi", [128, 1])
            hid1 = P.sb("hid1", [64, n])
            hid2 = P.sb("hid2", [64, n])
            ki = P.sb("ki", [64, n], mybir.dt.int32)
            kf = P.sb("kf", [64, n])
            win = P.sb("win", [128, 4, n])
            hw = [P.sb("hw%d" % i, [128, n]) for i in range(2)]
            asum = P.sb("asum", [128, 2])
            ht = [P.sb("ht%d" % i, [128, nch, 128], BF16) for i in range(2)]
            for (dst, src) in ((feats, tab["feats"][:, :]), (w1, L["hyw1_in"][l, :, :]), (w2, L["hyw2_in"][l, :, :]), (w3, L["hyw3_in"][l, :, :]),
                               (b1, L["hyb1_in"][l, :, :]), (b2, L["hyb2_in"][l, :, :]), (fr, L["hyfreq_in"][l, :, :]), (win, tab["window"][:, :, :])):
                kb.dma("sp", dst[:], src, writes=[dst])
            kb.op("dve", lambda e: e.memset(mpi[:], -math.pi), writes=[mpi])
            kb.op("dve", lambda e: e.tensor_tensor(out=b1[:], in0=b1[:], in1=fr[:], op=ALU.mult), reads=[b1, fr], writes=[b1])
            kb.op("dve", lambda e: e.tensor_tensor(out=b2[:], in0=b2[:], in1=fr[:], op=ALU.mult), reads=[b2, fr], writes=[b2])
            tiles = [(t, min(512, n - t)) for t in range(0, n, 512)]
            for (hid, w, K_, src, bb) in ((hid1, w1, 33, feats, b1), (hid2, w2, 64, hid1, b2)):
                for (t, wd) in tiles:
                    ps = kb.ps()
                    kb.op("pe", lambda e: e.matmul(ps[0:64, 0:wd], lhsT=w[0:K_, :], rhs=src[0:K_, t:t + wd], start=True, stop=True),
                          reads=[w, src], writes=[ps])
                    kb.op("dve", lambda e: e.tensor_scalar(out=hid[:, t:t + wd], in0=ps[0:64, 0:wd], scalar1=fr[:, 0:1], scalar2=bb[:, 0:1],
                                                           op0=ALU.mult, op1=ALU.add), reads=[ps, fr, bb], writes=[hid], waw=False)
                sin_inplace(kb, hid, hid[:], ki, ki[:], kf, kf[:], mpi)
            for j8 in range(8):
                h_ = hw[j8 % 2]
                hs = ht[j8 % 2]
                for (t, wd) in tiles:
                    ps = kb.ps()
                    kb.op("pe", lambda e: e.matmul(ps[:, 0:wd], lhsT=w3[:, j8 * 128:(j8 + 1) * 128], rhs=hid2[:, t:t + wd], start=True, stop=True),
                          reads=[w3, hid2], writes=[ps])
                    kb.op("dve", lambda e: e.tensor_tensor(out=h_[:, t:t + wd], in0=ps[:, 0:wd], in1=win[:, j8 % 4, t:t + wd], op=ALU.mult),
                          reads=[ps, win], writes=[h_], waw=False)
                kb.op("dve", lambda e: e.tensor_reduce(out=asum[:, 0:1], in_=h_[:], axis=AX.X, op=ALU.add, apply_absolute_value=True),
                      reads=[h_], writes=[asum])
                kb.op("dve", lambda e: e.reciprocal(out=asum[:, 1:2], in_=asum[:, 0:1]), reads=[asum], writes=[asum])
                kb.op("dve", lambda e: e.tensor_scalar(out=h_[:], in0=h_[:], scalar1=asum[:, 1:2], scalar2=None, op0=ALU.mult), reads=[h_, asum], writes=[h_])
                for i in range(nch):
                    ps = kb.ps()
                    kb.op("pe", lambda e: e.transpose(out=ps[:, 0:128], in_=h_[:, i * 128:(i + 1) * 128], identity=ident[:]), reads=[h_, ident], writes=[ps])
                    if i % 2:
                        kb.op("act", lambda e: e.copy(out=hs[:, i, :], in_=ps[:, 0:128]), reads=[ps], writes=[hs], waw=False)
                    else:
                        kb.op("dve", lambda e: e.tensor_copy(out=hs[:, i, :], in_=ps[:, 0:128]), reads=[ps], writes=[hs], waw=False)
                kb.dma("pool", HTOKn[:, j8 * 128:(j8 + 1) * 128].rearrange("(i p) c -> p i c", p=128), hs[:], reads=[hs], writes=[HTOKn])
        with Phase(kb) as P:
            rts = [P.sb("rt%d" % i, [128, nch, 256], BF16) for i in range(2)]
            ho = [P.sb("ho%d" % i, [128, 2, 256]) for i in range(2)]
            cnt = [0, 0]

            def load_rhs(cc):
                rt = rts[cnt[0] % 2]
                cnt[0] += 1
                for o in range(2):
                    c0 = o * 512 + cc * 128
                    kb.dma("sp", rt[:, :, o * 128:(o + 1) * 128], HTOKn[:, c0:c0 + 128].rearrange("(i p) c -> p i c", p=128), reads=[HTOKn], writes=[rt])
                return rt

            def consume(cc, fch, pre, pim):
                h_ = ho[cnt[1] % 2]
                cnt[1] += 1
                kb.op("act", lambda e: e.copy(out=h_[:, 0, :], in_=pre[:, 0:256]), reads=[pre], writes=[h_], waw=False)
                kb.op("dve", lambda e: e.tensor_copy(out=h_[:, 1, :], in_=pim[:, 0:256]), reads=[pim], writes=[h_], waw=False)
                for ri in range(2):
                    kb.dma("pool", HFn[ri, fch, :, cc, :], h_[:, ri, :], reads=[h_], writes=[HFn])
            fwd_dft(kb, P, n, tab, load_rhs, consume)
        for o in range(2):
            with Phase(kb) as P:
                hb = P.sb("hb", [128, 2, 4])
                kb.dma("sp", hb[:], L["hybias_in"][l, :, :, :], writes=[hb])
                rts = [P.sb("rt%d" % i, [128, nch, 256], BF16) for i in range(2)]
                Y = P.sb("Y", [128, nch, 2, 256], BF16)
                hf = [P.sb("hf%d" % i, [128, 2, 128]) for i in range(2)]
                ta = [P.sb("ta%d" % i, [128, 256]) for i in range(2)]
                tb = [P.sb("tb%d" % i, [128, 256]) for i in range(2)]
                gre = P.sb("gre", [128, nch, 256], BF16)
                gim = P.sb("gim", [128, nch, 256], BF16)
                ut = [P.sb("ut%d" % i, [128, 256]) for i in range(2)]
                xt = [P.sb("xt%d" % i, [128, 256]) for i in range(2)]
                gt = [P.sb("gt%d" % i, [128, 256]) for i in range(2)]
                zt = [P.sb("zt%d" % i, [128, 256]) for i in range(2)]
                zo = [P.sb("zo%d" % i, [128, 256], BF16) for i in range(2)]
                ztok = [P.sb("ztok%d" % i, [128, 2, 128], BF16) for i in range(2)]
                cnt = [0, 0, 0]

                def load_rhs(cc):
                    rt = rts[cnt[0] % 2]
                    cnt[0] += 1
                    for b in range(NB):
                        g0 = gtok(b, t0s)
                        kb.dma("sp", rt[:, :, b * 128:(b + 1) * 128], UTOK[o][g0:g0 + n, cc * 128:(cc + 1) * 128].rearrange("(i p) c -> p i c", p=128),
                               reads=[UTOK[o]], writes=[rt])
                    return rt

                def consume(cc, fch, pre, pim):
                    h_ = hf[cnt[1] % 2]
                    a_, b_ = ta[cnt[1] % 2], tb[cnt[1] % 2]
                    cnt[1] += 1
                    for ri in range(2):
                        kb.dma("sp", h_[:, ri, :], HFn[ri, fch, :, cc, o * 128:(o + 1) * 128], reads=[HFn], writes=[h_])
                    hre = h_[:, 0, :].unsqueeze(1).to_broadcast([128, 2, 128])
                    him = h_[:, 1, :].unsqueeze(1).to_broadcast([128, 2, 128])
                    v3 = lambda ap: ap.rearrange("p (b c) -> p b c", b=2)
                    kb.op("dve", lambda e: e.tensor_tensor(out=v3(a_[:]), in0=v3(pre[:, 0:256]), in1=hre, op=ALU.mult), reads=[pre, h_], writes=[a_])
                    kb.op("dve", lambda e: e.tensor_tensor(out=v3(b_[:]), in0=v3(pim[:, 0:256]), in1=him, op=ALU.mult), reads=[pim, h_], writes=[b_])
                    kb.op("dve", lambda e: e.tensor_tensor(out=Y[:, fch, 0, :], in0=a_[:], in1=b_[:], op=ALU.subtract), reads=[a_, b_], writes=[Y], waw=False)
                    if fch == 0:
                        kb.op("dve", lambda e: e.tensor_copy(out=Y[0:1, 0, 0, :], in_=a_[0:1, :]), reads=[a_, Y], writes=[Y])
                    a2, b2_ = ta[cnt[1] % 2], tb[cnt[1] % 2]
                    cnt[1] += 1
                    kb.op("dve", lambda e: e.tensor_tensor(out=v3(a2[:]), in0=v3(pre[:, 0:256]), in1=him, op=ALU.mult), reads=[pre, h_], writes=[a2])
                    kb.op("dve", lambda e: e.tensor_tensor(out=v3(b2_[:]), in0=v3(pim[:, 0:256]), in1=hre, op=ALU.mult), reads=[pim, h_], writes=[b2_])
                    kb.op("dve", lambda e: e.tensor_tensor(out=Y[:, fch, 1, :], in0=a2[:], in1=b2_[:], op=ALU.add), reads=[a2, b2_], writes=[Y], waw=False)
                    if fch == 0:
                        kb.op("dve", lambda e: e.tensor_copy(out=Y[0:1, 0, 1, :], in_=b_[0:1, :]), reads=[b_, Y], writes=[Y])
                    if fch == nch - 1:
                        inverse(cc)

                def inverse(cc):
                    for tt in range(n // 256):
                        kb.dma("sp", gre[:], tab["Gre"][tt, :, :, :], writes=[gre])
                        kb.dma("sp", gim[:], tab["Gim"][tt, :, :, :], writes=[gim])
                        for b in range(NB):
                            k = cnt[2]
                            cnt[2] += 1
                            g = gtok(b, t0s + tt * 256)
                            u_, x_, g_, z_, zo_, zk = ut[k % 2], xt[k % 2], gt[k % 2], zt[k % 2], zo[k % 2], ztok[k % 2]
                            usrc = HYC if o == 0 else Z1T
                            kb.dma("sp", u_[:], usrc[cc * 128:(cc + 1) * 128, g:g + 256], reads=[usrc], writes=[u_])
                            xr = 512 * (o + 1) + cc * 128
                            kb.dma("sp", x_[:], HYC[xr:xr + 128, g:g + 256], reads=[HYC], writes=[x_])
                            if o == 1:
                                kb.dma("sp", g_[:], PT[C_GH + cc * 128:C_GH + (cc + 1) * 128, g:g + 256], reads=[PT], writes=[g_])
                                kb.op("act", lambda e: e.activation(out=g_[:], in_=g_[:], func=AF.Silu), reads=[g_], writes=[g_])
                            ps = kb.ps()
                            for fch in range(nch):
                                kb.op("pe", lambda e: e.matmul(ps[:, 0:256], lhsT=Y[:, fch, 0, b * 128:(b + 1) * 128], rhs=gre[:, fch, :],
                                                               start=(fch == 0), stop=False), reads=[Y, gre], writes=[ps])
                                kb.op("pe", lambda e: e.matmul(ps[:, 0:256], lhsT=Y[:, fch, 1, b * 128:(b + 1) * 128], rhs=gim[:, fch, :],
                                                               start=False, stop=(fch == nch - 1)), reads=[Y, gim], writes=[ps])
                            kb.op("dve", lambda e: e.scalar_tensor_tensor(out=z_[:], in0=u_[:], scalar=hb[:, o, cc:cc + 1], in1=ps[:, 0:256],
                                                                          op0=ALU.mult, op1=ALU.add), reads=[u_, hb, ps], writes=[z_])
                            if o == 0:
                                kb.op("dve", lambda e: e.tensor_tensor(out=z_[:], in0=z_[:], in1=x_[:], op=ALU.mult), reads=[z_, x_], writes=[z_])
                                kb.dma("pool", Z1T[cc * 128:(cc + 1) * 128, g:g + 256], z_[:], reads=[z_], writes=[Z1T])
                                for hh in range(2):
                                    p2 = kb.ps()
                                    kb.op("pe", lambda e: e.transpose(out=p2[:, 0:128], in_=z_[:, hh * 128:(hh + 1) * 128], identity=ident[:]),
                                          reads=[z_, ident], writes=[p2])
                                    kb.op("act", lambda e: e.copy(out=zk[:, hh, :], in_=p2[:, 0:128]), reads=[p2], writes=[zk], waw=False)
                                kb.dma("pool", UTOK[1][g:g + 256, cc * 128:(cc + 1) * 128].rearrange("(i p) c -> p i c", p=128), zk[:],
                                       reads=[zk], writes=[UTOK[1]])
                            else:
                                kb.op("dve", lambda e: e.tensor_tensor(out=z_[:], in0=z_[:], in1=x_[:], op=ALU.mult), reads=[z_, x_], writes=[z_])
                                kb.op("dve", lambda e: e.tensor_tensor(out=zo_[:], in0=z_[:], in1=g_[:], op=ALU.mult), reads=[z_, g_], writes=[zo_])
                                kb.dma("pool", L["YSv"][:, 2, cc, g:g + 256], zo_[:], reads=[zo_], writes=[YST])
                fwd_dft(kb, P, n, tab, load_rhs, consume)


def tview(ap, d, tok_lo, n):
    return ap[:, tok_lo:tok_lo + n, :]


def fm_transpose(kb, L, src, dst):
    for c in range(4):
        ps = kb.ps()
        kb.op("pe", lambda e: e.transpose(out=ps[:, 0:128], in_=src[:, c * 128:(c + 1) * 128], identity=L["ident"][:]), reads=[src, L["ident"]], writes=[ps])
        if c % 2:
            kb.op("act", lambda e: e.copy(out=dst[:, c, :], in_=ps[:, 0:128]), reads=[ps], writes=[dst], waw=False)
        else:
            kb.op("dve", lambda e: e.tensor_copy(out=dst[:, c, :], in_=ps[:, 0:128]), reads=[ps], writes=[dst], waw=False)


TBLK = 8


def rwkv_prep(kb, l, L):
    PT, RKVTOK, XD, YSC, BONUST, YRAW = (L[k] for k in ("PT", "RKVTOK", "XD", "YSC", "BONUST", "YRAW"))
    for b in range(NB):
        with Phase(kb) as P:
            g0 = gtok(b, 0)
            mu = P.sb("mu", [128, 12])
            cw = P.sb("cw", [128, 12, 3])
            kb.dma("sp", mu[:], L["rwmu_in"][l, :, :], writes=[mu])
            kb.op("dve", lambda e: e.tensor_scalar(out=cw[:, :, 0], in0=mu[:], scalar1=0.5, scalar2=None, op0=ALU.mult), reads=[mu], writes=[cw], waw=False)
            kb.op("dve", lambda e: e.tensor_scalar(out=cw[:, :, 2], in0=mu[:], scalar1=0.5, scalar2=None, op0=ALU.mult), reads=[mu], writes=[cw], waw=False)
            kb.op("dve", lambda e: e.tensor_scalar(out=cw[:, :, 1], in0=mu[:], scalar1=-1.0, scalar2=1.0, op0=ALU.mult, op1=ALU.add), reads=[mu], writes=[cw], waw=False)
            xin = [P.sb("xin%d" % i, [128, TT]) for i in range(2)]
            v32 = P.sb("v32", [128, 4, TT])
            vt = P.sb("vt", [128, 18, 512])
            for grp in range(3):
                for c4 in range(4):
                    c = grp * 4 + c4
                    xi = xin[c % 2]
                    kb.dma("sp", xi[:], PT[C_RKV + c * 128:C_RKV + (c + 1) * 128, g0:g0 + TT], reads=[PT], writes=[xi])
                    dwconv3(kb, (xi, xi[:]), (v32, v32[:, c4, :]), (cw, cw[:, c, :]), SEGS)
                tok_transpose(kb, L, v32, vt, 4)
                kb.dma("pool", RKVTOK[grp][g0:g0 + TT, :].rearrange("(i p) c -> p i c", p=128), vt[:], reads=[vt], writes=[RKVTOK[grp]])
    with Phase(kb) as P:
        def bc(name, src):
            t = P.sb(name, [128, 512])
            kb.dma("sp", t[:], src.to_broadcast([128, 512]), writes=[t])
            return t
        w0 = [bc("w0%d" % d, L["rww0_in"][l, d:d + 1, :]) for d in range(2)]
        a0 = [bc("a0%d" % d, L["rwa0_in"][l, d:d + 1, :]) for d in range(2)]
        k_k = bc("k_k", L["rwkk_in"][l, 0:1, :])
        k_a = bc("k_a", L["rwka_in"][l, 0:1, :])
        r_k = bc("r_k", L["rwrk_in"][l, 0:1, :])
        w2 = P.sb("w2", [128, 512])
        a2 = P.sb("a2", [128, 512])
        kb.dma("sp", w2[:], L["rww2_in"][l, :, :], writes=[w2])
        kb.dma("sp", a2[:], L["rwa2_in"][l, :, :], writes=[a2])
        zw = P.sb("zw", [128, TT])
        za = P.sb("za", [128, TT])
        NTL = 22
        tl_ = [[P.sb("t%d_%d" % (j, i), [128, 512]) for i in range(NTL)] for j in range(2)]
        sm = [P.sb("sm%d" % i, [128, 8]) for i in range(2)]
        bst = [P.sb("bst%d" % i, [128, 4, 128]) for i in range(2)]
        v3 = lambda ap: ap.rearrange("p (h k) -> p h k", k=64)
        it = 0
        for b in range(NB):
            g0 = gtok(b, 0)
            kb.dma("sp", zw[:], PT[C_ZW:C_ZW + 128, g0:g0 + TT], reads=[PT], writes=[zw])
            kb.dma("sp", za[:], PT[C_ZA:C_ZA + 128, g0:g0 + TT], reads=[PT], writes=[za])
            kb.op("act", lambda e: e.activation(out=zw[:], in_=zw[:], func=AF.Tanh), reads=[zw], writes=[zw])
            for i in range(TT // 128):
                T = tl_[it % 2]
                s_ = sm[it % 2]
                bs_ = bst[it % 2]
                it += 1
                g = g0 + i * 128
                r, k, v = T[0], T[1], T[2]
                for (dst, src) in ((r, RKVTOK[0]), (k, RKVTOK[1]), (v, RKVTOK[2])):
                    kb.dma("sp", dst[:], src[g:g + 128, :], reads=[src], writes=[dst])
                kx, sq, kk = T[3], T[4], T[5]
                kb.op("dve", lambda e: e.tensor_tensor(out=kx[:], in0=k[:], in1=k_k[:], op=ALU.mult), reads=[k, k_k], writes=[kx])
                kb.op("act", lambda e: e.activation(out=sq[:], in_=kx[:], func=AF.Square), reads=[kx], writes=[sq])
                kb.op("dve", lambda e: e.tensor_reduce(out=s_[:], in_=v3(sq[:]), axis=AX.X, op=ALU.add), reads=[sq], writes=[s_])
                kb.op("dve", lambda e: e.tensor_scalar(out=s_[:], in0=s_[:], scalar1=1e-12, scalar2=None, op0=ALU.max), reads=[s_], writes=[s_])
                kb.op("act", lambda e: e.activation(out=s_[:], in_=s_[:], func=AF.Sqrt), reads=[s_], writes=[s_])
                kb.op("dve", lambda e: e.reciprocal(out=s_[:], in_=s_[:]), reads=[s_], writes=[s_])
                kb.op("dve", lambda e: e.tensor_tensor(out=v3(kk[:]), in0=v3(kx[:]), in1=s_[:].unsqueeze(2).to_broadcast([128, 8, 64]), op=ALU.mult),
                      reads=[kx, s_], writes=[kk])
                kds = []
                for d in range(2):
                    wr, dec, aa, kd, bd = T[6 + d * 5], T[7 + d * 5], T[8 + d * 5], T[9 + d * 5], T[10 + d * 5]
                    pw = kb.ps()
                    kb.op("pe", lambda e: e.matmul(pw[:, :], lhsT=zw[d * 64:(d + 1) * 64, i * 128:(i + 1) * 128], rhs=w2[d * 64:(d + 1) * 64, :], start=True, stop=True),
                          reads=[zw, w2], writes=[pw])
                    pa = kb.ps()
                    kb.op("pe", lambda e: e.matmul(pa[:, :], lhsT=za[d * 64:(d + 1) * 64, i * 128:(i + 1) * 128], rhs=a2[d * 64:(d + 1) * 64, :], start=True, stop=True),
                          reads=[za, a2], writes=[pa])
                    kb.op("dve", lambda e: e.tensor_tensor(out=wr[:], in0=pw[:, :], in1=w0[d][:], op=ALU.add), reads=[pw, w0[d]], writes=[wr])
                    kb.op("act", lambda e: e.activation(out=wr[:], in_=wr[:], func=AF.Sigmoid), reads=[wr], writes=[wr])
                    kb.op("act", lambda e: e.activation(out=dec[:], in_=wr[:], func=AF.Exp, scale=-math.exp(-0.5)), reads=[wr], writes=[dec])
                    kb.op("dve", lambda e: e.tensor_tensor(out=aa[:], in0=pa[:, :], in1=a0[d][:], op=ALU.add), reads=[pa, a0[d]], writes=[aa])
                    kb.op("act", lambda e: e.activation(out=aa[:], in_=aa[:], func=AF.Sigmoid), reads=[aa], writes=[aa])
                    kb.op("dve", lambda e: e.tensor_tensor(out=bd[:], in0=kk[:], in1=aa[:], op=ALU.mult), reads=[kk, aa], writes=[bd])
                    kb.op("dve", lambda e: e.scalar_tensor_tensor(out=kd[:], in0=aa[:], scalar=-1.0, in1=k_a[:], op0=ALU.add, op1=ALU.mult), reads=[aa, k_a], writes=[kd])
                    kb.op("dve", lambda e: e.scalar_tensor_tensor(out=kd[:], in0=kd[:], scalar=1.0, in1=k[:], op0=ALU.add, op1=ALU.mult), reads=[kd, k], writes=[kd])
                    kds.append(kd)
                    for fi, src in enumerate((dec, kk, bd, kd, r)):
                        kb.dma("pool", tview(XD[d, b * 8:(b + 1) * 8, :, fi * 64:(fi + 1) * 64], d, i * 128, 128).rearrange("h t k -> t h k"),
                               v3(src[:]), reads=[src], writes=[XD])
                    v4 = v[:].rearrange("p (h g i) -> p h g i", h=8, g=8)
                    for vg in range(8):
                        kb.dma("pool", tview(L["XV"][d, vg, b * 8:(b + 1) * 8, :, :], d, i * 128, 128).rearrange("h t i -> t h i"),
                               v4[:, :, vg, :], reads=[v], writes=[L["XV"]])
                rk, ks, bo = T[16], T[17], T[18]
                kb.op("dve", lambda e: e.tensor_tensor(out=rk[:], in0=r[:], in1=r_k[:], op=ALU.mult), reads=[r, r_k], writes=[rk])
                kb.op("dve", lambda e: e.tensor_tensor(out=ks[:], in0=kds[0][:], in1=kds[1][:], op=ALU.add), reads=[kds[0], kds[1]], writes=[ks])
                kb.op("dve", lambda e: e.tensor_tensor(out=ks[:], in0=ks[:], in1=rk[:], op=ALU.mult), reads=[ks, rk], writes=[ks])
                kb.op("dve", lambda e: e.tensor_reduce(out=s_[:], in_=v3(ks[:]), axis=AX.X, op=ALU.add), reads=[ks, s_], writes=[s_])
                kb.op("dve", lambda e: e.tensor_tensor(out=v3(bo[:]), in0=v3(v[:]), in1=s_[:].unsqueeze(2).to_broadcast([128, 8, 64]), op=ALU.mult),
                      reads=[v, s_], writes=[bo])
                fm_transpose(kb, L, bo, bs_)
                kb.dma("pool", BONUST.t.rearrange("(c p) g -> p c g", p=128)[:, :, g:g + 128], bs_[:], reads=[bs_], writes=[BONUST])


def rwkv_scan(kb, l, L, others, budget):
    XD, YSC = L["XD"], L["YSC"]
    with Phase(kb) as P:
        S = P.sb("S", [128, 2, 8, 64])
        T1 = P.sb("T1", [128, 2, 8, 64])
        sa = P.sb("sa", [128, 2, 8])
        X = [P.sb("X%d" % i, [128, 2, TBLK, 320]) for i in range(2)]
        V = [P.sb("V%d" % i, [128, 2, TBLK, 8]) for i in range(2)]
        Yb = [P.sb("Y%d" % i, [128, 2, TBLK, 8]) for i in range(2)]
        T3 = [P.sb("T3_%d" % i, [128, 2, 8, 64]) for i in range(4)]
        kb.op("dve", lambda e: e.memset(S[:], 0.0), writes=[S])
        nblk = TT // TBLK
        SH = [128, 2, 8, 64]
        bk = lambda ap: ap.unsqueeze(2).to_broadcast(SH)
        bv = lambda ap: ap.unsqueeze(3).to_broadcast(SH)

        XV = L["XV"]

        def tok0_b(j):
            s0 = j * TBLK
            if s0 < TC:
                return TC - s0 - TBLK
            return TT - (s0 - TC) - TBLK

        def load(j):
            x, v = X[j % 2], V[j % 2]
            s0 = j * TBLK
            tb = tok0_b(j)
            kb.dma("pool", x[:, 0, :, :].rearrange("p t k -> p (t k)"),
                   XD[0, :, s0:s0 + TBLK, :].rearrange("q t k -> q (t k)").unsqueeze(0).to_broadcast([8, 16, TBLK * 320]), reads=[XD], writes=[x])
            for vg in range(8):
                kb.dma("pool", x[vg * 16:(vg + 1) * 16, 1, :, :], XD[1, :, tb:tb + TBLK, :][:, ::-1, :], reads=[XD], writes=[x])
            kb.dma("pool", v[:, 0, :, :].rearrange("p t i -> p (t i)"),
                   XV[0, :, :, s0:s0 + TBLK, :].rearrange("g q t i -> (g q) (t i)"), reads=[XV], writes=[v])
            kb.dma("pool", v[:, 1, :, :], XV[1, :, :, tb:tb + TBLK, :].rearrange("g q t i -> (g q) t i")[:, ::-1, :], reads=[XV], writes=[v])
        def store(j):
            y = Yb[j % 2]
            kb.dma("pool", YSC[0, :, :, j * TBLK:(j + 1) * TBLK, :].rearrange("g q t i -> (g q) (t i)"),
                   y[:, 0, :, :].rearrange("p t i -> p (t i)"), reads=[y], writes=[YSC])
            tb = tok0_b(j)
            kb.dma("pool", YSC[1, :, :, tb:tb + TBLK, :].rearrange("g q t i -> (g q) t i")[:, ::-1, :], y[:, 1, :, :], reads=[y], writes=[YSC])

        load(0)
        load(1)
        co = Interleave(others) if others is not None else None
        kb.co = co
        deferred = []
        k3 = 0
        for j in range(nblk):
            x, v, y = X[j % 2], V[j % 2], Yb[j % 2]
            for c in range(TBLK):
                t3 = T3[k3 % 4]
                k3 += 1
                kb.op("pool", lambda e: e.tensor_tensor(out=t3[:], in0=bv(v[:, :, c, :]), in1=bk(x[:, :, c, 192:256]), op=ALU.mult), reads=[v, x], writes=[t3])
                if c == 2:
                    for f in deferred:
                        f()
                    deferred = []
                kb.op("dve", lambda e: e.tensor_tensor(out=T1[:], in0=S[:], in1=bk(x[:, :, c, 64:128]), op=ALU.mult), reads=[S, x], writes=[T1])
                kb.op("dve", lambda e: e.tensor_reduce(out=sa[:], in_=T1[:], axis=AX.X, op=ALU.add), reads=[T1], writes=[sa])
                kb.op("dve", lambda e: e.tensor_tensor(out=S[:], in0=S[:], in1=bk(x[:, :, c, 0:64]), op=ALU.mult), reads=[S, x], writes=[S])
                kb.op("dve", lambda e: e.tensor_tensor(out=T1[:], in0=bv(sa[:]), in1=bk(x[:, :, c, 128:192]), op=ALU.mult), reads=[sa, x], writes=[T1])
                kb.op("dve", lambda e: e.tensor_tensor(out=S[:], in0=S[:], in1=T1[:], op=ALU.subtract), reads=[S, T1], writes=[S])
                kb.op("dve", lambda e: e.tensor_tensor(out=S[:], in0=S[:], in1=t3[:], op=ALU.add), reads=[S, t3], writes=[S])
                kb.op("dve", lambda e: e.tensor_tensor(out=T1[:], in0=S[:], in1=bk(x[:, :, c, 256:320]), op=ALU.mult), reads=[S, x], writes=[T1])
                kb.op("dve", lambda e: e.tensor_reduce(out=y[:, :, c, :], in_=T1[:], axis=AX.X, op=ALU.add), reads=[T1], writes=[y], waw=False)
                if c == 3 and co is not None:
                    co.give(budget // 2)
            deferred.append(lambda j=j: store(j))
            if j + 2 < nblk:
                deferred.append(lambda j=j: load(j + 2))
            if co is not None:
                co.give(budget - budget // 2)
        for f in deferred:
            f()
        if co is not None:
            if not co.done:
                print("[build] interleave: worker still had work after the scan (emitted %d ops so far)" % co.nops)
            co.finish()
            print("[build] interleave: worker ops=%d, scan blocks=%d, budget=%d" % (co.nops, nblk, budget))
        kb.co = None


def rwkv_readout(kb, l, L):
    PT, RKVTOK, XD, YSC, BONUST, YRAW = (L[k] for k in ("PT", "RKVTOK", "XD", "YSC", "BONUST", "YRAW"))
    with Phase(kb) as P:
        ya = [P.sb("ya%d" % i, [128, 512]) for i in range(2)]
        yb_ = [P.sb("yb%d" % i, [128, 512]) for i in range(2)]
        st = [P.sb("st%d" % i, [128, 4, 128]) for i in range(2)]
        v3 = lambda ap: ap.rearrange("p (h k) -> p h k", k=64)
        it = 0
        for b in range(NB):
            for i in range(TT // 128):
                a_, b_, s_ = ya[it % 2], yb_[it % 2], st[it % 2]
                it += 1
                g = gtok(b, i * 128)
                for d, dst in ((0, a_), (1, b_)):
                    d4 = dst[:].rearrange("p (h g i) -> p h g i", h=8, g=8)
                    for vg in range(8):
                        kb.dma("sp", d4[:, :, vg, :], tview(YSC[d, vg, b * 8:(b + 1) * 8, :, :], d, i * 128, 128).rearrange("h t i -> t h i"),
                               reads=[YSC], writes=[dst])
                kb.op("dve", lambda e: e.tensor_tensor(out=a_[:], in0=a_[:], in1=b_[:], op=ALU.add), reads=[a_, b_], writes=[a_])
                fm_transpose(kb, L, a_, s_)
                kb.dma("pool", YRAW.t.rearrange("(c p) g -> p c g", p=128)[:, :, g:g + 128], s_[:], reads=[s_], writes=[YRAW])
    with Phase(kb) as P:
        gw = P.sb("gw", [128, 4])
        gb = P.sb("gb", [128, 4])
        kb.dma("sp", gw[:], L["rwgw_in"][l, :, :], writes=[gw])
        kb.dma("sp", gb[:], L["rwgb_in"][l, :, :], writes=[gb])
        groupnorm_gate_readout(kb, P, L, YRAW, C_GA, gw, gb, 64e-5, 0, BONUST)


def gtok(b, t0):
    return b * TT + t0


def build_program(depth=DEPTH, debug=(), ext_in=(), skip=(), budget=112):
    nc = bass.Bass("TRN2", target_bir_lowering=False)
    kb = KB(nc)

    def din(name, shape, dt=F32):
        return Buf(nc.dram_tensor(name, list(shape), dt, kind="ExternalInput").ap(), name)

    x_in = din("x", [NB, TL, D])
    ctx_in = din("ctx", [NB, TC, D])
    fnw_in = din("final_norm_w", [128, 8])
    ident_in = din("ident", [128, 128])
    out_d = Buf(nc.dram_tensor("out", [NB, TL, D], F32, kind="ExternalOutput").ap(), "out")

    cond_in = din("condT", [128, 8, 3])
    adaw_in = din("ada_w", [DEPTH, 128, 8, 3 * D])
    adab_in = din("ada_b", [DEPTH, 128, 24])
    normw_in = din("norm_w", [DEPTH, 128, 8])
    inw_in = din("in_w", [DEPTH, NBLK, 128, 8, 128])

    mergew_in = din("merge_w", [DEPTH, 32, 128, 8, 128])
    mergeb_in = din("merge_b", [DEPTH, 128, 32])
    branchw_in = din("branch_w", [DEPTH, 32, 128, 4, 128])
    outw_in = din("out_w", [DEPTH, 8, 128, 8, 128])

    maskF_in = din("maskF", [128, 4, 512])
    maskB_in = din("maskB", [128, 4, 512])
    blk64_in = din("blk64", [128, 128])
    ropec_in = din("rope_cos", [128, TL])
    ropes_in = din("rope_sin", [128, TL])
    ssdcw_in = din("ssd_conv_w", [DEPTH, 128, 8, 3])
    ssdcb_in = din("ssd_conv_b", [DEPTH, 128, 8])
    ssddtb_in = din("ssd_dt_bias", [DEPTH, 16, 1])
    ssdalog_in = din("ssd_a_log", [DEPTH, 16, 1])
    ssdD_in = din("ssd_d", [DEPTH, 128, 4])
    ssdnw_in = din("ssd_norm_w", [DEPTH, 128, 4])
    retlg_in = din("ret_log_decay", [DEPTH, 16, 1])
    retgw_in = din("ret_gn_w", [DEPTH, 128, 4])
    retgb_in = din("ret_gn_b", [DEPTH, 128, 4])

    hycw_in = din("hy_conv_w", [DEPTH, 128, 12, 3])
    hycb_in = din("hy_conv_b", [DEPTH, 128, 12])
    hyw1_in = din("hy_w1", [DEPTH, 33, 64])
    hyb1_in = din("hy_b1", [DEPTH, 64, 1])
    hyw2_in = din("hy_w2", [DEPTH, 64, 64])
    hyb2_in = din("hy_b2", [DEPTH, 64, 1])
    hyw3_in = din("hy_w3", [DEPTH, 64, 1024])
    hyfreq_in = din("hy_freq", [DEPTH, 64, 1])
    hybias_in = din("hy_bias", [DEPTH, 128, 2, 4])
    hytab = {}
    for n_ in (TL, TC):
        k_ = n_ // 128
        hytab[n_] = dict(feats=din("hy_feats%d" % n_, [33, n_]), window=din("hy_window%d" % n_, [128, 4, n_]),
                         Fc=din("hy_Fc%d" % n_, [k_, 128, k_, 128], BF16), Fs=din("hy_Fs%d" % n_, [k_, 128, k_, 128], BF16),
                         Gre=din("hy_Gre%d" % n_, [n_ // 256, 128, k_, 256], BF16), Gim=din("hy_Gim%d" % n_, [n_ // 256, 128, k_, 256], BF16))

    rwmu_in = din("rwkv_mu", [DEPTH, 128, 12])
    rww0_in = din("rwkv_w0", [DEPTH, 2, 512])
    rww2_in = din("rwkv_w2", [DEPTH, 128, 512])
    rwa0_in = din("rwkv_a0", [DEPTH, 2, 512])
    rwa2_in = din("rwkv_a2", [DEPTH, 128, 512])
    rwkk_in = din("rwkv_k_k", [DEPTH, 1, 512])
    rwka_in = din("rwkv_k_a", [DEPTH, 1, 512])
    rwrk_in = din("rwkv_r_k", [DEPTH, 1, 512])
    rwgw_in = din("rwkv_gn_w", [DEPTH, 128, 4])
    rwgb_in = din("rwkv_gn_b", [DEPTH, 128, 4])

    def scratch(name, shape, dt=F32):
        if name in ext_in:
            return Buf(nc.dram_tensor(name, list(shape), dt, kind="ExternalInput").ap(), name)
        if name in debug:
            return Buf(nc.dram_tensor(name, list(shape), dt, kind="ExternalOutput").ap(), name)
        return kb.dram(name, shape, dt)

    HT = scratch("HT", [D, NT])
    UTd = scratch("UTd", [D, NT], BF16)
    PT = scratch("PT", [NCOLS, NT])
    MWB = scratch("MWB", [32, 128, 8, 128], BF16)
    BWB = scratch("BWB", [32, 128, 4, 128], BF16)
    OWB = scratch("OWB", [8, 128, 8, 128], BF16)
    YST = scratch("YST", [4 * DBR, NT], BF16)
    AT = scratch("AT", [NB, 16, TT])
    YRAW = scratch("YRAW", [DBR, NT])
    XS = scratch("XS", [DBR, NT])
    HYC = scratch("HYC", [3 * DBR, NT])
    UTOK = [scratch("UTOK%d" % i, [NT, DBR], BF16) for i in range(2)]
    Z1T = scratch("Z1T", [DBR, NT])
    HTOK = {n_: scratch("HTOK%d" % n_, [n_, 2 * DBR], BF16) for n_ in (TL, TC)}
    HF = {n_: scratch("HF%d" % n_, [2, n_ // 128, 128, 4, 256]) for n_ in (TL, TC)}
    RKVTOK = [scratch("RKVTOK%d" % i, [NT, DBR]) for i in range(3)]
    XD = scratch("XD", [2, 16, TT, 320])
    XV = scratch("XV", [2, 8, 16, TT, 8])
    YSC = scratch("YSC", [2, 8, 16, TT, 8])
    BONUST = scratch("BONUST", [DBR, NT])
    HTv = HT.t.rearrange("(c p) g -> p c g", p=128)
    YSv = YST.t.rearrange("(m c p) g -> p m c g", p=128, c=4)
    UTv = UTd.t.rearrange("(c p) g -> p c g", p=128)

    with Phase(kb) as G:
        ident = G.sb("ident", [128, 128])
        ones = G.sb("ones", [128, 128])
        fnw = G.sb("fnw", [128, 8])
        kb.dma("sp", ident[:], ident_in[:, :], writes=[ident])
        kb.dma("sp", fnw[:], fnw_in[:, :], writes=[fnw])
        kb.op("dve", lambda e: e.memset(ones[:], 1.0), writes=[ones])
        maskF = G.sb("maskF", [128, 4, 512])
        maskB = G.sb("maskB", [128, 4, 512])
        blk64 = G.sb("blk64", [128, 128])
        kb.dma("sp", maskF[:], maskF_in[:, :, :], writes=[maskF])
        kb.dma("sp", maskB[:], maskB_in[:, :, :], writes=[maskB])
        kb.dma("sp", blk64[:], blk64_in[:, :], writes=[blk64])

        with Phase(kb) as P:
            xin = [P.sb("xin%d" % i, [128, D]) for i in range(2)]
            stg = [P.sb("stg%d" % i, [128, 8, 512]) for i in range(2)]
            it = 0
            for b in range(NB):
                for (t0, tl) in TILES:
                    sg = stg[it % 2]
                    it += 1
                    for s in range(tl // 128):
                        xi = xin[s % 2]
                        tt = t0 + s * 128
                        src = ctx_in[b, tt:tt + 128, :] if tt < TC else x_in[b, tt - TC:tt - TC + 128, :]
                        kb.dma("sp", xi[:], src, writes=[xi])
                        for c in range(8):
                            ps = kb.ps()
                            kb.op("pe", lambda e: e.transpose(out=ps[:, 0:128], in_=xi[:, c * 128:(c + 1) * 128], identity=ident[:]),
                                  reads=[xi, ident], writes=[ps])
                            eng = "act" if c % 2 else "dve"
                            if eng == "act":
                                kb.op("act", lambda e: e.copy(out=sg[:, c, s * 128:(s + 1) * 128], in_=ps[:, 0:128]),
                                      reads=[ps], writes=[sg], waw=False)
                            else:
                                kb.op("dve", lambda e: e.tensor_copy(out=sg[:, c, s * 128:(s + 1) * 128], in_=ps[:, 0:128]),
                                      reads=[ps], writes=[sg], waw=False)
                    g0 = gtok(b, t0)
                    kb.dma("pool", HT.t.rearrange("(c p) g -> p c g", p=128)[:, :, g0:g0 + tl], sg[:, :, 0:tl],
                           reads=[sg], writes=[HT])

        for l in range(depth):
            with Phase(kb) as LP:
                modA = LP.sb("modA", [128, 8, 3])
                modS = LP.sb("modS", [128, 8, 3])
                modG = LP.sb("modG", [128, 8, 3])
                if "front" not in skip:
                    layer_front(kb, l, locals())
                LL = locals()

                def others():
                    if "hy" not in skip:
                        layer_hyena(kb, l, LL)
                    if "ssd" not in skip:
                        layer_ssd(kb, l, LL)
                    if "ret" not in skip:
                        layer_ret(kb, l, LL)
                if "rwkv" not in skip:
                    rwkv_prep(kb, l, LL)
                    if "nointer" in skip:
                        rwkv_scan(kb, l, LL, None, 0)
                        others()
                    else:
                        rwkv_scan(kb, l, LL, others, budget)
                    rwkv_readout(kb, l, LL)
                else:
                    others()
                if "merge" not in skip:
                    layer_merge(kb, l, locals())

        with Phase(kb) as P:
            hin = [P.sb("hin%d" % i, [128, 8, 512]) for i in range(2)]
            sq = P.sb("sq", [128, 8, 512])
            rstd = P.sb("rstd", [128, 512])
            yn = P.sb("yn", [128, 8, 512])
            ot = [P.sb("ot%d" % i, [128, D]) for i in range(2)]
            it = 0
            oi = 0
            for b in range(NB):
                for (t0, tl) in TILES[1:]:
                    hi = hin[it % 2]
                    it += 1
                    g0 = gtok(b, t0)
                    kb.dma("sp", hi[:], HT.t.rearrange("(c p) g -> p c g", p=128)[:, :, g0:g0 + tl], reads=[HT], writes=[hi])
                    kb.op("act", lambda e: e.activation(out=sq[:], in_=hi[:], func=AF.Square), reads=[hi], writes=[sq])
                    ps = kb.ps()
                    for c in range(8):
                        kb.op("pe", lambda e: e.matmul(ps[:], lhsT=ones[:], rhs=sq[:, c, :], start=(c == 0), stop=(c == 7)),
                              reads=[ones, sq], writes=[ps])
                    rsqrt(kb, rstd, rstd[:], ps, ps[:], 1.0 / D, 1e-6)
                    for c in range(8):
                        kb.op("dve", lambda e: e.scalar_tensor_tensor(out=yn[:, c, :], in0=hi[:, c, :], scalar=fnw[:, c:c + 1], in1=rstd[:],
                                                                      op0=ALU.mult, op1=ALU.mult),
                              reads=[hi, fnw, rstd], writes=[yn], waw=False)
                    for s in range(tl // 128):
                        o = ot[oi % 2]
                        oi += 1
                        for c in range(8):
                            ps2 = kb.ps()
                            kb.op("pe", lambda e: e.transpose(out=ps2[:, 0:128], in_=yn[:, c, s * 128:(s + 1) * 128], identity=ident[:]),
                                  reads=[yn, ident], writes=[ps2])
                            if c % 2:
                                kb.op("act", lambda e: e.copy(out=o[:, c * 128:(c + 1) * 128], in_=ps2[:, 0:128]), reads=[ps2], writes=[o], waw=False)
                            else:
                                kb.op("dve", lambda e: e.tensor_copy(out=o[:, c * 128:(c + 1) * 128], in_=ps2[:, 0:128]), reads=[ps2], writes=[o], waw=False)
                        tt = t0 - TC + s * 128
                        kb.dma("pool", out_d[b, tt:tt + 128, :], o[:], reads=[o], writes=[out_d])
    kb.barrier()
    return nc


_CACHE = {}


def _colmap():
    m = np.full(NCOLS, -1, dtype=np.int64)
    def put(dst, src, n):
        m[dst:dst + n] = np.arange(src, src + n)
    put(C_RKV, 0, 1536); put(C_ZW, 1536, 128); put(C_ZA, 1664, 128); put(C_GA, 1792, 512)
    put(C_XBC, 2304, 1024); put(C_GZ, 3344, 512); put(C_HY, 3856, 1536); put(C_GH, 5392, 512)
    put(C_QKV, 5904, 1536); put(C_GR, 7440, 512); put(C_DT, 3328, 16)
    d = np.arange(512)
    r = d % 32
    partner = np.where(r < 16, d + 16, d - 16)
    m[C_QSW:C_QSW + 512] = 5904 + partner
    m[C_KSW:C_KSW + 512] = 5904 + 512 + partner
    return m


_HYT = {}


def _hy_tables(n):
    if n in _HYT:
        return _HYT[n]
    f = np.float32
    nch = n // 128
    Lp = 2 * n
    j = np.arange(n, dtype=np.float32)
    t = (j / np.float32(max(n - 1, 1))).astype(f)
    ang = (np.float32(2.0 * math.pi) * j / np.float32(n)).astype(f)
    bands = np.linspace(1e-4, 15, 16).astype(f)
    feats = np.concatenate([t[:, None], np.cos(ang[:, None] * bands), -np.sin(ang[:, None] * bands)], axis=-1).astype(f)
    lag = (np.abs(j - n // 2) / np.float32(n / 2.0)).astype(f)
    deltas = np.abs(np.linspace(math.log(1e-2) / 1.5, math.log(1e-2) / 0.3, 512)).astype(f)
    window = (np.exp(-lag[:, None] * deltas[None, :]) + np.float32(0.05)).astype(f)
    sidx = np.arange(n, dtype=np.float64)[:, None]
    fidx = np.arange(n, dtype=np.float64)[None, :]
    Fc = np.cos(2 * np.pi * sidx * fidx / Lp)
    Fs = -np.sin(2 * np.pi * sidx * fidx / Lp)
    Fs[:, 0] = np.cos(np.pi * sidx[:, 0])
    tau = (np.arange(n, dtype=np.float64) + n // 2)[None, :]
    fcol = np.arange(n, dtype=np.float64)[:, None]
    wf = np.where(fcol == 0, 1.0, 2.0)
    Gre = wf / Lp * np.cos(2 * np.pi * fcol * tau / Lp)
    Gim = -2.0 / Lp * np.sin(2 * np.pi * fcol * tau / Lp)
    Gim[0, :] = np.cos(np.pi * tau[0]) / Lp

    def lay_f(M):
        return np.ascontiguousarray(M.reshape(nch, 128, nch, 128).transpose(2, 1, 0, 3)).astype(ml_dtypes.bfloat16)

    def lay_g(M):
        return np.ascontiguousarray(M.reshape(nch, 128, n // 256, 256).transpose(2, 1, 0, 3)).astype(ml_dtypes.bfloat16)

    out = dict(feats=np.ascontiguousarray(feats.T), window=np.ascontiguousarray(window.T.reshape(4, 128, n).transpose(1, 0, 2)),
               Fc=lay_f(Fc), Fs=lay_f(Fs), Gre=lay_g(Gre), Gim=lay_g(Gim))
    _HYT[n] = out
    return out


def _pc(a):
    n = a.shape[-1] // 128
    return np.ascontiguousarray(np.swapaxes(a.reshape(a.shape[:-1] + (n, 128)), -1, -2))


def shared_inputs(inp):
    f = np.float32
    sh = {}
    sh["final_norm_w"] = _pc(inp["final_norm_w"].astype(f))
    sh["ident"] = np.eye(128, dtype=f)
    L = DEPTH
    sh["ada_w"] = np.ascontiguousarray(inp["ada_w"].reshape(L, 8, 128, 3 * D).transpose(0, 2, 1, 3))
    sh["ada_b"] = _pc(inp["ada_b"])
    sh["norm_w"] = _pc(inp["norm_w"])
    sh["merge_w"] = np.ascontiguousarray(inp["merge_w"].reshape(L, 8, 128, 32, 128).transpose(0, 3, 2, 1, 4))
    sh["merge_b"] = _pc(inp["merge_b"])
    sh["branch_w"] = np.ascontiguousarray(inp["branch_w"].reshape(L, 4, 4, 128, 8, 128).transpose(0, 1, 4, 3, 2, 5).reshape(L, 32, 128, 4, 128))
    sh["out_w"] = np.ascontiguousarray(inp["out_w"].reshape(L, 8, 128, 8, 128).transpose(0, 3, 2, 1, 4))
    m_ = np.arange(128)[:, None, None]
    o_ = np.arange(4)[None, :, None]
    n_ = np.arange(512)[None, None, :]
    sh["maskF"] = np.where(n_ >= o_ * 128 + m_, 0.0, -30000.0).astype(f)
    sh["maskB"] = np.where(o_ * 128 + m_ >= n_, 0.0, -30000.0).astype(f)
    blk = np.zeros((128, 128), f)
    blk[:64, :64] = 1.0 / 64
    blk[64:, 64:] = 1.0 / 64
    sh["blk64"] = blk
    t = np.arange(TL)
    rows = (t // 64).astype(np.float64)
    cols = (t % 64).astype(np.float64)
    inv = (10000.0 ** (-np.arange(0, 32, 2, dtype=np.float32) / np.float32(32))).astype(np.float32)
    dch = np.arange(64)
    half = dch // 32
    r = dch % 32
    ang = np.where(half[:, None] == 0, rows[None, :], cols[None, :]).astype(np.float32) * inv[r % 16][:, None]
    cosT = np.cos(ang).astype(f)
    sinT = (np.sin(ang) * np.where(r < 16, -1.0, 1.0)[:, None]).astype(f)
    sh["rope_cos"] = np.ascontiguousarray(np.concatenate([cosT, cosT], 0))
    sh["rope_sin"] = np.ascontiguousarray(np.concatenate([sinT, sinT], 0))
    sh["ssd_conv_w"] = np.ascontiguousarray(inp["ssd_conv_w"].reshape(L, 8, 128, 3).transpose(0, 2, 1, 3))
    sh["ssd_conv_b"] = _pc(inp["ssd_conv_b"])
    sh["ssd_dt_bias"] = np.ascontiguousarray(inp["ssd_dt_bias"].reshape(L, 16, 1))
    sh["ssd_a_log"] = np.ascontiguousarray(inp["ssd_a_log"].reshape(L, 16, 1))
    sh["ssd_d"] = _pc(np.repeat(inp["ssd_d"], 64, axis=-1))
    sh["ssd_norm_w"] = _pc(inp["ssd_norm_w"])
    sh["ret_log_decay"] = np.ascontiguousarray(inp["ret_log_decay"].reshape(L, 16, 1))
    sh["ret_gn_w"] = _pc(inp["ret_gn_w"])
    sh["ret_gn_b"] = _pc(inp["ret_gn_b"])
    sh["hy_conv_w"] = np.ascontiguousarray(inp["hy_conv_w"].reshape(L, 12, 128, 3).transpose(0, 2, 1, 3))
    sh["hy_conv_b"] = _pc(inp["hy_conv_b"])
    sh["hy_w1"] = np.ascontiguousarray(inp["hy_w1"])
    sh["hy_b1"] = np.ascontiguousarray(inp["hy_b1"].reshape(L, 64, 1))
    sh["hy_w2"] = np.ascontiguousarray(inp["hy_w2"])
    sh["hy_b2"] = np.ascontiguousarray(inp["hy_b2"].reshape(L, 64, 1))
    sh["hy_w3"] = np.ascontiguousarray(inp["hy_w3"])
    sh["hy_freq"] = np.ascontiguousarray(inp["hy_freq"].reshape(L, 64, 1))
    sh["hy_bias"] = np.ascontiguousarray(inp["hy_bias"].reshape(L, 2, 4, 128).transpose(0, 3, 1, 2))
    for n_ in (TL, TC):
        for k_, v_ in _hy_tables(n_).items():
            sh["hy_%s%d" % (k_, n_)] = v_
    sh["rwkv_mu"] = _pc(inp["rwkv_mu"].reshape(L, 1536))
    sh["rwkv_w0"] = np.ascontiguousarray(inp["rwkv_w0"])
    sh["rwkv_w2"] = np.ascontiguousarray(inp["rwkv_w2"].reshape(L, 128, 512))
    sh["rwkv_a0"] = np.ascontiguousarray(inp["rwkv_a0"])
    sh["rwkv_a2"] = np.ascontiguousarray(inp["rwkv_a2"].reshape(L, 128, 512))
    sh["rwkv_k_k"] = np.ascontiguousarray(inp["rwkv_k_k"].reshape(L, 1, 512))
    sh["rwkv_k_a"] = np.ascontiguousarray(inp["rwkv_k_a"].reshape(L, 1, 512))
    sh["rwkv_r_k"] = np.ascontiguousarray(inp["rwkv_r_k"].reshape(L, 1, 512))
    sh["rwkv_gn_w"] = _pc(inp["rwkv_gn_w"])
    sh["rwkv_gn_b"] = _pc(inp["rwkv_gn_b"])
    cm = _colmap()
    w = inp["in_w"]
    wa = np.zeros((L, D, NCOLS), dtype=f)
    ok = cm >= 0
    wa[:, :, ok] = w[:, :, cm[ok]]
    sh["in_w"] = np.ascontiguousarray(wa.reshape(L, 8, 128, NBLK, 128).transpose(0, 3, 2, 1, 4))
    return sh


def core_inputs(inp, sh, core):
    bs = slice(core * NB, (core + 1) * NB)
    m = dict(sh)
    m["x"] = np.ascontiguousarray(inp["x"][bs])
    m["ctx"] = np.ascontiguousarray(inp["ctx"][bs])
    cond = np.concatenate([inp["c"][bs], inp["c_ctx"][None, :]], axis=0)
    m["condT"] = np.ascontiguousarray(cond.reshape(3, 8, 128).transpose(2, 1, 0))
    return m


def kernel(**inputs):
    inp = {k: np.asarray(v) for k, v in inputs.items()}
    if "nc" not in _CACHE:
        _CACHE["nc"] = build_program()
    nc = _CACHE["nc"]
    sh = shared_inputs(inp)
    in_maps = [core_inputs(inp, sh, core) for core in range(NCORES)]
    res = run_bass_kernel_spmd(nc, in_maps, core_ids=list(range(NCORES)))
    out = np.concatenate([r["out"] for r in res.results], axis=0)
    return out.astype(np.float32)
```
